# Optimizing a Trainium2 kernel written in Bass

```python
import math
import jax, jax.numpy as jnp
from jax import lax
import numpy as np

D_MODEL = 1024
BATCH = 16
SEQ = 2048
DEPTH = 1

N_META = 16
GRID_W = 64
CONV_DIM = 1024
CONV_K = 31
N_HEADS = 16
N_KV_HEADS = 4
HEAD_DIM = 64
GQA_GROUP = N_HEADS // N_KV_HEADS
ATTN_DIM = N_HEADS * HEAD_DIM
KV_DIM = N_KV_HEADS * HEAD_DIM
ROPE_FREQS = HEAD_DIM // 4
ROPE_THETA = 10000.0
Q_BLOCK = 128
NORM_EPS = 1e-6

IN_SPLITS = [CONV_DIM, CONV_DIM, CONV_DIM,
             ATTN_DIM, KV_DIM, KV_DIM, ATTN_DIM,
             D_MODEL, D_MODEL]
IN_DIM = sum(IN_SPLITS)
IN_OFFSETS = list(np.cumsum(IN_SPLITS)[:-1].tolist())

kernel_name = "hybrid_conformer_gqa_gated_encoder"


def rms_norm(x, g, eps=NORM_EPS):
    xf = x.astype(jnp.float32)
    y = xf * lax.rsqrt(jnp.mean(xf * xf, axis=-1, keepdims=True) + eps)
    return (y * g.astype(jnp.float32)).astype(x.dtype)


def layer_norm(x, g, b, eps=NORM_EPS):
    xf = x.astype(jnp.float32)
    mu = jnp.mean(xf, axis=-1, keepdims=True)
    xc = xf - mu
    y = xc * lax.rsqrt(jnp.mean(xc * xc, axis=-1, keepdims=True) + eps)
    return (y * g.astype(jnp.float32) + b.astype(jnp.float32)).astype(x.dtype)


def rope_tables(n_tok):
    rows = n_tok // GRID_W
    row_ids = jnp.concatenate([jnp.zeros((N_META,), jnp.float32),
                               jnp.repeat(jnp.arange(rows, dtype=jnp.float32), GRID_W)])
    col_ids = jnp.concatenate([jnp.zeros((N_META,), jnp.float32),
                               jnp.tile(jnp.arange(GRID_W, dtype=jnp.float32), rows)])
    inv_freq = ROPE_THETA ** (-jnp.arange(ROPE_FREQS, dtype=jnp.float32) / ROPE_FREQS)
    a_row = row_ids[:, None] * inv_freq[None, :]
    a_col = col_ids[:, None] * inv_freq[None, :]
    ang = jnp.concatenate([a_row, a_row, a_col, a_col], axis=-1)
    return jnp.cos(ang), jnp.sin(ang)


def apply_rope2d(x, cos, sin):
    xs = x.reshape(x.shape[:-1] + (2, 2, ROPE_FREQS))
    rot = jnp.stack([-xs[..., 1, :], xs[..., 0, :]], axis=-2).reshape(x.shape)
    c = cos[None, :, None, :].astype(x.dtype)
    s = sin[None, :, None, :].astype(x.dtype)
    return x * c + rot * s


def conv_branch(val, glu_gate, z, conv_w, conv_b, cn_g, cn_b, w_proj):
    u = val * jax.nn.sigmoid(glu_gate)
    kern = conv_w.reshape(CONV_K, 1, CONV_DIM).astype(u.dtype)
    pad = CONV_K // 2
    c = lax.conv_general_dilated(u, kern, window_strides=(1,), padding=[(pad, pad)],
                                 dimension_numbers=("NWC", "WIO", "NWC"),
                                 feature_group_count=CONV_DIM)
    c = c + conv_b.astype(c.dtype)
    c = jax.nn.silu(layer_norm(c, cn_g, cn_b))
    c = c * jax.nn.silu(z)
    return jnp.einsum("blc,cd->bld", c, w_proj.astype(c.dtype))


def attn_branch(q, k, v, z, q_g, k_g, w_proj, cos, sin):
    B, L, _ = q.shape
    n_tok = L - N_META
    q = q.reshape(B, L, N_HEADS, HEAD_DIM)
    k = k.reshape(B, L, N_KV_HEADS, HEAD_DIM)
    v = v.reshape(B, L, N_KV_HEADS, HEAD_DIM)
    q = apply_rope2d(rms_norm(q, q_g), cos, sin)
    k = apply_rope2d(rms_norm(k, k_g), cos, sin)
    q = q.reshape(B, L, N_KV_HEADS, GQA_GROUP, HEAD_DIM)
    scale = 1.0 / math.sqrt(HEAD_DIM)

    def attend(qb):
        s = jnp.einsum("bqkgd,bskd->bkgqs", qb, k).astype(jnp.float32) * scale
        p = jax.nn.softmax(s, axis=-1).astype(v.dtype)
        return jnp.einsum("bkgqs,bskd->bqkgd", p, v)

    o_meta = attend(q[:, :N_META])
    n_blk = n_tok // Q_BLOCK
    q_real = q[:, N_META:].reshape(B, n_blk, Q_BLOCK, N_KV_HEADS, GQA_GROUP, HEAD_DIM)
    o_real = lax.map(attend, jnp.moveaxis(q_real, 1, 0))
    o_real = jnp.moveaxis(o_real, 0, 1).reshape(B, n_tok, N_KV_HEADS, GQA_GROUP, HEAD_DIM)
    o = jnp.concatenate([o_meta, o_real], axis=1).reshape(B, L, ATTN_DIM)
    o = o * jax.nn.silu(z)
    return jnp.einsum("bla,ad->bld", o, w_proj.astype(o.dtype))


def hybrid_layer(h, norm_g, w_in, conv_w, conv_b, cn_g, cn_b, w_conv_out,
                 q_g, k_g, w_attn_out, w_out, cos, sin):
    xn = rms_norm(h, norm_g)
    proj = jnp.einsum("bld,de->ble", xn, w_in.astype(xn.dtype))
    (c_val, c_glu, c_z, q, k, v, a_z, g_c, g_a) = jnp.split(proj, IN_OFFSETS, axis=-1)
    y_c = conv_branch(c_val, c_glu, c_z, conv_w, conv_b, cn_g, cn_b, w_conv_out)
    y_a = attn_branch(q, k, v, a_z, q_g, k_g, w_attn_out, cos, sin)
    merged = jax.nn.sigmoid(g_c) * y_c + jax.nn.sigmoid(g_a) * y_a
    return jnp.einsum("bld,de->ble", merged, w_out.astype(merged.dtype))


def setup_inputs(seed: int = 0) -> dict:
    key = jax.random.key(seed)
    ks = jax.random.split(key, 16)
    f32 = jnp.float32
    nrm = lambda k, shape, s: jax.random.normal(k, shape, f32) * s
    return {
        "x": nrm(ks[0], (BATCH, SEQ, D_MODEL), 1.0),
        "meta_tokens": nrm(ks[1], (N_META, D_MODEL), 1.0),
        "norm_g": 1.0 + nrm(ks[2], (DEPTH, D_MODEL), 0.02),
        "w_in": nrm(ks[3], (DEPTH, D_MODEL, IN_DIM), D_MODEL ** -0.5),
        "conv_w": nrm(ks[4], (DEPTH, CONV_K, CONV_DIM), CONV_K ** -0.5),
        "conv_b": nrm(ks[5], (DEPTH, CONV_DIM), 0.02),
        "conv_norm_g": 1.0 + nrm(ks[6], (DEPTH, CONV_DIM), 0.02),
        "conv_norm_b": nrm(ks[7], (DEPTH, CONV_DIM), 0.02),
        "w_conv_out": nrm(ks[8], (DEPTH, CONV_DIM, D_MODEL), CONV_DIM ** -0.5),
        "q_norm_g": 1.0 + nrm(ks[9], (DEPTH, HEAD_DIM), 0.02),
        "k_norm_g": 1.0 + nrm(ks[10], (DEPTH, HEAD_DIM), 0.02),
        "w_attn_out": nrm(ks[11], (DEPTH, ATTN_DIM, D_MODEL), ATTN_DIM ** -0.5),
        "w_out": nrm(ks[12], (DEPTH, D_MODEL, D_MODEL), D_MODEL ** -0.5),
    }


def reference(x, meta_tokens, norm_g, w_in, conv_w, conv_b, conv_norm_g, conv_norm_b,
              w_conv_out, q_norm_g, k_norm_g, w_attn_out, w_out):
    B, n_tok, _ = x.shape
    meta = jnp.broadcast_to(meta_tokens[None].astype(x.dtype), (B, N_META, D_MODEL))
    h = jnp.concatenate([meta, x], axis=1)
    cos, sin = rope_tables(n_tok)
    for layer in range(DEPTH):
        h = h + hybrid_layer(h, norm_g[layer], w_in[layer], conv_w[layer], conv_b[layer],
                             conv_norm_g[layer], conv_norm_b[layer], w_conv_out[layer],
                             q_norm_g[layer], k_norm_g[layer], w_attn_out[layer],
                             w_out[layer], cos, sin)
    return h[:, N_META:]
```

```python
import math
import numpy as np
import ml_dtypes
import concourse.bass as bass
import concourse.mybir as mybir
from concourse.bass_utils import run_bass_kernel_spmd

F32 = mybir.dt.float32
BF16 = mybir.dt.bfloat16
AF = mybir.ActivationFunctionType
ALU = mybir.AluOpType

NMETA = 16
CONV_K = 31
PAD = CONV_K // 2
HD = 64
EPS = 1e-6
ROPE_THETA = 10000.0
EPOCH = 12000


def full_cfg():
    return dict(D=1024, T=2048, NH=16, NKV=4, BPC=2, GW=64)


class Res:
    __slots__ = ("atoms", "psum")

    def __init__(self, atoms, psum=False):
        self.atoms = tuple(atoms)
        self.psum = psum


class Op:
    __slots__ = ("idx", "eng", "fn", "deps", "dsem", "dcount", "signal", "sig", "eidx")

    def __init__(self, idx, eng, fn, dsem):
        self.idx = idx
        self.eng = eng
        self.fn = fn
        self.deps = {}
        self.dsem = dsem
        self.dcount = 0
        self.signal = False
        self.sig = 0
        self.eidx = 0


class DSem:
    def __init__(self, name):
        self.name = name
        self.count = 0
        self.handle = None


class Prog:
    ENGS = ("pe", "act", "dve", "pool", "sp")

    def __init__(self):
        self.ops = []
        self.state = {}
        self.natoms = 0
        self.dsems = []

    def atoms(self, n=1):
        a = list(range(self.natoms, self.natoms + n))
        self.natoms += n
        return a

    def res(self, psum=False):
        return Res(self.atoms(1), psum)

    def dsem(self, name):
        d = DSem(name)
        self.dsems.append(d)
        return d

    def add(self, eng, fn, reads=(), writes=(), dsem=None):
        op = Op(len(self.ops), eng, fn, dsem)
        if dsem is not None:
            dsem.count += 1
            op.dcount = dsem.count
        deps = op.deps
        st = self.state

        def dep(o, kind):
            if o is op:
                return
            k = deps.get(o)
            if k is None or kind == "raw":
                deps[o] = kind

        for r in reads:
            for a in r.atoms:
                s = st.get(a)
                if s is None:
                    s = st[a] = [None, {}, []]
                if s[0] is not None:
                    dep(s[0], "raw")
                if r.psum:
                    for e, o in s[1].items():
                        if e != eng:
                            dep(o, "excl")
                if dsem is not None:
                    s[2].append(op)
                else:
                    s[1][eng] = op
        for w in writes:
            for a in w.atoms:
                s = st.get(a)
                if s is None:
                    s = st[a] = [None, {}, []]
                if s[0] is not None:
                    if not (dsem is not None and s[0].dsem is dsem and not s[1] and not s[2]):
                        dep(s[0], "waw")
                for e, o in s[1].items():
                    dep(o, "war")
                for o in s[2]:
                    dep(o, "war")
                s[0] = op
                s[1] = {}
                s[2] = []
        self.ops.append(op)
        return op

    def emit(self, nc):
        ops = self.ops
        for op in ops:
            keep = {}
            for d, kind in op.deps.items():
                if d.dsem is None and d.eng == op.eng and op.dsem is None and op.eng == "pe":
                    continue
                keep[d] = kind
            op.deps = keep
            for d in keep:
                if d.dsem is None:
                    d.signal = True
        cnt = {e: 0 for e in self.ENGS}
        for op in ops:
            if op.dsem is None and op.signal:
                cnt[op.eng] += 1
                op.sig = cnt[op.eng]
        import contextlib

        stack = contextlib.ExitStack()
        with stack:
            esems = {}
            for e in self.ENGS:
                n = (cnt[e] + EPOCH - 1) // EPOCH
                esems[e] = [stack.enter_context(nc.semaphore(f"c_{e}_{i}")) for i in range(max(n, 1))]
            for d in self.dsems:
                d.handle = stack.enter_context(nc.semaphore(f"d_{d.name}"))
            block = stack.enter_context(nc.Block())
            per_eng = {e: [o for o in ops if o.eng == e] for e in self.ENGS}

            def run_engine(ename, h):
                known = {e: 0 for e in self.ENGS}
                dknown = {}
                for op in per_eng[ename]:
                    waits = []
                    need = {}
                    dneed = {}
                    for d in op.deps:
                        if d.dsem is not None:
                            if dknown.get(d.dsem, 0) < d.dcount and dneed.get(d.dsem, 0) < d.dcount:
                                dneed[d.dsem] = d.dcount
                        else:
                            if known[d.eng] < d.sig and need.get(d.eng, 0) < d.sig:
                                need[d.eng] = d.sig
                    for e, s in need.items():
                        known[e] = s
                        ep, loc = (s - 1) // EPOCH, (s - 1) % EPOCH + 1
                        waits.append((esems[e][ep], loc))
                    for ds, c in dneed.items():
                        dknown[ds] = c
                        waits.append((ds.handle, 16 * c))
                    attach = op.dsem is None and len(waits) > 0
                    for sem, val in (waits[:-1] if attach else waits):
                        h.wait_ge(sem, val)
                    ins = op.fn(h)
                    if attach:
                        sem, val = waits[-1]
                        ins._wait_ge(sem, val)
                    if op.dsem is not None:
                        ins.then_inc(op.dsem.handle, 16)
                    elif op.signal:
                        ep = (op.sig - 1) // EPOCH
                        ins.then_inc(esems[op.eng][ep], 1)

            @block.tensor
            def _(h):
                run_engine("pe", h)

            @block.scalar
            def _(h):
                run_engine("act", h)

            @block.vector
            def _(h):
                run_engine("dve", h)

            @block.gpsimd
            def _(h):
                run_engine("pool", h)

            @block.sync
            def _(h):
                run_engine("sp", h)
                for d in self.dsems:
                    if d.count:
                        h.wait_ge(d.handle, 16 * d.count)


class Rot:
    def __init__(self, items):
        self.items = items
        self.i = 0

    def next(self):
        it = self.items[self.i % len(self.items)]
        self.i += 1
        return it


def build_program(cfg):
    D, T, NH, NKV, BPC = cfg["D"], cfg["T"], cfg["NH"], cfg["NKV"], cfg["BPC"]
    DC = D // 128
    KVD = NKV * HD
    G = NH // NKV
    L = T + NMETA
    NT = T // 128
    NQC = T // 512
    NHP = NH // 2
    IN_DIM = 7 * D + 2 * KVD
    OFF = dict(val=0, glu=D, z=2 * D, q=3 * D, k=4 * D, v=4 * D + KVD, az=4 * D + 2 * KVD,
               gc=5 * D + 2 * KVD, ga=6 * D + 2 * KVD)
    UW = T + NMETA + 2 * PAD
    NKC = NKV // 2
    VW = NKC * 3 * HD
    WG = min(512, D)
    NWG = D // WG
    CPG = WG // 128

    nc = bass.Bass("TRN2", target_bir_lowering=False)
    P = Prog()

    def dram(name, shape, dt=F32, kind="ExternalInput"):
        return nc.dram_tensor(name, list(shape), dt, kind=kind).ap()

    x_d = dram("x", [BPC, T, D])
    meta_d = dram("meta", [NMETA, D])
    win_d = dram("w_in", [D, IN_DIM])
    wco_d = dram("w_co", [D, D])
    wao_d = dram("w_ao", [D, D])
    wout_d = dram("w_out", [D, D])
    gN_d = dram("gN", [128, D])
    convw_d = dram("convw", [128, DC * CONV_K])
    pvec_d = dram("pvec", [128, 3 * DC])
    qkg_d = dram("qkg", [128, 2])
    cos_d = dram("cosT", [128, L])
    sin_d = dram("sinT", [128, L])
    cmat_d = dram("cmat", [128, 4 * 128], BF16)
    out_d = dram("out", [BPC, T, D], F32, kind="ExternalOutput")

    arena_elems = nc.sbuf_bytes_remaining // 4 - 64
    arena = nc.alloc_sbuf_tensor("arena", [128, arena_elems], F32)
    ATOM = 512
    cur = [0]

    def alloc_bytes(nbytes):
        nb = (nbytes + 63) // 64 * 64
        o = cur[0]
        cur[0] += nb
        assert cur[0] <= arena_elems * 4, f"SBUF overflow {cur[0]} > {arena_elems * 4}"
        return o

    def view(off, shape, dt):
        esz = 2 if dt == BF16 else 4
        n = int(np.prod(shape))
        assert off % 4 == 0
        a = arena[:, off // 4: off // 4 + (n * esz + 3) // 4]
        if dt == BF16:
            a = a.bitcast(BF16)[:, 0:n]
        if len(shape) == 2:
            a = a.rearrange("p (a b) -> p a b", b=shape[1])
        elif len(shape) == 3:
            a = a.rearrange("p (a b c) -> p a b c", b=shape[1], c=shape[2])
        return a

    def alloc(shape, dt):
        esz = 2 if dt == BF16 else 4
        off = alloc_bytes(int(np.prod(shape)) * esz)
        return view(off, shape, dt)

    cmat = alloc([4 * 128], BF16)
    ident_bf, rotm, bones, onesm = (cmat[:, i * 128:(i + 1) * 128] for i in range(4))
    identf = alloc([128], F32)
    cexp = alloc([2], F32)
    gN = alloc([D], F32)
    convw = alloc([DC * CONV_K], F32)
    pvec = alloc([3 * DC], F32)
    qkg = alloc([2], F32)
    cosT = alloc([L], F32)
    sinT = alloc([L], F32)
    CONST = P.res()
    IDF = P.res()
    CEXP = P.res()
    xnT = alloc([DC, L], BF16)
    big1 = alloc([DC, T], BF16)
    big2 = alloc([DC, T], BF16)
    wbufs = [alloc([DC, WG], BF16) for _ in range(2)]
    WB = [P.res() for _ in range(2)]
    WSEM = [P.dsem(f"w{i}") for i in range(2)]
    wrot = [0]
    wbase = [(wbufs[i], WB[i], WSEM[i]) for i in range(2)]
    wpool = [list(wbase)]
    wx_count = [0]

    def extra_wbufs(R, n):
        out = []
        for _ in range(n):
            a, r = R.alloc([DC, WG], BF16)
            wx_count[0] += 1
            out.append((a, r, P.dsem(f"wx{wx_count[0]}")))
        return out
    chunks = [(c * 512, 512) for c in range(NQC)] + [(T, NMETA)]
    XN = [P.res() for _ in chunks]
    B1 = {(j, c): P.res() for j in range(DC) for c in range(NQC)}
    B2 = {(j, c): P.res() for j in range(DC) for c in range(NQC)}

    cur[0] = (cur[0] + ATOM - 1) // ATOM * ATOM
    scratch_base = alloc_bytes(0)
    scratch_size = arena_elems * 4 - scratch_base
    scratch_atoms = P.atoms((scratch_size + ATOM - 1) // ATOM)

    class Region:
        def __init__(self):
            self.off = 0

        def alloc(self, shape, dt, psum=False):
            esz = 2 if dt == BF16 else 4
            nb = (int(np.prod(shape)) * esz + ATOM - 1) // ATOM * ATOM
            o = self.off
            self.off += nb
            assert self.off <= scratch_size, f"scratch overflow {self.off} > {scratch_size}"
            a0, a1 = o // ATOM, (o + nb - 1) // ATOM
            return view(scratch_base + o, shape, dt), Res(scratch_atoms[a0:a1 + 1])

        def rot(self, n, shape, dt):
            return Rot([self.alloc(shape, dt) for _ in range(n)])

    psum_all = nc.alloc_psum_tensor("psum_all", [128, 8 * 512], F32)
    banks = [(psum_all[:, i * 512:(i + 1) * 512], P.res(psum=True)) for i in range(8)]

    def MM(out, lhs, rhs, start, stop, reads, writes):
        P.add("pe", lambda h: h.matmul(out, lhs, rhs, start=start, stop=stop), reads=reads, writes=writes)

    def TR(out, in_, ident, reads, writes):
        P.add("pe", lambda h: h.transpose(out, in_, ident), reads=reads, writes=writes)

    def ACT(out, in_, func, reads, writes, **kw):
        P.add("act", lambda h: h.activation(out=out, in_=in_, func=func, **kw), reads=reads, writes=writes)

    def TT(out, in0, in1, op, reads, writes, eng="dve"):
        P.add(eng, lambda h: h.tensor_tensor(out=out, in0=in0, in1=in1, op=op), reads=reads, writes=writes)

    def TS(out, in0, s1, s2, op0, op1, reads, writes, eng="dve"):
        if op1 is None:
            P.add(eng, lambda h: h.tensor_scalar(out=out, in0=in0, scalar1=s1, scalar2=None, op0=op0),
                  reads=reads, writes=writes)
        else:
            P.add(eng, lambda h: h.tensor_scalar(out=out, in0=in0, scalar1=s1, scalar2=s2, op0=op0, op1=op1),
                  reads=reads, writes=writes)

    def STT(out, in0, scalar, in1, op0, op1, reads, writes, eng="dve"):
        P.add(eng, lambda h: h.scalar_tensor_tensor(out=out, in0=in0, scalar=scalar, in1=in1, op0=op0, op1=op1),
              reads=reads, writes=writes)

    def POW(out, in_, col, reads, writes, p0=0, pn=128):
        n = int(np.prod(out.shape[1:]))
        shp = [pn] + list(out.shape[1:])
        e = cexp[p0:p0 + pn, col:col + 1]
        if len(shp) == 2:
            e = e.broadcast_to(shp)
        P.add("pool", lambda h: h.tensor_tensor(out=out, in0=in_, in1=e, op=ALU.pow),
              reads=list(reads) + [CEXP], writes=writes)

    def CP(out, in_, reads, writes, eng="dve"):
        P.add(eng, lambda h: h.tensor_copy(out=out, in_=in_), reads=reads, writes=writes)

    def MSET(ap, val, writes, eng="pool"):
        P.add(eng, lambda h: h.memset(ap, val), writes=writes)

    def DMA(eng, out, in_, reads, writes, dsem):
        P.add(eng, lambda h: h.dma_start(out=out, in_=in_), reads=reads, writes=writes, dsem=dsem)

    def wload(src2d, ncols):
        wb, wr, ws = wpool[0][wrot[0] % len(wpool[0])]
        wrot[0] += 1
        DMA("pool", wb[:, :, 0:ncols], src2d.rearrange("(j p) e -> p j e", p=128), [], [wr], ws)
        return wb, wr

    def mm_group(bank, brs, lhs_list, rhs_list, reads, prows=128, ncols=512):
        n = len(lhs_list)
        for k in range(n):
            MM(bank[0:prows, 0:ncols], lhs_list[k], rhs_list[k], k == 0, k == n - 1, reads, [brs])

    bsel = [0]

    nbset = [list(range(8))]

    def nb(avoid=()):
        while True:
            bsel[0] += 1
            bk = banks[nbset[0][bsel[0] % len(nbset[0])]]
            if all(bk[0] is not a for a in avoid):
                return bk

    csem = P.dsem("const")
    for dst, src in ((cmat, cmat_d), (gN, gN_d), (convw, convw_d), (pvec, pvec_d), (qkg, qkg_d),
                     (cosT, cos_d), (sinT, sin_d)):
        DMA("sp", dst, src, [], [CONST], csem)
    CP(identf, ident_bf, [CONST], [IDF])
    MSET(cexp[:, 0:1], -0.5, [CEXP])
    MSET(cexp[:, 1:2], -1.0, [CEXP])

    conv_b = lambda j: pvec[:, j:j + 1]
    cn_g = lambda j: pvec[:, DC + j:DC + j + 1]
    cn_b = lambda j: pvec[:, 2 * DC + j:2 * DC + j + 1]

    for b in range(BPC):
        wpool[0] = list(wbase)
        R = Region()
        NXB = 6
        xin = R.rot(NXB, [D], F32)
        xin_sems = [P.dsem(f"xin{b}_{i}") for i in range(NXB)]
        sqj, SQJ = R.alloc([D], BF16)
        msr = R.rot(4, [1], F32)
        rsr = R.rot(4, [1], F32)
        xnr = R.rot(3, [D], F32)
        tcount = [0]

        def a_stage1(i):
            rows = 128 if i < NT else NMETA
            src = x_d[b, i * 128:(i + 1) * 128, :] if i < NT else meta_d[:, :]
            kx = xin.i % NXB
            xb, xr = xin.next()
            DMA("sp", xb[0:rows], src, [], [xr], xin_sems[kx])
            ms, msres = msr.next()
            ACT(sqj[0:rows], xb[0:rows], AF.Square, [xr], [SQJ, msres], scale=float(D) ** -0.5,
                accum_out=ms[0:rows, 0:1])
            rs, rsres = rsr.next()
            ACT(rs[0:rows], ms[0:rows], AF.Ln, [msres], [rsres], bias=EPS)
            ACT(rs[0:rows], rs[0:rows], AF.Exp, [rsres], [rsres], scale=-0.5)
            xn, xnres = xnr.next()
            STT(xn[0:rows], xb[0:rows], rs[0:rows, 0:1], gN[0:rows], ALU.mult, ALU.mult,
                [xr, rsres, CONST], [xnres])
            return xn, xnres

        def a_stage2(i, xn, xnres):
            rows = 128 if i < NT else NMETA
            col0 = i * 128 if i < NT else T
            c = (i // 4) if i < NT else NQC
            for g0 in range(0, DC, 4):
                ng = min(4, DC - g0)
                bank, brs = nb()
                tcount[0] += 1
                for jj in range(ng):
                    j = g0 + jj
                    TR(bank[:, jj * 128: jj * 128 + rows], xn[0:rows, j * 128:(j + 1) * 128],
                       identf[0:rows, 0:rows], [xnres, IDF], [brs])
                srcv = bank[:, 0:ng * 128].rearrange("p (j t) -> p j t", t=128)[:, :, 0:rows]
                dstv = xnT[:, g0:g0 + ng, col0:col0 + rows]
                if tcount[0] % 2:
                    ACT(dstv, srcv, AF.Copy, [brs], [XN[c]])
                else:
                    CP(dstv, srcv, [brs], [XN[c]])

        prev = None
        for i in range(NT + 1):
            cur_ = a_stage1(i)
            if prev is not None:
                a_stage2(i - 1, *prev)
            prev = cur_
        a_stage2(NT, *prev)

        R = Region()
        uT = [R.alloc([UW], BF16) for _ in range(CPG)]
        diag = R.rot(2, [CONV_K, 128], BF16)
        sgr = R.rot(2, [512], F32)
        wpool[0] = list(wbase) + extra_wbufs(R, 1)
        for jj in range(CPG):
            ua, ur = uT[jj]
            MSET(ua[:, 0:PAD], 0.0, [ur])
            MSET(ua[:, UW - PAD:UW], 0.0, [ur])

        for s in range(NWG):
            wval, wvr = wload(win_d[:, OFF["val"] + s * WG: OFF["val"] + (s + 1) * WG], WG)
            wglu, wgr = wload(win_d[:, OFF["glu"] + s * WG: OFF["glu"] + (s + 1) * WG], WG)
            for jj in range(CPG):
                j = s * CPG + jj
                ua, ur = uT[jj]
                dg, dgr = diag.next()
                TT(dg, identf.unsqueeze(1).broadcast_to([128, CONV_K, 128]),
                   convw[:, j * CONV_K:(j + 1) * CONV_K].unsqueeze(2).broadcast_to([128, CONV_K, 128]),
                   ALU.mult, [IDF, CONST], [dgr], eng="pool")
                for c, (t0, ncol) in enumerate(chunks):
                    bv, bvr = nb()
                    bg, bgr = nb()
                    mm_group(bv, bvr, [wval[:, k, jj * 128:(jj + 1) * 128] for k in range(DC)],
                             [xnT[:, k, t0:t0 + ncol] for k in range(DC)], [wvr, XN[c]], ncols=ncol)
                    mm_group(bg, bgr, [wglu[:, k, jj * 128:(jj + 1) * 128] for k in range(DC)],
                             [xnT[:, k, t0:t0 + ncol] for k in range(DC)], [wgr, XN[c]], ncols=ncol)
                    sg, sgres = sgr.next()
                    ACT(sg[:, 0:ncol], bg[:, 0:ncol], AF.Sigmoid, [bgr], [sgres])
                    ucol = (PAD + NMETA + t0) if c < NQC else PAD
                    TT(ua[:, ucol:ucol + ncol], bv[:, 0:ncol], sg[:, 0:ncol], ALU.mult, [bvr, sgres], [ur])
                for c in range(NQC):
                    t0 = c * 512
                    bc, bcr = nb()
                    mm_group(bc, bcr, [dg[:, k, :] for k in range(CONV_K)],
                             [ua[:, t0 + NMETA + k: t0 + NMETA + k + 512] for k in range(CONV_K)], [dgr, ur])
                    ACT(big1[:, j, t0:t0 + 512], bc[:, :], AF.Identity, [bcr, CONST], [B1[j, c]], bias=conv_b(j))
                    ACT(big2[:, j, t0:t0 + 512], bc[:, :], AF.Square, [bcr, CONST], [B2[j, c]], bias=conv_b(j))

        (mean, mres), (msq, qres), (rstd, rres), (nmr, nres) = [R.alloc([512], F32) for _ in range(4)]
        cnr = R.rot(2, [512], F32)
        s1r = R.rot(2, [512], BF16)
        szr = R.rot(2, [512], BF16)
        assert NWG <= 2
        wz = [wload(win_d[:, OFF["z"] + s * WG: OFF["z"] + (s + 1) * WG], WG) for s in range(NWG)]
        for c in range(NQC):
            t0 = c * 512
            bs, bsr = nb()
            bq, bqr = nb()
            mm_group(bs, bsr, [onesm] * DC, [big1[:, j, t0:t0 + 512] for j in range(DC)],
                     [CONST] + [B1[j, c] for j in range(DC)])
            mm_group(bq, bqr, [onesm] * DC, [big2[:, j, t0:t0 + 512] for j in range(DC)],
                     [CONST] + [B2[j, c] for j in range(DC)])
            TS(mean, bs[:, :], 1.0 / D, None, ALU.mult, None, [bsr], [mres])
            TT(msq, mean, mean, ALU.mult, [mres], [qres])
            STT(rstd, bq[:, :], 1.0 / D, msq, ALU.mult, ALU.subtract, [bqr, qres], [rres])
            ACT(rstd, rstd, AF.Ln, [rres], [rres], bias=EPS)
            ACT(rstd, rstd, AF.Exp, [rres], [rres], scale=-0.5)
            STT(nmr, mean, -1.0, rstd, ALU.mult, ALU.mult, [mres, rres], [nres])
            for j in range(DC):
                wzv, wzr = wz[j // CPG]
                jj = j % CPG
                bz, bzr = nb()
                mm_group(bz, bzr, [wzv[:, k, jj * 128:(jj + 1) * 128] for k in range(DC)],
                         [xnT[:, k, t0:t0 + 512] for k in range(DC)], [wzr, XN[c]])
                sz, szres = szr.next()
                ACT(sz, bz[:, :], AF.Silu, [bzr], [szres])
                cn, cnres = cnr.next()
                TT(cn, big1[:, j, t0:t0 + 512], rstd, ALU.mult, [B1[j, c], rres], [cnres])
                TT(cn, cn, nmr, ALU.add, [cnres, nres], [cnres])
                s1, s1res = s1r.next()
                ACT(s1, cn, AF.Silu, [cnres, CONST], [s1res], scale=cn_g(j), bias=cn_b(j))
                TT(big1[:, j, t0:t0 + 512], s1, sz, ALU.mult, [s1res, szres], [B1[j, c]])

        sgcr = R.rot(2, [512], F32)
        for s in range(NWG):
            wc, wcr = wload(wco_d[:, s * WG:(s + 1) * WG], WG)
            wg_, wgr_ = wload(win_d[:, OFF["gc"] + s * WG: OFF["gc"] + (s + 1) * WG], WG)
            for mi in range(CPG):
                m = s * CPG + mi
                for c in range(NQC):
                    t0 = c * 512
                    by, byr = nb()
                    bg, bgr = nb()
                    mm_group(by, byr, [wc[:, k, mi * 128:(mi + 1) * 128] for k in range(DC)],
                             [big1[:, k, t0:t0 + 512] for k in range(DC)], [wcr] + [B1[k, c] for k in range(DC)])
                    mm_group(bg, bgr, [wg_[:, k, mi * 128:(mi + 1) * 128] for k in range(DC)],
                             [xnT[:, k, t0:t0 + 512] for k in range(DC)], [wgr_, XN[c]])
                    sg, sgres = sgcr.next()
                    ACT(sg, bg[:, :], AF.Sigmoid, [bgr], [sgres])
                    TT(big2[:, m, t0:t0 + 512], by[:, :], sg, ALU.mult, [byr, sgres], [B2[m, c]])

        wpool[0] = list(wbase)
        R = Region()
        KW = T + 128
        KT = [(R.alloc([KW], BF16), R.alloc([KW], BF16)) for _ in range(NKC)]
        VA, VAr = R.alloc([NT + 1, VW], BF16)
        QTr = R.rot(2, [T], BF16)
        SAZr = R.rot(2, [T], BF16)
        GS = 2 if NQC % 2 == 0 else 1
        PTr = R.rot(2, [GS * 512], BF16)
        sqr_ = R.rot(2, [512], BF16)
        sbr_ = R.rot(2, [512], BF16)
        rsr_ = R.rot(2, [512], F32)
        t1r_ = R.rot(1, [512], F32)
        t2r_ = R.rot(1, [512], F32)
        recr = R.rot(GS, [512], F32)
        osbr = R.rot(GS, [512], F32)

        def prep_qk(wr, lhs_list, gcol, t0, ncol, c, dsts):
            bp, bpr = nb()
            mm_group(bp, bpr, lhs_list, [xnT[:, k, t0:t0 + ncol] for k in range(DC)], [wr, XN[c]], ncols=ncol)
            sq, sqres = sqr_.next()
            sb, sbres = sbr_.next()
            ACT(sq[:, 0:ncol], bp[:, 0:ncol], AF.Square, [bpr], [sqres], scale=0.125)
            ACT(sb[:, 0:ncol], bp[:, 0:ncol], AF.Identity, [bpr, CONST], [sbres], scale=qkg[:, gcol:gcol + 1])
            bss, bssr = nb()
            brt, brtr = nb()
            MM(bss[:, 0:ncol], bones, sq[:, 0:ncol], True, True, [CONST, sqres], [bssr])
            MM(brt[:, 0:ncol], rotm, sb[:, 0:ncol], True, True, [CONST, sbres], [brtr])
            rs, rsres = rsr_.next()
            t1, t1res = t1r_.next()
            t2, t2res = t2r_.next()
            ACT(rs[:, 0:ncol], bss[:, 0:ncol], AF.Ln, [bssr], [rsres], bias=EPS)
            ACT(rs[:, 0:ncol], rs[:, 0:ncol], AF.Exp, [rsres], [rsres], scale=-0.5)
            TT(t1[:, 0:ncol], sb[:, 0:ncol], cosT[:, t0:t0 + ncol], ALU.mult, [sbres, CONST], [t1res])
            TT(t2[:, 0:ncol], brt[:, 0:ncol], sinT[:, t0:t0 + ncol], ALU.mult, [brtr, CONST], [t2res])
            TT(t1[:, 0:ncol], t1[:, 0:ncol], t2[:, 0:ncol], ALU.add, [t1res, t2res], [t1res])
            for dst, dres, p0, p1 in dsts:
                TT(dst[p0:p1, t0:t0 + ncol], t1[p0:p1, 0:ncol], rs[p0:p1, 0:ncol], ALU.mult, [t1res, rsres], [dres])

        def prep_q_stages(wr, lhs_list, gcol, t0, ncol, c, dsts, bk_p, bk_s, bk_r):
            (bp, bpr), (bss, bssr), (brt, brtr) = bk_p, bk_s, bk_r
            st = {}

            def s0():
                mm_group(bp, bpr, lhs_list, [xnT[:, k, t0:t0 + ncol] for k in range(DC)], [wr, XN[c]], ncols=ncol)

            def s1():
                st["sq"] = sqr_.next()
                st["sb"] = sbr_.next()
                sq, sqres = st["sq"]
                sb, sbres = st["sb"]
                ACT(sq[:, 0:ncol], bp[:, 0:ncol], AF.Square, [bpr], [sqres], scale=0.125)
                ACT(sb[:, 0:ncol], bp[:, 0:ncol], AF.Identity, [bpr, CONST], [sbres], scale=qkg[:, gcol:gcol + 1])

            def s2():
                sq, sqres = st["sq"]
                sb, sbres = st["sb"]
                MM(bss[:, 0:ncol], bones, sq[:, 0:ncol], True, True, [CONST, sqres], [bssr])
                MM(brt[:, 0:ncol], rotm, sb[:, 0:ncol], True, True, [CONST, sbres], [brtr])

            def s3():
                st["rs"] = rsr_.next()
                rs, rsres = st["rs"]
                ACT(rs[:, 0:ncol], bss[:, 0:ncol], AF.Ln, [bssr], [rsres], bias=EPS)
                ACT(rs[:, 0:ncol], rs[:, 0:ncol], AF.Exp, [rsres], [rsres], scale=-0.5)

            def s4():
                sb, sbres = st["sb"]
                rs, rsres = st["rs"]
                t1, t1res = t1r_.next()
                t2, t2res = t2r_.next()
                TT(t1[:, 0:ncol], sb[:, 0:ncol], cosT[:, t0:t0 + ncol], ALU.mult, [sbres, CONST], [t1res])
                TT(t2[:, 0:ncol], brt[:, 0:ncol], sinT[:, t0:t0 + ncol], ALU.mult, [brtr, CONST], [t2res])
                TT(t1[:, 0:ncol], t1[:, 0:ncol], t2[:, 0:ncol], ALU.add, [t1res, t2res], [t1res])
                for dst, dres, p0, p1 in dsts:
                    TT(dst[p0:p1, t0:t0 + ncol], t1[p0:p1, 0:ncol], rs[p0:p1, 0:ncol], ALU.mult,
                       [t1res, rsres], [dres])

            return [s0, s1, s2, s3, s4]

        wkv, wkvr = wload(win_d[:, OFF["k"]: OFF["k"] + 2 * KVD], 2 * KVD)
        for kc in range(NKC):
            (kta, ktares), (ktb, ktbres) = KT[kc]
            MSET(kta, 0.0, [ktares])
            MSET(ktb, 0.0, [ktbres])
            for c0 in range(0, len(chunks), 2):
                stl = []
                for ci, c in enumerate([c for c in (c0, c0 + 1) if c < len(chunks)]):
                    t0, ncol = chunks[c]
                    stl.append(prep_q_stages(wkvr, [wkv[:, k, kc * 128:(kc + 1) * 128] for k in range(DC)], 1,
                                             t0, ncol, c, [(kta, ktares, 0, HD), (ktb, ktbres, HD, 128)],
                                             banks[ci], banks[2 + 2 * ci], banks[3 + 2 * ci]))
                for si in range(5):
                    for stg in stl:
                        stg[si]()
        MSET(VA[:, 0:NT, :], 1.0, [VAr])
        MSET(VA[:, NT, :], 0.0, [VAr])
        MSET(VA[0:NMETA, NT, :].rearrange("p (c b d) -> p c b d", b=3, d=HD)[:, :, 1, :], 1.0, [VAr])
        for i in range(NT + 1):
            rows = 128 if i < NT else NMETA
            col0 = i * 128 if i < NT else T
            c = (i // 4) if i < NT else NQC
            bv, bvr = nb()
            mm_group(bv, bvr, [xnT[:, k, col0:col0 + rows] for k in range(DC)],
                     [wkv[:, k, KVD:2 * KVD] for k in range(DC)], [wkvr, XN[c]], prows=rows, ncols=KVD)
            CP(VA[0:rows, i, :].rearrange("p (c b d) -> p c b d", b=3, d=HD)[:, :, ::2, :],
               bv[0:rows, 0:KVD].rearrange("p (c b d) -> p c b d", b=2, d=HD), [bvr], [VAr])

        key_tiles = [(i * 128, 128) for i in range(NT)] + [(T, 128)]
        nbset[0] = [0, 1, 2, 3]
        unit = [0]
        step = [0]
        nk = len(key_tiles)
        for hp in range(NHP):
            wq, wqr = wload(win_d[:, OFF["q"] + hp * 128: OFF["q"] + (hp + 1) * 128], 128)
            waz, wazr = wload(win_d[:, OFF["az"] + hp * 128: OFF["az"] + (hp + 1) * 128], 128)
            qt, qtres = QTr.next()
            saz, sazres = SAZr.next()
            nbset[0] = list(range(8))
            for c0 in range(0, NQC, 2):
                cs = [c for c in (c0, c0 + 1) if c < NQC]
                stl = []
                for ci, c in enumerate(cs):
                    stl.append(prep_q_stages(wqr, [wq[:, k, 0:128] for k in range(DC)], 0, c * 512, 512, c,
                                             [(qt, qtres, 0, 128)],
                                             banks[ci], banks[2 + 2 * ci], banks[3 + 2 * ci]))
                for si in range(5):
                    for stg in stl:
                        stg[si]()
            bas = []
            for c in range(NQC):
                t0 = c * 512
                ba, bar = banks[6 + (c % 2)]
                mm_group(ba, bar, [waz[:, k, 0:128] for k in range(DC)],
                         [xnT[:, k, t0:t0 + 512] for k in range(DC)], [wazr, XN[c]])
                ACT(saz[:, t0:t0 + 512], ba, AF.Silu, [bar], [sazres])
            nbset[0] = [0, 1, 2, 3]
            kc = hp // G

            def make_epi(obanks, r0, d0, cg):
                def epi():
                    epi2 = []
                    for e in range(GS):
                        c = cg * GS + e
                        t0 = c * 512
                        bo, bor = obanks[e]
                        rec, recres = recr.next()
                        osb, osres = osbr.next()
                        CP(osb[r0:r0 + HD], bo[r0:r0 + HD, :], [bor], [osres])
                        CP(rec[r0:r0 + HD], bo[d0:d0 + HD, :], [bor], [recres])
                        epi2.append((rec, recres, osb, osres, t0, c))
                    for rec, recres, osb, osres, t0, c in epi2:
                        P.add("dve", lambda h, o=rec[r0:r0 + HD]: h.reciprocal(out=o, in_=o),
                              reads=[recres], writes=[recres])
                        TT(rec[r0:r0 + HD], rec[r0:r0 + HD], saz[r0:r0 + HD, t0:t0 + 512], ALU.mult,
                           [recres, sazres], [recres])
                        TT(big1[r0:r0 + HD, hp, t0:t0 + 512], osb[r0:r0 + HD], rec[r0:r0 + HD], ALU.mult,
                           [osres, recres], [B1[hp, c]])
                return epi

            pend = None
            for hh in range(2):
                kt, ktres = KT[kc][hh]
                r0 = hh * HD
                d0 = (1 - hh) * HD
                v0 = kc * 3 * HD + hh * HD
                for cg in range(NQC // GS):
                    unit[0] += 1
                    obanks = [banks[4 + e] for e in range(GS)]
                    for i, (k0, krows) in enumerate(key_tiles):
                        step[0] += 1
                        sb0 = (2 * (step[0] % 2)) if GS == 2 else (step[0] % 4)
                        sbanks = [banks[sb0 + e] for e in range(GS)]
                        for e in range(GS):
                            t0 = (cg * GS + e) * 512
                            MM(sbanks[e][0], kt[:, k0:k0 + krows], qt[:, t0:t0 + 512],
                               True, True, [ktres, qtres], [sbanks[e][1]])
                        pt, ptres = PTr.next()
                        ACT(pt, psum_all[:, sb0 * 512:(sb0 + GS) * 512], AF.Exp, [sbk[1] for sbk in sbanks], [ptres],
                            scale=0.125)
                        if pend is not None:
                            for a in pend[0]:
                                MM(*a)
                            if pend[1] is not None:
                                pend[1]()
                        pend = ([(obanks[e][0], VA[0:krows, i, v0:v0 + 128], pt[:, e * 512:(e + 1) * 512],
                                  i == 0, i == nk - 1, [VAr, ptres], [obanks[e][1]]) for e in range(GS)],
                                make_epi(obanks, r0, d0, cg) if i == nk - 1 else None)
            for a in pend[0]:
                MM(*a)
            pend[1]()
        nbset[0] = list(range(8))

        RT = Region()
        sgar = RT.rot(2, [512], F32)
        wpool[0] = list(wbase) + extra_wbufs(RT, 2)
        for s in range(NWG):
            wc, wcr = wload(wao_d[:, s * WG:(s + 1) * WG], WG)
            wg_, wgr_ = wload(win_d[:, OFF["ga"] + s * WG: OFF["ga"] + (s + 1) * WG], WG)
            for mi in range(CPG):
                m = s * CPG + mi
                for c in range(NQC):
                    t0 = c * 512
                    by, byr = nb()
                    bg, bgr = nb()
                    mm_group(by, byr, [wc[:, k, mi * 128:(mi + 1) * 128] for k in range(DC)],
                             [big1[:, k, t0:t0 + 512] for k in range(DC)], [wcr] + [B1[k, c] for k in range(DC)])
                    mm_group(bg, bgr, [wg_[:, k, mi * 128:(mi + 1) * 128] for k in range(DC)],
                             [xnT[:, k, t0:t0 + 512] for k in range(DC)], [wgr_, XN[c]])
                    sg, sgres = sgar.next()
                    ACT(sg, bg[:, :], AF.Sigmoid, [bgr], [sgres])
                    TT(sg, by[:, :], sg, ALU.mult, [byr, sgres], [sgres])
                    TT(big2[:, m, t0:t0 + 512], sg, big2[:, m, t0:t0 + 512], ALU.add,
                       [sgres, B2[m, c]], [B2[m, c]])

        R = RT
        ND = 4
        xres = R.rot(ND, [D], F32)
        xres_sems = [P.dsem(f"xres{b}_{i}") for i in range(ND)]
        ost = R.rot(ND, [D], F32)
        ost_sems = [P.dsem(f"ost{b}_{i}") for i in range(ND)]
        wo = [wload(wout_d[:, s * WG:(s + 1) * WG], WG) for s in range(NWG)]
        for i in range(NT):
            c = i // 4
            kx = xres.i % ND
            xb, xr = xres.next()
            DMA("sp", xb, x_d[b, i * 128:(i + 1) * 128, :], [], [xr], xres_sems[kx])
            ko = ost.i % ND
            ob, obr = ost.next()
            for s in range(NWG):
                wv, wr = wo[s]
                bo, bor = nb()
                mm_group(bo, bor, [big2[:, k, i * 128:(i + 1) * 128] for k in range(DC)],
                         [wv[:, k, 0:WG] for k in range(DC)], [wr] + [B2[k, c] for k in range(DC)], ncols=WG)
                TT(ob[:, s * WG:(s + 1) * WG], bo[:, 0:WG], xb[:, s * WG:(s + 1) * WG], ALU.add,
                   [bor, xr], [obr])
            DMA("sp", out_d[b, i * 128:(i + 1) * 128, :], ob, [obr], [], ost_sems[ko])

    P.emit(nc)
    return nc


def rope_tables_T(cfg):
    T, GW = cfg["T"], cfg["GW"]
    nf = HD // 4
    rows = T // GW
    row_ids = np.repeat(np.arange(rows, dtype=np.float32), GW)
    col_ids = np.tile(np.arange(GW, dtype=np.float32), rows)
    row_ids = np.concatenate([row_ids, np.zeros(NMETA, np.float32)])
    col_ids = np.concatenate([col_ids, np.zeros(NMETA, np.float32)])
    inv_freq = (np.float32(ROPE_THETA) ** (-np.arange(nf, dtype=np.float32) / np.float32(nf))).astype(np.float32)
    a_row = row_ids[:, None] * inv_freq[None, :]
    a_col = col_ids[:, None] * inv_freq[None, :]
    ang = np.concatenate([a_row, a_row, a_col, a_col], axis=-1).astype(np.float32)
    cosT = np.cos(ang).T.astype(np.float32)
    sinT = np.sin(ang).T.astype(np.float32)
    return (np.ascontiguousarray(np.concatenate([cosT, cosT], 0)),
            np.ascontiguousarray(np.concatenate([sinT, sinT], 0)))


def const_mats():
    ident = np.eye(128, dtype=np.float32)
    rot = np.zeros((128, 128), np.float32)
    for m in range(128):
        if (m % 32) < 16:
            rot[m + 16, m] = -1.0
        else:
            rot[m - 16, m] = 1.0
    bones = np.zeros((128, 128), np.float32)
    bones[0:64, 0:64] = 1.0
    bones[64:128, 64:128] = 1.0
    ones = np.ones((128, 128), np.float32)
    return np.ascontiguousarray(np.concatenate([ident, rot, bones, ones], 1).astype(ml_dtypes.bfloat16))


def host_inputs(cfg, x, meta_tokens, norm_g, w_in, conv_w, conv_b, conv_norm_g, conv_norm_b,
                w_conv_out, q_norm_g, k_norm_g, w_attn_out, w_out, n_cores):
    D, BPC = cfg["D"], cfg["BPC"]
    DC = D // 128
    f = lambda a: np.ascontiguousarray(np.asarray(a, dtype=np.float32))
    cosT, sinT = rope_tables_T(cfg)
    convw = f(conv_w[0]).T.reshape(DC, 128, CONV_K).transpose(1, 0, 2).reshape(128, DC * CONV_K)
    pv = lambda v: f(v[0]).reshape(DC, 128).T
    pvec = np.concatenate([pv(conv_b), pv(conv_norm_g), pv(conv_norm_b)], axis=1)
    qkg = np.stack([np.tile(f(q_norm_g[0]), 2), np.tile(f(k_norm_g[0]), 2)], axis=1)
    NH, NKV = cfg["NH"], cfg["NKV"]
    G = NH // NKV
    KVD = NKV * HD
    perm = []
    for kc in range(NKV // 2):
        for r in range(G):
            perm += [2 * kc * G + r, (2 * kc + 1) * G + r]
    perm = np.array(perm)
    w_in_p = f(w_in[0]).copy()
    for off in (3 * D, 4 * D + 2 * KVD):
        blk = w_in_p[:, off:off + D].reshape(D, NH, HD)[:, perm, :].reshape(D, D)
        w_in_p[:, off:off + D] = blk
    w_ao_p = f(w_attn_out[0]).reshape(NH, HD, D)[perm].reshape(D, D)
    shared = {
        "meta": f(meta_tokens), "w_in": f(w_in_p), "w_co": f(w_conv_out[0]), "w_ao": f(w_ao_p),
        "w_out": f(w_out[0]), "gN": f(np.broadcast_to(f(norm_g[0])[None, :], (128, D))),
        "convw": f(convw), "pvec": f(pvec), "qkg": f(qkg), "cosT": cosT, "sinT": sinT, "cmat": const_mats(),
    }
    x = f(x)
    return [dict(shared, x=np.ascontiguousarray(x[i * BPC:(i + 1) * BPC])) for i in range(n_cores)]


_NC_CACHE = {}


def kernel(x, meta_tokens, norm_g, w_in, conv_w, conv_b, conv_norm_g, conv_norm_b,
           w_conv_out, q_norm_g, k_norm_g, w_attn_out, w_out):
    cfg = full_cfg()
    n_cores = 8
    in_maps = host_inputs(cfg, x, meta_tokens, norm_g, w_in, conv_w, conv_b, conv_norm_g, conv_norm_b,
                          w_conv_out, q_norm_g, k_norm_g, w_attn_out, w_out, n_cores)
    nc = build_program(cfg)
    res = run_bass_kernel_spmd(nc, in_maps, core_ids=list(range(n_cores)))
    return np.concatenate([np.asarray(r["out"], dtype=np.float32) for r in res.results], axis=0)
```

```python
import math
import numpy as np
import ml_dtypes
import concourse.bass as bass
import concourse.mybir as mybir
from concourse.bass_utils import run_bass_kernel_spmd

F32 = mybir.dt.float32
BF16 = mybir.dt.bfloat16
AF = mybir.ActivationFunctionType
ALU = mybir.AluOpType

NMETA = 16
CONV_K = 31
PAD = CONV_K // 2
HD = 64
EPS = 1e-6
ROPE_THETA = 10000.0
EPOCH = 12000


def full_cfg():
    return dict(D=1024, T=2048, NH=16, NKV=4, BPC=2, GW=64)


class Res:
    __slots__ = ("atoms", "psum")

    def __init__(self, atoms, psum=False):
        self.atoms = tuple(atoms)
        self.psum = psum


class Op:
    __slots__ = ("idx", "eng", "fn", "deps", "dsem", "dcount", "signal", "sig", "eidx")

    def __init__(self, idx, eng, fn, dsem):
        self.idx = idx
        self.eng = eng
        self.fn = fn
        self.deps = {}
        self.dsem = dsem
        self.dcount = 0
        self.signal = False
        self.sig = 0
        self.eidx = 0


class DSem:
    def __init__(self, name):
        self.name = name
        self.count = 0
        self.handle = None


class Prog:
    ENGS = ("pe", "act", "dve", "pool", "sp")

    def __init__(self):
        self.ops = []
        self.state = {}
        self.natoms = 0
        self.dsems = []

    def atoms(self, n=1):
        a = list(range(self.natoms, self.natoms + n))
        self.natoms += n
        return a

    def res(self, psum=False):
        return Res(self.atoms(1), psum)

    def dsem(self, name):
        d = DSem(name)
        self.dsems.append(d)
        return d

    def add(self, eng, fn, reads=(), writes=(), dsem=None):
        op = Op(len(self.ops), eng, fn, dsem)
        if dsem is not None:
            dsem.count += 1
            op.dcount = dsem.count
        deps = op.deps
        st = self.state

        def dep(o, kind):
            if o is op:
                return
            k = deps.get(o)
            if k is None or kind == "raw":
                deps[o] = kind

        for r in reads:
            for a in r.atoms:
                s = st.get(a)
                if s is None:
                    s = st[a] = [None, {}, []]
                if s[0] is not None:
                    dep(s[0], "raw")
                if r.psum:
                    for e, o in s[1].items():
                        if e != eng:
                            dep(o, "excl")
                if dsem is not None:
                    s[2].append(op)
                else:
                    s[1][eng] = op
        for w in writes:
            for a in w.atoms:
                s = st.get(a)
                if s is None:
                    s = st[a] = [None, {}, []]
                if s[0] is not None:
                    if not (dsem is not None and s[0].dsem is dsem and not s[1] and not s[2]):
                        dep(s[0], "waw")
                for e, o in s[1].items():
                    dep(o, "war")
                for o in s[2]:
                    dep(o, "war")
                s[0] = op
                s[1] = {}
                s[2] = []
        self.ops.append(op)
        return op

    def emit(self, nc):
        ops = self.ops
        for op in ops:
            keep = {}
            for d, kind in op.deps.items():
                if d.dsem is None and d.eng == op.eng and op.dsem is None and op.eng == "pe":
                    continue
                keep[d] = kind
            op.deps = keep
            for d in keep:
                if d.dsem is None:
                    d.signal = True
        cnt = {e: 0 for e in self.ENGS}
        for op in ops:
            if op.dsem is None and op.signal:
                cnt[op.eng] += 1
                op.sig = cnt[op.eng]
        import contextlib

        stack = contextlib.ExitStack()
        with stack:
            esems = {}
            for e in self.ENGS:
                n = (cnt[e] + EPOCH - 1) // EPOCH
                esems[e] = [stack.enter_context(nc.semaphore(f"c_{e}_{i}")) for i in range(max(n, 1))]
            for d in self.dsems:
                d.handle = stack.enter_context(nc.semaphore(f"d_{d.name}"))
            block = stack.enter_context(nc.Block())
            per_eng = {e: [o for o in ops if o.eng == e] for e in self.ENGS}

            def run_engine(ename, h):
                known = {e: 0 for e in self.ENGS}
                dknown = {}
                for op in per_eng[ename]:
                    waits = []
                    need = {}
                    dneed = {}
                    for d in op.deps:
                        if d.dsem is not None:
                            if dknown.get(d.dsem, 0) < d.dcount and dneed.get(d.dsem, 0) < d.dcount:
                                dneed[d.dsem] = d.dcount
                        else:
                            if known[d.eng] < d.sig and need.get(d.eng, 0) < d.sig:
                                need[d.eng] = d.sig
                    for e, s in need.items():
                        known[e] = s
                        ep, loc = (s - 1) // EPOCH, (s - 1) % EPOCH + 1
                        waits.append((esems[e][ep], loc))
                    for ds, c in dneed.items():
                        dknown[ds] = c
                        waits.append((ds.handle, 16 * c))
                    attach = op.dsem is None and len(waits) > 0
                    for sem, val in (waits[:-1] if attach else waits):
                        h.wait_ge(sem, val)
                    ins = op.fn(h)
                    if attach:
                        sem, val = waits[-1]
                        ins._wait_ge(sem, val)
                    if op.dsem is not None:
                        ins.then_inc(op.dsem.handle, 16)
                    elif op.signal:
                        ep = (op.sig - 1) // EPOCH
                        ins.then_inc(esems[op.eng][ep], 1)

            @block.tensor
            def _(h):
                run_engine("pe", h)

            @block.scalar
            def _(h):
                run_engine("act", h)

            @block.vector
            def _(h):
                run_engine("dve", h)

            @block.gpsimd
            def _(h):
                run_engine("pool", h)

            @block.sync
            def _(h):
                run_engine("sp", h)
                for d in self.dsems:
                    if d.count:
                        h.wait_ge(d.handle, 16 * d.count)


class Rot:
    def __init__(self, items):
        self.items = items
        self.i = 0

    def next(self):
        it = self.items[self.i % len(self.items)]
        self.i += 1
        return it


def build_program(cfg):
    D, T, NH, NKV, BPC = cfg["D"], cfg["T"], cfg["NH"], cfg["NKV"], cfg["BPC"]
    DC = D // 128
    KVD = NKV * HD
    G = NH // NKV
    L = T + NMETA
    NT = T // 128
    NQC = T // 512
    NHP = NH // 2
    IN_DIM = 7 * D + 2 * KVD
    OFF = dict(val=0, glu=D, z=2 * D, q=3 * D, k=4 * D, v=4 * D + KVD, az=4 * D + 2 * KVD,
               gc=5 * D + 2 * KVD, ga=6 * D + 2 * KVD)
    UW = T + NMETA + 2 * PAD
    NKC = NKV // 2
    VW = NKC * 3 * HD
    WG = min(512, D)
    NWG = D // WG
    CPG = WG // 128

    nc = bass.Bass("TRN2", target_bir_lowering=False)
    P = Prog()

    def dram(name, shape, dt=F32, kind="ExternalInput"):
        return nc.dram_tensor(name, list(shape), dt, kind=kind).ap()

    x_d = dram("x", [BPC, T, D])
    meta_d = dram("meta", [NMETA, D])
    win_d = dram("w_in", [D, IN_DIM])
    wco_d = dram("w_co", [D, D])
    wao_d = dram("w_ao", [D, D])
    wout_d = dram("w_out", [D, D])
    gN_d = dram("gN", [128, D])
    convw_d = dram("convw", [128, DC * CONV_K])
    pvec_d = dram("pvec", [128, 3 * DC])
    qkg_d = dram("qkg", [128, 2])
    cos_d = dram("cosT", [128, L])
    sin_d = dram("sinT", [128, L])
    cmat_d = dram("cmat", [128, 4 * 128], BF16)
    out_d = dram("out", [BPC, T, D], F32, kind="ExternalOutput")

    arena_elems = nc.sbuf_bytes_remaining // 4 - 64
    arena = nc.alloc_sbuf_tensor("arena", [128, arena_elems], F32)
    ATOM = 512
    cur = [0]

    def alloc_bytes(nbytes):
        nb = (nbytes + 63) // 64 * 64
        o = cur[0]
        cur[0] += nb
        assert cur[0] <= arena_elems * 4, f"SBUF overflow {cur[0]} > {arena_elems * 4}"
        return o

    def view(off, shape, dt):
        esz = 2 if dt == BF16 else 4
        n = int(np.prod(shape))
        assert off % 4 == 0
        a = arena[:, off // 4: off // 4 + (n * esz + 3) // 4]
        if dt == BF16:
            a = a.bitcast(BF16)[:, 0:n]
        if len(shape) == 2:
            a = a.rearrange("p (a b) -> p a b", b=shape[1])
        elif len(shape) == 3:
            a = a.rearrange("p (a b c) -> p a b c", b=shape[1], c=shape[2])
        return a

    def alloc(shape, dt):
        esz = 2 if dt == BF16 else 4
        off = alloc_bytes(int(np.prod(shape)) * esz)
        return view(off, shape, dt)

    cmat = alloc([4 * 128], BF16)
    ident_bf, rotm, bones, onesm = (cmat[:, i * 128:(i + 1) * 128] for i in range(4))
    identf = alloc([128], F32)
    cexp = alloc([2], F32)
    gN = alloc([D], F32)
    convw = alloc([DC * CONV_K], F32)
    pvec = alloc([3 * DC], F32)
    qkg = alloc([2], F32)
    cosT = alloc([L], F32)
    sinT = alloc([L], F32)
    CONST = P.res()
    IDF = P.res()
    CEXP = P.res()
    xnT = alloc([DC, L], BF16)
    big1 = alloc([DC, T], BF16)
    big2 = alloc([DC, T], BF16)
    wbufs = [alloc([DC, WG], BF16) for _ in range(2)]
    WB = [P.res() for _ in range(2)]
    WSEM = [P.dsem(f"w{i}") for i in range(2)]
    wrot = [0]
    wbase = [(wbufs[i], WB[i], WSEM[i]) for i in range(2)]
    wpool = [list(wbase)]
    wx_count = [0]

    def extra_wbufs(R, n):
        out = []
        for _ in range(n):
            a, r = R.alloc([DC, WG], BF16)
            wx_count[0] += 1
            out.append((a, r, P.dsem(f"wx{wx_count[0]}")))
        return out
    chunks = [(c * 512, 512) for c in range(NQC)] + [(T, NMETA)]
    XN = [P.res() for _ in chunks]
    B1 = {(j, c): P.res() for j in range(DC) for c in range(NQC)}
    B2 = {(j, c): P.res() for j in range(DC) for c in range(NQC)}

    cur[0] = (cur[0] + ATOM - 1) // ATOM * ATOM
    scratch_base = alloc_bytes(0)
    scratch_size = arena_elems * 4 - scratch_base
    scratch_atoms = P.atoms((scratch_size + ATOM - 1) // ATOM)

    class Region:
        def __init__(self):
            self.off = 0

        def alloc(self, shape, dt, psum=False):
            esz = 2 if dt == BF16 else 4
            nb = (int(np.prod(shape)) * esz + ATOM - 1) // ATOM * ATOM
            o = self.off
            self.off += nb
            assert self.off <= scratch_size, f"scratch overflow {self.off} > {scratch_size}"
            a0, a1 = o // ATOM, (o + nb - 1) // ATOM
            return view(scratch_base + o, shape, dt), Res(scratch_atoms[a0:a1 + 1])

        def rot(self, n, shape, dt):
            return Rot([self.alloc(shape, dt) for _ in range(n)])

    psum_all = nc.alloc_psum_tensor("psum_all", [128, 8 * 512], F32)
    banks = [(psum_all[:, i * 512:(i + 1) * 512], P.res(psum=True)) for i in range(8)]

    def MM(out, lhs, rhs, start, stop, reads, writes):
        P.add("pe", lambda h: h.matmul(out, lhs, rhs, start=start, stop=stop), reads=reads, writes=writes)

    def TR(out, in_, ident, reads, writes):
        P.add("pe", lambda h: h.transpose(out, in_, ident), reads=reads, writes=writes)

    def ACT(out, in_, func, reads, writes, **kw):
        P.add("act", lambda h: h.activation(out=out, in_=in_, func=func, **kw), reads=reads, writes=writes)

    def TT(out, in0, in1, op, reads, writes, eng="dve"):
        P.add(eng, lambda h: h.tensor_tensor(out=out, in0=in0, in1=in1, op=op), reads=reads, writes=writes)

    def TS(out, in0, s1, s2, op0, op1, reads, writes, eng="dve"):
        if op1 is None:
            P.add(eng, lambda h: h.tensor_scalar(out=out, in0=in0, scalar1=s1, scalar2=None, op0=op0),
                  reads=reads, writes=writes)
        else:
            P.add(eng, lambda h: h.tensor_scalar(out=out, in0=in0, scalar1=s1, scalar2=s2, op0=op0, op1=op1),
                  reads=reads, writes=writes)

    def STT(out, in0, scalar, in1, op0, op1, reads, writes, eng="dve"):
        P.add(eng, lambda h: h.scalar_tensor_tensor(out=out, in0=in0, scalar=scalar, in1=in1, op0=op0, op1=op1),
              reads=reads, writes=writes)

    def POW(out, in_, col, reads, writes, p0=0, pn=128):
        n = int(np.prod(out.shape[1:]))
        shp = [pn] + list(out.shape[1:])
        e = cexp[p0:p0 + pn, col:col + 1]
        if len(shp) == 2:
            e = e.broadcast_to(shp)
        P.add("pool", lambda h: h.tensor_tensor(out=out, in0=in_, in1=e, op=ALU.pow),
              reads=list(reads) + [CEXP], writes=writes)

    def CP(out, in_, reads, writes, eng="dve"):
        P.add(eng, lambda h: h.tensor_copy(out=out, in_=in_), reads=reads, writes=writes)

    def MSET(ap, val, writes, eng="pool"):
        P.add(eng, lambda h: h.memset(ap, val), writes=writes)

    def DMA(eng, out, in_, reads, writes, dsem):
        P.add(eng, lambda h: h.dma_start(out=out, in_=in_), reads=reads, writes=writes, dsem=dsem)

    def wload(src2d, ncols):
        wb, wr, ws = wpool[0][wrot[0] % len(wpool[0])]
        wrot[0] += 1
        DMA("pool", wb[:, :, 0:ncols], src2d.rearrange("(j p) e -> p j e", p=128), [], [wr], ws)
        return wb, wr

    def mm_group(bank, brs, lhs_list, rhs_list, reads, prows=128, ncols=512):
        n = len(lhs_list)
        for k in range(n):
            MM(bank[0:prows, 0:ncols], lhs_list[k], rhs_list[k], k == 0, k == n - 1, reads, [brs])

    bsel = [0]

    nbset = [list(range(8))]

    def nb(avoid=()):
        while True:
            bsel[0] += 1
            bk = banks[nbset[0][bsel[0] % len(nbset[0])]]
            if all(bk[0] is not a for a in avoid):
                return bk

    csem = P.dsem("const")
    for dst, src in ((cmat, cmat_d), (gN, gN_d), (convw, convw_d), (pvec, pvec_d), (qkg, qkg_d),
                     (cosT, cos_d), (sinT, sin_d)):
        DMA("sp", dst, src, [], [CONST], csem)
    CP(identf, ident_bf, [CONST], [IDF])
    MSET(cexp[:, 0:1], -0.5, [CEXP])
    MSET(cexp[:, 1:2], -1.0, [CEXP])

    conv_b = lambda j: pvec[:, j:j + 1]
    cn_g = lambda j: pvec[:, DC + j:DC + j + 1]
    cn_b = lambda j: pvec[:, 2 * DC + j:2 * DC + j + 1]

    for b in range(BPC):
        wpool[0] = list(wbase)
        R = Region()
        NXB = 6
        xin = R.rot(NXB, [D], F32)
        xin_sems = [P.dsem(f"xin{b}_{i}") for i in range(NXB)]
        sqj, SQJ = R.alloc([D], BF16)
        msr = R.rot(4, [1], F32)
        rsr = R.rot(4, [1], F32)
        xnr = R.rot(3, [D], F32)
        tcount = [0]

        def a_stage1(i):
            rows = 128 if i < NT else NMETA
            src = x_d[b, i * 128:(i + 1) * 128, :] if i < NT else meta_d[:, :]
            kx = xin.i % NXB
            xb, xr = xin.next()
            DMA("sp", xb[0:rows], src, [], [xr], xin_sems[kx])
            ms, msres = msr.next()
            ACT(sqj[0:rows], xb[0:rows], AF.Square, [xr], [SQJ, msres], scale=float(D) ** -0.5,
                accum_out=ms[0:rows, 0:1])
            rs, rsres = rsr.next()
            ACT(rs[0:rows], ms[0:rows], AF.Ln, [msres], [rsres], bias=EPS)
            ACT(rs[0:rows], rs[0:rows], AF.Exp, [rsres], [rsres], scale=-0.5)
            xn, xnres = xnr.next()
            STT(xn[0:rows], xb[0:rows], rs[0:rows, 0:1], gN[0:rows], ALU.mult, ALU.mult,
                [xr, rsres, CONST], [xnres])
            return xn, xnres

        def a_stage2(i, xn, xnres):
            rows = 128 if i < NT else NMETA
            col0 = i * 128 if i < NT else T
            c = (i // 4) if i < NT else NQC
            for g0 in range(0, DC, 4):
                ng = min(4, DC - g0)
                bank, brs = nb()
                tcount[0] += 1
                for jj in range(ng):
                    j = g0 + jj
                    TR(bank[:, jj * 128: jj * 128 + rows], xn[0:rows, j * 128:(j + 1) * 128],
                       identf[0:rows, 0:rows], [xnres, IDF], [brs])
                srcv = bank[:, 0:ng * 128].rearrange("p (j t) -> p j t", t=128)[:, :, 0:rows]
                dstv = xnT[:, g0:g0 + ng, col0:col0 + rows]
                if tcount[0] % 2:
                    ACT(dstv, srcv, AF.Copy, [brs], [XN[c]])
                else:
                    CP(dstv, srcv, [brs], [XN[c]])

        prev = None
        for i in range(NT + 1):
            cur_ = a_stage1(i)
            if prev is not None:
                a_stage2(i - 1, *prev)
            prev = cur_
        a_stage2(NT, *prev)

        R = Region()
        uT = [R.alloc([UW], BF16) for _ in range(CPG)]
        diag = R.rot(2, [CONV_K, 128], BF16)
        sgr = R.rot(2, [512], F32)
        wpool[0] = list(wbase) + extra_wbufs(R, 1)
        for jj in range(CPG):
            ua, ur = uT[jj]
            MSET(ua[:, 0:PAD], 0.0, [ur])
            MSET(ua[:, UW - PAD:UW], 0.0, [ur])

        for s in range(NWG):
            wval, wvr = wload(win_d[:, OFF["val"] + s * WG: OFF["val"] + (s + 1) * WG], WG)
            wglu, wgr = wload(win_d[:, OFF["glu"] + s * WG: OFF["glu"] + (s + 1) * WG], WG)
            for jj in range(CPG):
                j = s * CPG + jj
                ua, ur = uT[jj]
                dg, dgr = diag.next()
                TT(dg, identf.unsqueeze(1).broadcast_to([128, CONV_K, 128]),
                   convw[:, j * CONV_K:(j + 1) * CONV_K].unsqueeze(2).broadcast_to([128, CONV_K, 128]),
                   ALU.mult, [IDF, CONST], [dgr], eng="pool")
                for c, (t0, ncol) in enumerate(chunks):
                    bv, bvr = nb()
                    bg, bgr = nb()
                    mm_group(bv, bvr, [wval[:, k, jj * 128:(jj + 1) * 128] for k in range(DC)],
                             [xnT[:, k, t0:t0 + ncol] for k in range(DC)], [wvr, XN[c]], ncols=ncol)
                    mm_group(bg, bgr, [wglu[:, k, jj * 128:(jj + 1) * 128] for k in range(DC)],
                             [xnT[:, k, t0:t0 + ncol] for k in range(DC)], [wgr, XN[c]], ncols=ncol)
                    sg, sgres = sgr.next()
                    ACT(sg[:, 0:ncol], bg[:, 0:ncol], AF.Sigmoid, [bgr], [sgres])
                    ucol = (PAD + NMETA + t0) if c < NQC else PAD
                    TT(ua[:, ucol:ucol + ncol], bv[:, 0:ncol], sg[:, 0:ncol], ALU.mult, [bvr, sgres], [ur])
                for c in range(NQC):
                    t0 = c * 512
                    bc, bcr = nb()
                    mm_group(bc, bcr, [dg[:, k, :] for k in range(CONV_K)],
                             [ua[:, t0 + NMETA + k: t0 + NMETA + k + 512] for k in range(CONV_K)], [dgr, ur])
                    ACT(big1[:, j, t0:t0 + 512], bc[:, :], AF.Identity, [bcr, CONST], [B1[j, c]], bias=conv_b(j))
                    ACT(big2[:, j, t0:t0 + 512], bc[:, :], AF.Square, [bcr, CONST], [B2[j, c]], bias=conv_b(j))

        (mean, mres), (msq, qres), (rstd, rres), (nmr, nres) = [R.alloc([512], F32) for _ in range(4)]
        cnr = R.rot(2, [512], F32)
        s1r = R.rot(2, [512], BF16)
        szr = R.rot(2, [512], BF16)
        assert NWG <= 2
        wz = [wload(win_d[:, OFF["z"] + s * WG: OFF["z"] + (s + 1) * WG], WG) for s in range(NWG)]
        for c in range(NQC):
            t0 = c * 512
            bs, bsr = nb()
            bq, bqr = nb()
            mm_group(bs, bsr, [onesm] * DC, [big1[:, j, t0:t0 + 512] for j in range(DC)],
                     [CONST] + [B1[j, c] for j in range(DC)])
            mm_group(bq, bqr, [onesm] * DC, [big2[:, j, t0:t0 + 512] for j in range(DC)],
                     [CONST] + [B2[j, c] for j in range(DC)])
            TS(mean, bs[:, :], 1.0 / D, None, ALU.mult, None, [bsr], [mres])
            TT(msq, mean, mean, ALU.mult, [mres], [qres])
            STT(rstd, bq[:, :], 1.0 / D, msq, ALU.mult, ALU.subtract, [bqr, qres], [rres])
            ACT(rstd, rstd, AF.Ln, [rres], [rres], bias=EPS)
            ACT(rstd, rstd, AF.Exp, [rres], [rres], scale=-0.5)
            STT(nmr, mean, -1.0, rstd, ALU.mult, ALU.mult, [mres, rres], [nres])
            for j in range(DC):
                wzv, wzr = wz[j // CPG]
                jj = j % CPG
                bz, bzr = nb()
                mm_group(bz, bzr, [wzv[:, k, jj * 128:(jj + 1) * 128] for k in range(DC)],
                         [xnT[:, k, t0:t0 + 512] for k in range(DC)], [wzr, XN[c]])
                sz, szres = szr.next()
                ACT(sz, bz[:, :], AF.Silu, [bzr], [szres])
                cn, cnres = cnr.next()
                TT(cn, big1[:, j, t0:t0 + 512], rstd, ALU.mult, [B1[j, c], rres], [cnres])
                TT(cn, cn, nmr, ALU.add, [cnres, nres], [cnres])
                s1, s1res = s1r.next()
                ACT(s1, cn, AF.Silu, [cnres, CONST], [s1res], scale=cn_g(j), bias=cn_b(j))
                TT(big1[:, j, t0:t0 + 512], s1, sz, ALU.mult, [s1res, szres], [B1[j, c]])

        sgcr = R.rot(2, [512], F32)
        for s in range(NWG):
            wc, wcr = wload(wco_d[:, s * WG:(s + 1) * WG], WG)
            wg_, wgr_ = wload(win_d[:, OFF["gc"] + s * WG: OFF["gc"] + (s + 1) * WG], WG)
            for mi in range(CPG):
                m = s * CPG + mi
                for c in range(NQC):
                    t0 = c * 512
                    by, byr = nb()
                    bg, bgr = nb()
                    mm_group(by, byr, [wc[:, k, mi * 128:(mi + 1) * 128] for k in range(DC)],
                             [big1[:, k, t0:t0 + 512] for k in range(DC)], [wcr] + [B1[k, c] for k in range(DC)])
                    mm_group(bg, bgr, [wg_[:, k, mi * 128:(mi + 1) * 128] for k in range(DC)],
                             [xnT[:, k, t0:t0 + 512] for k in range(DC)], [wgr_, XN[c]])
                    sg, sgres = sgcr.next()
                    ACT(sg, bg[:, :], AF.Sigmoid, [bgr], [sgres])
                    TT(big2[:, m, t0:t0 + 512], by[:, :], sg, ALU.mult, [byr, sgres], [B2[m, c]])

        wpool[0] = list(wbase)
        R = Region()
        KW = T + 128
        KT = [(R.alloc([KW], BF16), R.alloc([KW], BF16)) for _ in range(NKC)]
        VA, VAr = R.alloc([NT + 1, VW], BF16)
        QTr = R.rot(2, [T], BF16)
        SAZr = R.rot(2, [T], BF16)
        GS = 2 if NQC % 2 == 0 else 1
        PTr = R.rot(3, [GS * 512], BF16)
        sqr_ = R.rot(2, [512], BF16)
        sbr_ = R.rot(2, [512], BF16)
        rsr_ = R.rot(2, [512], F32)
        t1r_ = R.rot(2, [512], F32)
        t2r_ = R.rot(1, [512], F32)
        recr = R.rot(2, [512], F32)

        def prep_qk(wr, lhs_list, gcol, t0, ncol, c, dsts):
            bp, bpr = nb()
            mm_group(bp, bpr, lhs_list, [xnT[:, k, t0:t0 + ncol] for k in range(DC)], [wr, XN[c]], ncols=ncol)
            sq, sqres = sqr_.next()
            sb, sbres = sbr_.next()
            ACT(sq[:, 0:ncol], bp[:, 0:ncol], AF.Square, [bpr], [sqres], scale=0.125)
            ACT(sb[:, 0:ncol], bp[:, 0:ncol], AF.Identity, [bpr, CONST], [sbres], scale=qkg[:, gcol:gcol + 1])
            bss, bssr = nb()
            brt, brtr = nb()
            MM(bss[:, 0:ncol], bones, sq[:, 0:ncol], True, True, [CONST, sqres], [bssr])
            MM(brt[:, 0:ncol], rotm, sb[:, 0:ncol], True, True, [CONST, sbres], [brtr])
            rs, rsres = rsr_.next()
            t1, t1res = t1r_.next()
            t2, t2res = t2r_.next()
            ACT(rs[:, 0:ncol], bss[:, 0:ncol], AF.Ln, [bssr], [rsres], bias=EPS)
            ACT(rs[:, 0:ncol], rs[:, 0:ncol], AF.Exp, [rsres], [rsres], scale=-0.5)
            TT(t1[:, 0:ncol], sb[:, 0:ncol], cosT[:, t0:t0 + ncol], ALU.mult, [sbres, CONST], [t1res])
            TT(t2[:, 0:ncol], brt[:, 0:ncol], sinT[:, t0:t0 + ncol], ALU.mult, [brtr, CONST], [t2res])
            TT(t1[:, 0:ncol], t1[:, 0:ncol], t2[:, 0:ncol], ALU.add, [t1res, t2res], [t1res])
            for dst, dres, p0, p1 in dsts:
                TT(dst[p0:p1, t0:t0 + ncol], t1[p0:p1, 0:ncol], rs[p0:p1, 0:ncol], ALU.mult, [t1res, rsres], [dres])

        def prep_q_stages(wr, lhs_list, gcol, t0, ncol, c, dsts, bk_p, bk_s, bk_r):
            (bp, bpr), (bss, bssr), (brt, brtr) = bk_p, bk_s, bk_r
            st = {}

            def s0():
                mm_group(bp, bpr, lhs_list, [xnT[:, k, t0:t0 + ncol] for k in range(DC)], [wr, XN[c]], ncols=ncol)

            def s1():
                st["sq"] = sqr_.next()
                st["sb"] = sbr_.next()
                sq, sqres = st["sq"]
                sb, sbres = st["sb"]
                ACT(sq[:, 0:ncol], bp[:, 0:ncol], AF.Square, [bpr], [sqres], scale=0.125)
                TS(sb[:, 0:ncol], bp[:, 0:ncol], qkg[:, gcol:gcol + 1], None, ALU.mult, None, [bpr, CONST], [sbres])

            def s2():
                sq, sqres = st["sq"]
                sb, sbres = st["sb"]
                MM(bss[:, 0:ncol], bones, sq[:, 0:ncol], True, True, [CONST, sqres], [bssr])
                MM(brt[:, 0:ncol], rotm, sb[:, 0:ncol], True, True, [CONST, sbres], [brtr])

            def s3():
                st["rs"] = rsr_.next()
                rs, rsres = st["rs"]
                ACT(rs[:, 0:ncol], bss[:, 0:ncol], AF.Ln, [bssr], [rsres], bias=EPS)
                ACT(rs[:, 0:ncol], rs[:, 0:ncol], AF.Exp, [rsres], [rsres], scale=-0.5)

            def s4():
                sb, sbres = st["sb"]
                rs, rsres = st["rs"]
                t1, t1res = t1r_.next()
                t2, t2res = t2r_.next()
                TT(t1[:, 0:ncol], sb[:, 0:ncol], cosT[:, t0:t0 + ncol], ALU.mult, [sbres, CONST], [t1res])
                TT(t2[:, 0:ncol], brt[:, 0:ncol], sinT[:, t0:t0 + ncol], ALU.mult, [brtr, CONST], [t2res])
                TT(t1[:, 0:ncol], t1[:, 0:ncol], t2[:, 0:ncol], ALU.add, [t1res, t2res], [t1res])
                for dst, dres, p0, p1 in dsts:
                    TT(dst[p0:p1, t0:t0 + ncol], t1[p0:p1, 0:ncol], rs[p0:p1, 0:ncol], ALU.mult,
                       [t1res, rsres], [dres])

            return [s0, s1, s2, s3, s4]

        wkv, wkvr = wload(win_d[:, OFF["k"]: OFF["k"] + 2 * KVD], 2 * KVD)
        for kc in range(NKC):
            (kta, ktares), (ktb, ktbres) = KT[kc]
            MSET(kta, 0.0, [ktares])
            MSET(ktb, 0.0, [ktbres])
            for c0 in range(0, len(chunks), 2):
                stl = []
                for ci, c in enumerate([c for c in (c0, c0 + 1) if c < len(chunks)]):
                    t0, ncol = chunks[c]
                    stl.append(prep_q_stages(wkvr, [wkv[:, k, kc * 128:(kc + 1) * 128] for k in range(DC)], 1,
                                             t0, ncol, c, [(kta, ktares, 0, HD), (ktb, ktbres, HD, 128)],
                                             banks[ci], banks[2 + 2 * ci], banks[3 + 2 * ci]))
                for si in range(5):
                    for stg in stl:
                        stg[si]()
        MSET(VA[:, 0:NT, :], 1.0, [VAr])
        MSET(VA[:, NT, :], 0.0, [VAr])
        MSET(VA[0:NMETA, NT, :].rearrange("p (c b d) -> p c b d", b=3, d=HD)[:, :, 1, :], 1.0, [VAr])
        for i in range(NT + 1):
            rows = 128 if i < NT else NMETA
            col0 = i * 128 if i < NT else T
            c = (i // 4) if i < NT else NQC
            bv, bvr = nb()
            mm_group(bv, bvr, [xnT[:, k, col0:col0 + rows] for k in range(DC)],
                     [wkv[:, k, KVD:2 * KVD] for k in range(DC)], [wkvr, XN[c]], prows=rows, ncols=KVD)
            CP(VA[0:rows, i, :].rearrange("p (c b d) -> p c b d", b=3, d=HD)[:, :, ::2, :],
               bv[0:rows, 0:KVD].rearrange("p (c b d) -> p c b d", b=2, d=HD), [bvr], [VAr])

        key_tiles = [(i * 128, 128) for i in range(NT)] + [(T, 128)]
        nbset[0] = [0, 1, 2, 3]
        unit = [0]
        step = [0]
        nk = len(key_tiles)
        for hp in range(NHP):
            wq, wqr = wload(win_d[:, OFF["q"] + hp * 128: OFF["q"] + (hp + 1) * 128], 128)
            waz, wazr = wload(win_d[:, OFF["az"] + hp * 128: OFF["az"] + (hp + 1) * 128], 128)
            qt, qtres = QTr.next()
            saz, sazres = SAZr.next()
            nbset[0] = list(range(8))
            for c0 in range(0, NQC, 2):
                cs = [c for c in (c0, c0 + 1) if c < NQC]
                stl = []
                for ci, c in enumerate(cs):
                    stl.append(prep_q_stages(wqr, [wq[:, k, 0:128] for k in range(DC)], 0, c * 512, 512, c,
                                             [(qt, qtres, 0, 128)],
                                             banks[ci], banks[2 + 2 * ci], banks[3 + 2 * ci]))
                for si in range(5):
                    for stg in stl:
                        stg[si]()
            bas = []
            for c in range(NQC):
                t0 = c * 512
                ba, bar = banks[6 + (c % 2)]
                mm_group(ba, bar, [waz[:, k, 0:128] for k in range(DC)],
                         [xnT[:, k, t0:t0 + 512] for k in range(DC)], [wazr, XN[c]])
                ACT(saz[:, t0:t0 + 512], ba, AF.Silu, [bar], [sazres])
            nbset[0] = [0, 1, 2, 3]
            kc = hp // G

            def make_epi(obanks, r0, d0, cg):
                def epi():
                    for e in range(GS):
                        c = cg * GS + e
                        t0 = c * 512
                        bo, bor = obanks[e]
                        rec, recres = recr.next()
                        P.add("dve", lambda h, o=rec[r0:r0 + HD], i_=bo[d0:d0 + HD, :]: h.reciprocal(out=o, in_=i_),
                              reads=[bor], writes=[recres])
                        TT(rec[r0:r0 + HD], rec[r0:r0 + HD], saz[r0:r0 + HD, t0:t0 + 512], ALU.mult,
                           [recres, sazres], [recres])
                        TT(big1[r0:r0 + HD, hp, t0:t0 + 512], bo[r0:r0 + HD, :], rec[r0:r0 + HD], ALU.mult,
                           [bor, recres], [B1[hp, c]])
                return epi

            pend = None
            for hh in range(2):
                kt, ktres = KT[kc][hh]
                r0 = hh * HD
                d0 = (1 - hh) * HD
                v0 = kc * 3 * HD + hh * HD
                for cg in range(NQC // GS):
                    unit[0] += 1
                    obanks = [banks[4 + 2 * (unit[0] % 2) + e] if GS == 2 else banks[4 + (unit[0] % 4)]
                              for e in range(GS)]
                    for i, (k0, krows) in enumerate(key_tiles):
                        step[0] += 1
                        sb0 = (2 * (step[0] % 2)) if GS == 2 else (step[0] % 4)
                        sbanks = [banks[sb0 + e] for e in range(GS)]
                        for e in range(GS):
                            t0 = (cg * GS + e) * 512
                            MM(sbanks[e][0], kt[:, k0:k0 + krows], qt[:, t0:t0 + 512],
                               True, True, [ktres, qtres], [sbanks[e][1]])
                        pt, ptres = PTr.next()
                        ACT(pt, psum_all[:, sb0 * 512:(sb0 + GS) * 512], AF.Exp, [sbk[1] for sbk in sbanks], [ptres],
                            scale=0.125)
                        if pend is not None:
                            for a in pend[0]:
                                MM(*a)
                            if pend[1] is not None:
                                pend[1]()
                        pend = ([(obanks[e][0], VA[0:krows, i, v0:v0 + 128], pt[:, e * 512:(e + 1) * 512],
                                  i == 0, i == nk - 1, [VAr, ptres], [obanks[e][1]]) for e in range(GS)],
                                make_epi(obanks, r0, d0, cg) if i == nk - 1 else None)
            for a in pend[0]:
                MM(*a)
            pend[1]()
        nbset[0] = list(range(8))

        RT = Region()
        sgar = RT.rot(2, [512], F32)
        wpool[0] = list(wbase) + extra_wbufs(RT, 2)
        for s in range(NWG):
            wc, wcr = wload(wao_d[:, s * WG:(s + 1) * WG], WG)
            wg_, wgr_ = wload(win_d[:, OFF["ga"] + s * WG: OFF["ga"] + (s + 1) * WG], WG)
            for mi in range(CPG):
                m = s * CPG + mi
                for c in range(NQC):
                    t0 = c * 512
                    by, byr = nb()
                    bg, bgr = nb()
                    mm_group(by, byr, [wc[:, k, mi * 128:(mi + 1) * 128] for k in range(DC)],
                             [big1[:, k, t0:t0 + 512] for k in range(DC)], [wcr] + [B1[k, c] for k in range(DC)])
                    mm_group(bg, bgr, [wg_[:, k, mi * 128:(mi + 1) * 128] for k in range(DC)],
                             [xnT[:, k, t0:t0 + 512] for k in range(DC)], [wgr_, XN[c]])
                    sg, sgres = sgar.next()
                    ACT(sg, bg[:, :], AF.Sigmoid, [bgr], [sgres])
                    TT(sg, by[:, :], sg, ALU.mult, [byr, sgres], [sgres])
                    TT(big2[:, m, t0:t0 + 512], sg, big2[:, m, t0:t0 + 512], ALU.add,
                       [sgres, B2[m, c]], [B2[m, c]])

        R = RT
        ND = 4
        xres = R.rot(ND, [D], F32)
        xres_sems = [P.dsem(f"xres{b}_{i}") for i in range(ND)]
        ost = R.rot(ND, [D], F32)
        ost_sems = [P.dsem(f"ost{b}_{i}") for i in range(ND)]
        wo = [wload(wout_d[:, s * WG:(s + 1) * WG], WG) for s in range(NWG)]
        for i in range(NT):
            c = i // 4
            kx = xres.i % ND
            xb, xr = xres.next()
            DMA("sp", xb, x_d[b, i * 128:(i + 1) * 128, :], [], [xr], xres_sems[kx])
            ko = ost.i % ND
            ob, obr = ost.next()
            for s in range(NWG):
                wv, wr = wo[s]
                bo, bor = nb()
                mm_group(bo, bor, [big2[:, k, i * 128:(i + 1) * 128] for k in range(DC)],
                         [wv[:, k, 0:WG] for k in range(DC)], [wr] + [B2[k, c] for k in range(DC)], ncols=WG)
                TT(ob[:, s * WG:(s + 1) * WG], bo[:, 0:WG], xb[:, s * WG:(s + 1) * WG], ALU.add,
                   [bor, xr], [obr])
            DMA("sp", out_d[b, i * 128:(i + 1) * 128, :], ob, [obr], [], ost_sems[ko])

    P.emit(nc)
    return nc


def rope_tables_T(cfg):
    T, GW = cfg["T"], cfg["GW"]
    nf = HD // 4
    rows = T // GW
    row_ids = np.repeat(np.arange(rows, dtype=np.float32), GW)
    col_ids = np.tile(np.arange(GW, dtype=np.float32), rows)
    row_ids = np.concatenate([row_ids, np.zeros(NMETA, np.float32)])
    col_ids = np.concatenate([col_ids, np.zeros(NMETA, np.float32)])
    inv_freq = (np.float32(ROPE_THETA) ** (-np.arange(nf, dtype=np.float32) / np.float32(nf))).astype(np.float32)
    a_row = row_ids[:, None] * inv_freq[None, :]
    a_col = col_ids[:, None] * inv_freq[None, :]
    ang = np.concatenate([a_row, a_row, a_col, a_col], axis=-1).astype(np.float32)
    cosT = np.cos(ang).T.astype(np.float32)
    sinT = np.sin(ang).T.astype(np.float32)
    return (np.ascontiguousarray(np.concatenate([cosT, cosT], 0)),
            np.ascontiguousarray(np.concatenate([sinT, sinT], 0)))


def const_mats():
    ident = np.eye(128, dtype=np.float32)
    rot = np.zeros((128, 128), np.float32)
    for m in range(128):
        if (m % 32) < 16:
            rot[m + 16, m] = -1.0
        else:
            rot[m - 16, m] = 1.0
    bones = np.zeros((128, 128), np.float32)
    bones[0:64, 0:64] = 1.0
    bones[64:128, 64:128] = 1.0
    ones = np.ones((128, 128), np.float32)
    return np.ascontiguousarray(np.concatenate([ident, rot, bones, ones], 1).astype(ml_dtypes.bfloat16))


def host_inputs(cfg, x, meta_tokens, norm_g, w_in, conv_w, conv_b, conv_norm_g, conv_norm_b,
                w_conv_out, q_norm_g, k_norm_g, w_attn_out, w_out, n_cores):
    D, BPC = cfg["D"], cfg["BPC"]
    DC = D // 128
    f = lambda a: np.ascontiguousarray(np.asarray(a, dtype=np.float32))
    cosT, sinT = rope_tables_T(cfg)
    convw = f(conv_w[0]).T.reshape(DC, 128, CONV_K).transpose(1, 0, 2).reshape(128, DC * CONV_K)
    pv = lambda v: f(v[0]).reshape(DC, 128).T
    pvec = np.concatenate([pv(conv_b), pv(conv_norm_g), pv(conv_norm_b)], axis=1)
    qkg = np.stack([np.tile(f(q_norm_g[0]), 2), np.tile(f(k_norm_g[0]), 2)], axis=1)
    NH, NKV = cfg["NH"], cfg["NKV"]
    G = NH // NKV
    KVD = NKV * HD
    perm = []
    for kc in range(NKV // 2):
        for r in range(G):
            perm += [2 * kc * G + r, (2 * kc + 1) * G + r]
    perm = np.array(perm)
    w_in_p = f(w_in[0]).copy()
    for off in (3 * D, 4 * D + 2 * KVD):
        blk = w_in_p[:, off:off + D].reshape(D, NH, HD)[:, perm, :].reshape(D, D)
        w_in_p[:, off:off + D] = blk
    w_ao_p = f(w_attn_out[0]).reshape(NH, HD, D)[perm].reshape(D, D)
    shared = {
        "meta": f(meta_tokens), "w_in": f(w_in_p), "w_co": f(w_conv_out[0]), "w_ao": f(w_ao_p),
        "w_out": f(w_out[0]), "gN": f(np.broadcast_to(f(norm_g[0])[None, :], (128, D))),
        "convw": f(convw), "pvec": f(pvec), "qkg": f(qkg), "cosT": cosT, "sinT": sinT, "cmat": const_mats(),
    }
    x = f(x)
    return [dict(shared, x=np.ascontiguousarray(x[i * BPC:(i + 1) * BPC])) for i in range(n_cores)]


_NC_CACHE = {}


def kernel(x, meta_tokens, norm_g, w_in, conv_w, conv_b, conv_norm_g, conv_norm_b,
           w_conv_out, q_norm_g, k_norm_g, w_attn_out, w_out):
    cfg = full_cfg()
    n_cores = 8
    in_maps = host_inputs(cfg, x, meta_tokens, norm_g, w_in, conv_w, conv_b, conv_norm_g, conv_norm_b,
                          w_conv_out, q_norm_g, k_norm_g, w_attn_out, w_out, n_cores)
    nc = build_program(cfg)
    res = run_bass_kernel_spmd(nc, in_maps, core_ids=list(range(n_cores)))
    return np.concatenate([np.asarray(r["out"], dtype=np.float32) for r in res.results], axis=0)
```

```python
import math
import numpy as np
import ml_dtypes
import concourse.bass as bass
import concourse.mybir as mybir
from concourse.bass_utils import run_bass_kernel_spmd

F32 = mybir.dt.float32
BF16 = mybir.dt.bfloat16
AF = mybir.ActivationFunctionType
ALU = mybir.AluOpType

NMETA = 16
CONV_K = 31
PAD = CONV_K // 2
HD = 64
EPS = 1e-6
ROPE_THETA = 10000.0
EPOCH = 12000


def full_cfg():
    return dict(D=1024, T=2048, NH=16, NKV=4, BPC=2, GW=64)


class Res:
    __slots__ = ("atoms", "psum")

    def __init__(self, atoms, psum=False):
        self.atoms = tuple(atoms)
        self.psum = psum


class Op:
    __slots__ = ("idx", "eng", "fn", "deps", "dsem", "dcount", "signal", "sig", "eidx")

    def __init__(self, idx, eng, fn, dsem):
        self.idx = idx
        self.eng = eng
        self.fn = fn
        self.deps = {}
        self.dsem = dsem
        self.dcount = 0
        self.signal = False
        self.sig = 0
        self.eidx = 0


class DSem:
    def __init__(self, name):
        self.name = name
        self.count = 0
        self.handle = None


class Prog:
    ENGS = ("pe", "act", "dve", "pool", "sp")

    def __init__(self):
        self.ops = []
        self.state = {}
        self.natoms = 0
        self.dsems = []

    def atoms(self, n=1):
        a = list(range(self.natoms, self.natoms + n))
        self.natoms += n
        return a

    def res(self, psum=False):
        return Res(self.atoms(1), psum)

    def dsem(self, name):
        d = DSem(name)
        self.dsems.append(d)
        return d

    def add(self, eng, fn, reads=(), writes=(), dsem=None):
        op = Op(len(self.ops), eng, fn, dsem)
        if dsem is not None:
            dsem.count += 1
            op.dcount = dsem.count
        deps = op.deps
        st = self.state

        def dep(o, kind):
            if o is op:
                return
            k = deps.get(o)
            if k is None or kind == "raw":
                deps[o] = kind

        for r in reads:
            for a in r.atoms:
                s = st.get(a)
                if s is None:
                    s = st[a] = [None, {}, []]
                if s[0] is not None:
                    dep(s[0], "raw")
                if r.psum:
                    for e, o in s[1].items():
                        if e != eng:
                            dep(o, "excl")
                if dsem is not None:
                    s[2].append(op)
                else:
                    s[1][eng] = op
        for w in writes:
            for a in w.atoms:
                s = st.get(a)
                if s is None:
                    s = st[a] = [None, {}, []]
                if s[0] is not None:
                    if not (dsem is not None and s[0].dsem is dsem and not s[1] and not s[2]):
                        dep(s[0], "waw")
                for e, o in s[1].items():
                    dep(o, "war")
                for o in s[2]:
                    dep(o, "war")
                s[0] = op
                s[1] = {}
                s[2] = []
        self.ops.append(op)
        return op

    def emit(self, nc):
        ops = self.ops
        for op in ops:
            keep = {}
            for d, kind in op.deps.items():
                if d.dsem is None and d.eng == op.eng and op.dsem is None and op.eng == "pe":
                    continue
                keep[d] = kind
            op.deps = keep
            for d in keep:
                if d.dsem is None:
                    d.signal = True
        cnt = {e: 0 for e in self.ENGS}
        for op in ops:
            if op.dsem is None and op.signal:
                cnt[op.eng] += 1
                op.sig = cnt[op.eng]
        import contextlib

        stack = contextlib.ExitStack()
        with stack:
            esems = {}
            for e in self.ENGS:
                n = (cnt[e] + EPOCH - 1) // EPOCH
                esems[e] = [stack.enter_context(nc.semaphore(f"c_{e}_{i}")) for i in range(max(n, 1))]
            for d in self.dsems:
                d.handle = stack.enter_context(nc.semaphore(f"d_{d.name}"))
            block = stack.enter_context(nc.Block())
            per_eng = {e: [o for o in ops if o.eng == e] for e in self.ENGS}

            def run_engine(ename, h):
                known = {e: 0 for e in self.ENGS}
                dknown = {}
                for op in per_eng[ename]:
                    waits = []
                    need = {}
                    dneed = {}
                    for d in op.deps:
                        if d.dsem is not None:
                            if dknown.get(d.dsem, 0) < d.dcount and dneed.get(d.dsem, 0) < d.dcount:
                                dneed[d.dsem] = d.dcount
                        else:
                            if known[d.eng] < d.sig and need.get(d.eng, 0) < d.sig:
                                need[d.eng] = d.sig
                    for e, s in need.items():
                        known[e] = s
                        ep, loc = (s - 1) // EPOCH, (s - 1) % EPOCH + 1
                        waits.append((esems[e][ep], loc))
                    for ds, c in dneed.items():
                        dknown[ds] = c
                        waits.append((ds.handle, 16 * c))
                    attach = op.dsem is None and len(waits) > 0
                    for sem, val in (waits[:-1] if attach else waits):
                        h.wait_ge(sem, val)
                    ins = op.fn(h)
                    if attach:
                        sem, val = waits[-1]
                        ins._wait_ge(sem, val)
                    if op.dsem is not None:
                        ins.then_inc(op.dsem.handle, 16)
                    elif op.signal:
                        ep = (op.sig - 1) // EPOCH
                        ins.then_inc(esems[op.eng][ep], 1)

            @block.tensor
            def _(h):
                run_engine("pe", h)

            @block.scalar
            def _(h):
                run_engine("act", h)

            @block.vector
            def _(h):
                run_engine("dve", h)

            @block.gpsimd
            def _(h):
                run_engine("pool", h)

            @block.sync
            def _(h):
                run_engine("sp", h)
                for d in self.dsems:
                    if d.count:
                        h.wait_ge(d.handle, 16 * d.count)


class Rot:
    def __init__(self, items):
        self.items = items
        self.i = 0

    def next(self):
        it = self.items[self.i % len(self.items)]
        self.i += 1
        return it


def build_program(cfg):
    D, T, NH, NKV, BPC = cfg["D"], cfg["T"], cfg["NH"], cfg["NKV"], cfg["BPC"]
    DC = D // 128
    KVD = NKV * HD
    G = NH // NKV
    L = T + NMETA
    NT = T // 128
    NQC = T // 512
    NHP = NH // 2
    IN_DIM = 7 * D + 2 * KVD
    OFF = dict(val=0, glu=D, z=2 * D, q=3 * D, k=4 * D, v=4 * D + KVD, az=4 * D + 2 * KVD,
               gc=5 * D + 2 * KVD, ga=6 * D + 2 * KVD)
    UW = T + NMETA + 2 * PAD
    NKC = NKV // 2
    VW = NKC * 3 * HD
    WG = min(512, D)
    NWG = D // WG
    CPG = WG // 128

    nc = bass.Bass("TRN2", target_bir_lowering=False)
    P = Prog()

    def dram(name, shape, dt=F32, kind="ExternalInput"):
        return nc.dram_tensor(name, list(shape), dt, kind=kind).ap()

    x_d = dram("x", [BPC, T, D])
    meta_d = dram("meta", [NMETA, D])
    win_d = dram("w_in", [D, IN_DIM])
    wco_d = dram("w_co", [D, D])
    wao_d = dram("w_ao", [D, D])
    wout_d = dram("w_out", [D, D])
    gN_d = dram("gN", [128, D])
    convw_d = dram("convw", [128, DC * CONV_K])
    pvec_d = dram("pvec", [128, 3 * DC])
    qkg_d = dram("qkg", [128, 2])
    cos_d = dram("cosT", [128, L])
    sin_d = dram("sinT", [128, L])
    cmat_d = dram("cmat", [128, 4 * 128], BF16)
    out_d = dram("out", [BPC, T, D], F32, kind="ExternalOutput")

    arena_elems = nc.sbuf_bytes_remaining // 4 - 64
    arena = nc.alloc_sbuf_tensor("arena", [128, arena_elems], F32)
    ATOM = 512
    cur = [0]

    def alloc_bytes(nbytes):
        nb = (nbytes + 63) // 64 * 64
        o = cur[0]
        cur[0] += nb
        assert cur[0] <= arena_elems * 4, f"SBUF overflow {cur[0]} > {arena_elems * 4}"
        return o

    def view(off, shape, dt):
        esz = 2 if dt == BF16 else 4
        n = int(np.prod(shape))
        assert off % 4 == 0
        a = arena[:, off // 4: off // 4 + (n * esz + 3) // 4]
        if dt == BF16:
            a = a.bitcast(BF16)[:, 0:n]
        if len(shape) == 2:
            a = a.rearrange("p (a b) -> p a b", b=shape[1])
        elif len(shape) == 3:
            a = a.rearrange("p (a b c) -> p a b c", b=shape[1], c=shape[2])
        return a

    def alloc(shape, dt):
        esz = 2 if dt == BF16 else 4
        off = alloc_bytes(int(np.prod(shape)) * esz)
        return view(off, shape, dt)

    cmat = alloc([4 * 128], BF16)
    ident_bf, rotm, bones, onesm = (cmat[:, i * 128:(i + 1) * 128] for i in range(4))
    identf = alloc([128], F32)
    cexp = alloc([2], F32)
    gN = alloc([D], F32)
    convw = alloc([DC * CONV_K], F32)
    pvec = alloc([3 * DC], F32)
    qkg = alloc([2], F32)
    cosT = alloc([L], F32)
    sinT = alloc([L], F32)
    CONST = P.res()
    IDF = P.res()
    CEXP = P.res()
    xnT = alloc([DC, L], BF16)
    big1 = alloc([DC, T], BF16)
    big2 = alloc([DC, T], BF16)
    wbufs = [alloc([DC, WG], BF16) for _ in range(2)]
    WB = [P.res() for _ in range(2)]
    WSEM = [P.dsem(f"w{i}") for i in range(2)]
    wrot = [0]
    wbase = [(wbufs[i], WB[i], WSEM[i]) for i in range(2)]
    wpool = [list(wbase)]
    wx_count = [0]

    def extra_wbufs(R, n):
        out = []
        for _ in range(n):
            a, r = R.alloc([DC, WG], BF16)
            wx_count[0] += 1
            out.append((a, r, P.dsem(f"wx{wx_count[0]}")))
        return out
    chunks = [(c * 512, 512) for c in range(NQC)] + [(T, NMETA)]
    XN = [P.res() for _ in chunks]
    B1 = {(j, c): P.res() for j in range(DC) for c in range(NQC)}
    B2 = {(j, c): P.res() for j in range(DC) for c in range(NQC)}

    cur[0] = (cur[0] + ATOM - 1) // ATOM * ATOM
    scratch_base = alloc_bytes(0)
    scratch_size = arena_elems * 4 - scratch_base
    scratch_atoms = P.atoms((scratch_size + ATOM - 1) // ATOM)

    class Region:
        def __init__(self):
            self.off = 0

        def alloc(self, shape, dt, psum=False):
            esz = 2 if dt == BF16 else 4
            nb = (int(np.prod(shape)) * esz + ATOM - 1) // ATOM * ATOM
            o = self.off
            self.off += nb
            assert self.off <= scratch_size, f"scratch overflow {self.off} > {scratch_size}"
            a0, a1 = o // ATOM, (o + nb - 1) // ATOM
            return view(scratch_base + o, shape, dt), Res(scratch_atoms[a0:a1 + 1])

        def rot(self, n, shape, dt):
            return Rot([self.alloc(shape, dt) for _ in range(n)])

    psum_all = nc.alloc_psum_tensor("psum_all", [128, 8 * 512], F32)
    banks = [(psum_all[:, i * 512:(i + 1) * 512], P.res(psum=True)) for i in range(8)]

    def MM(out, lhs, rhs, start, stop, reads, writes):
        P.add("pe", lambda h: h.matmul(out, lhs, rhs, start=start, stop=stop), reads=reads, writes=writes)

    def TR(out, in_, ident, reads, writes):
        P.add("pe", lambda h: h.transpose(out, in_, ident), reads=reads, writes=writes)

    def ACT(out, in_, func, reads, writes, **kw):
        P.add("act", lambda h: h.activation(out=out, in_=in_, func=func, **kw), reads=reads, writes=writes)

    def TT(out, in0, in1, op, reads, writes, eng="dve"):
        P.add(eng, lambda h: h.tensor_tensor(out=out, in0=in0, in1=in1, op=op), reads=reads, writes=writes)

    def TS(out, in0, s1, s2, op0, op1, reads, writes, eng="dve"):
        if op1 is None:
            P.add(eng, lambda h: h.tensor_scalar(out=out, in0=in0, scalar1=s1, scalar2=None, op0=op0),
                  reads=reads, writes=writes)
        else:
            P.add(eng, lambda h: h.tensor_scalar(out=out, in0=in0, scalar1=s1, scalar2=s2, op0=op0, op1=op1),
                  reads=reads, writes=writes)

    def STT(out, in0, scalar, in1, op0, op1, reads, writes, eng="dve"):
        P.add(eng, lambda h: h.scalar_tensor_tensor(out=out, in0=in0, scalar=scalar, in1=in1, op0=op0, op1=op1),
              reads=reads, writes=writes)

    def POW(out, in_, col, reads, writes, p0=0, pn=128):
        n = int(np.prod(out.shape[1:]))
        shp = [pn] + list(out.shape[1:])
        e = cexp[p0:p0 + pn, col:col + 1]
        if len(shp) == 2:
            e = e.broadcast_to(shp)
        P.add("pool", lambda h: h.tensor_tensor(out=out, in0=in_, in1=e, op=ALU.pow),
              reads=list(reads) + [CEXP], writes=writes)

    def CP(out, in_, reads, writes, eng="dve"):
        P.add(eng, lambda h: h.tensor_copy(out=out, in_=in_), reads=reads, writes=writes)

    def MSET(ap, val, writes, eng="pool"):
        P.add(eng, lambda h: h.memset(ap, val), writes=writes)

    def DMA(eng, out, in_, reads, writes, dsem):
        P.add(eng, lambda h: h.dma_start(out=out, in_=in_), reads=reads, writes=writes, dsem=dsem)

    def wload(src2d, ncols):
        wb, wr, ws = wpool[0][wrot[0] % len(wpool[0])]
        wrot[0] += 1
        DMA("pool", wb[:, :, 0:ncols], src2d.rearrange("(j p) e -> p j e", p=128), [], [wr], ws)
        return wb, wr

    def mm_group(bank, brs, lhs_list, rhs_list, reads, prows=128, ncols=512):
        n = len(lhs_list)
        for k in range(n):
            MM(bank[0:prows, 0:ncols], lhs_list[k], rhs_list[k], k == 0, k == n - 1, reads, [brs])

    bsel = [0]

    nbset = [list(range(8))]

    def nb(avoid=()):
        while True:
            bsel[0] += 1
            bk = banks[nbset[0][bsel[0] % len(nbset[0])]]
            if all(bk[0] is not a for a in avoid):
                return bk

    csem = P.dsem("const")
    for dst, src in ((cmat, cmat_d), (gN, gN_d), (convw, convw_d), (pvec, pvec_d), (qkg, qkg_d),
                     (cosT, cos_d), (sinT, sin_d)):
        DMA("sp", dst, src, [], [CONST], csem)
    CP(identf, ident_bf, [CONST], [IDF])
    MSET(cexp[:, 0:1], -0.5, [CEXP])
    MSET(cexp[:, 1:2], -1.0, [CEXP])

    conv_b = lambda j: pvec[:, j:j + 1]
    cn_g = lambda j: pvec[:, DC + j:DC + j + 1]
    cn_b = lambda j: pvec[:, 2 * DC + j:2 * DC + j + 1]

    for b in range(BPC):
        wpool[0] = list(wbase)
        R = Region()
        NXB = 6
        xin = R.rot(NXB, [D], F32)
        xin_sems = [P.dsem(f"xin{b}_{i}") for i in range(NXB)]
        sqj, SQJ = R.alloc([D], BF16)
        msr = R.rot(4, [1], F32)
        rsr = R.rot(4, [1], F32)
        xnr = R.rot(3, [D], F32)
        tcount = [0]

        def a_stage1(i):
            rows = 128 if i < NT else NMETA
            src = x_d[b, i * 128:(i + 1) * 128, :] if i < NT else meta_d[:, :]
            kx = xin.i % NXB
            xb, xr = xin.next()
            DMA("sp", xb[0:rows], src, [], [xr], xin_sems[kx])
            ms, msres = msr.next()
            ACT(sqj[0:rows], xb[0:rows], AF.Square, [xr], [SQJ, msres], scale=float(D) ** -0.5,
                accum_out=ms[0:rows, 0:1])
            rs, rsres = rsr.next()
            ACT(rs[0:rows], ms[0:rows], AF.Ln, [msres], [rsres], bias=EPS)
            ACT(rs[0:rows], rs[0:rows], AF.Exp, [rsres], [rsres], scale=-0.5)
            xn, xnres = xnr.next()
            STT(xn[0:rows], xb[0:rows], rs[0:rows, 0:1], gN[0:rows], ALU.mult, ALU.mult,
                [xr, rsres, CONST], [xnres])
            return xn, xnres

        def a_stage2(i, xn, xnres):
            rows = 128 if i < NT else NMETA
            col0 = i * 128 if i < NT else T
            c = (i // 4) if i < NT else NQC
            for g0 in range(0, DC, 4):
                ng = min(4, DC - g0)
                bank, brs = nb()
                tcount[0] += 1
                for jj in range(ng):
                    j = g0 + jj
                    TR(bank[:, jj * 128: jj * 128 + rows], xn[0:rows, j * 128:(j + 1) * 128],
                       identf[0:rows, 0:rows], [xnres, IDF], [brs])
                srcv = bank[:, 0:ng * 128].rearrange("p (j t) -> p j t", t=128)[:, :, 0:rows]
                dstv = xnT[:, g0:g0 + ng, col0:col0 + rows]
                if tcount[0] % 2:
                    ACT(dstv, srcv, AF.Copy, [brs], [XN[c]])
                else:
                    CP(dstv, srcv, [brs], [XN[c]])

        prev = None
        for i in range(NT + 1):
            cur_ = a_stage1(i)
            if prev is not None:
                a_stage2(i - 1, *prev)
            prev = cur_
        a_stage2(NT, *prev)

        R = Region()
        uT = [R.alloc([UW], BF16) for _ in range(CPG)]
        diag = R.rot(2, [CONV_K, 128], BF16)
        sgr = R.rot(2, [512], F32)
        wpool[0] = list(wbase) + extra_wbufs(R, 1)
        for jj in range(CPG):
            ua, ur = uT[jj]
            MSET(ua[:, 0:PAD], 0.0, [ur])
            MSET(ua[:, UW - PAD:UW], 0.0, [ur])

        for s in range(NWG):
            wval, wvr = wload(win_d[:, OFF["val"] + s * WG: OFF["val"] + (s + 1) * WG], WG)
            wglu, wgr = wload(win_d[:, OFF["glu"] + s * WG: OFF["glu"] + (s + 1) * WG], WG)
            for jj in range(CPG):
                j = s * CPG + jj
                ua, ur = uT[jj]
                dg, dgr = diag.next()
                TT(dg, identf.unsqueeze(1).broadcast_to([128, CONV_K, 128]),
                   convw[:, j * CONV_K:(j + 1) * CONV_K].unsqueeze(2).broadcast_to([128, CONV_K, 128]),
                   ALU.mult, [IDF, CONST], [dgr], eng="pool")
                for c, (t0, ncol) in enumerate(chunks):
                    bv, bvr = nb()
                    bg, bgr = nb()
                    mm_group(bv, bvr, [wval[:, k, jj * 128:(jj + 1) * 128] for k in range(DC)],
                             [xnT[:, k, t0:t0 + ncol] for k in range(DC)], [wvr, XN[c]], ncols=ncol)
                    mm_group(bg, bgr, [wglu[:, k, jj * 128:(jj + 1) * 128] for k in range(DC)],
                             [xnT[:, k, t0:t0 + ncol] for k in range(DC)], [wgr, XN[c]], ncols=ncol)
                    sg, sgres = sgr.next()
                    ACT(sg[:, 0:ncol], bg[:, 0:ncol], AF.Sigmoid, [bgr], [sgres])
                    ucol = (PAD + NMETA + t0) if c < NQC else PAD
                    TT(ua[:, ucol:ucol + ncol], bv[:, 0:ncol], sg[:, 0:ncol], ALU.mult, [bvr, sgres], [ur])
                for c in range(NQC):
                    t0 = c * 512
                    bc, bcr = nb()
                    mm_group(bc, bcr, [dg[:, k, :] for k in range(CONV_K)],
                             [ua[:, t0 + NMETA + k: t0 + NMETA + k + 512] for k in range(CONV_K)], [dgr, ur])
                    ACT(big1[:, j, t0:t0 + 512], bc[:, :], AF.Identity, [bcr, CONST], [B1[j, c]], bias=conv_b(j))
                    ACT(big2[:, j, t0:t0 + 512], bc[:, :], AF.Square, [bcr, CONST], [B2[j, c]], bias=conv_b(j))

        (mean, mres), (msq, qres), (rstd, rres), (nmr, nres) = [R.alloc([512], F32) for _ in range(4)]
        cnr = R.rot(2, [512], F32)
        s1r = R.rot(2, [512], BF16)
        szr = R.rot(2, [512], BF16)
        assert NWG <= 2
        wz = [wload(win_d[:, OFF["z"] + s * WG: OFF["z"] + (s + 1) * WG], WG) for s in range(NWG)]
        for c in range(NQC):
            t0 = c * 512
            bs, bsr = nb()
            bq, bqr = nb()
            mm_group(bs, bsr, [onesm] * DC, [big1[:, j, t0:t0 + 512] for j in range(DC)],
                     [CONST] + [B1[j, c] for j in range(DC)])
            mm_group(bq, bqr, [onesm] * DC, [big2[:, j, t0:t0 + 512] for j in range(DC)],
                     [CONST] + [B2[j, c] for j in range(DC)])
            TS(mean, bs[:, :], 1.0 / D, None, ALU.mult, None, [bsr], [mres])
            TT(msq, mean, mean, ALU.mult, [mres], [qres])
            STT(rstd, bq[:, :], 1.0 / D, msq, ALU.mult, ALU.subtract, [bqr, qres], [rres])
            ACT(rstd, rstd, AF.Ln, [rres], [rres], bias=EPS)
            ACT(rstd, rstd, AF.Exp, [rres], [rres], scale=-0.5)
            STT(nmr, mean, -1.0, rstd, ALU.mult, ALU.mult, [mres, rres], [nres])
            for j in range(DC):
                wzv, wzr = wz[j // CPG]
                jj = j % CPG
                bz, bzr = nb()
                mm_group(bz, bzr, [wzv[:, k, jj * 128:(jj + 1) * 128] for k in range(DC)],
                         [xnT[:, k, t0:t0 + 512] for k in range(DC)], [wzr, XN[c]])
                sz, szres = szr.next()
                ACT(sz, bz[:, :], AF.Silu, [bzr], [szres])
                cn, cnres = cnr.next()
                TT(cn, big1[:, j, t0:t0 + 512], rstd, ALU.mult, [B1[j, c], rres], [cnres])
                TT(cn, cn, nmr, ALU.add, [cnres, nres], [cnres])
                s1, s1res = s1r.next()
                ACT(s1, cn, AF.Silu, [cnres, CONST], [s1res], scale=cn_g(j), bias=cn_b(j))
                TT(big1[:, j, t0:t0 + 512], s1, sz, ALU.mult, [s1res, szres], [B1[j, c]])

        sgcr = R.rot(2, [512], F32)
        for s in range(NWG):
            wc, wcr = wload(wco_d[:, s * WG:(s + 1) * WG], WG)
            wg_, wgr_ = wload(win_d[:, OFF["gc"] + s * WG: OFF["gc"] + (s + 1) * WG], WG)
            for mi in range(CPG):
                m = s * CPG + mi
                for c in range(NQC):
                    t0 = c * 512
                    by, byr = nb()
                    bg, bgr = nb()
                    mm_group(by, byr, [wc[:, k, mi * 128:(mi + 1) * 128] for k in range(DC)],
                             [big1[:, k, t0:t0 + 512] for k in range(DC)], [wcr] + [B1[k, c] for k in range(DC)])
                    mm_group(bg, bgr, [wg_[:, k, mi * 128:(mi + 1) * 128] for k in range(DC)],
                             [xnT[:, k, t0:t0 + 512] for k in range(DC)], [wgr_, XN[c]])
                    sg, sgres = sgcr.next()
                    ACT(sg, bg[:, :], AF.Sigmoid, [bgr], [sgres])
                    TT(big2[:, m, t0:t0 + 512], by[:, :], sg, ALU.mult, [byr, sgres], [B2[m, c]])

        wpool[0] = list(wbase)
        R = Region()
        KW = T + 128
        KT = [(R.alloc([KW], BF16), R.alloc([KW], BF16)) for _ in range(NKC)]
        VA, VAr = R.alloc([NT + 1, VW], BF16)
        QTr = R.rot(2, [T], BF16)
        SAZr = R.rot(2, [T], BF16)
        GS = 2 if NQC % 2 == 0 else 1
        PTr = R.rot(3, [GS * 512], BF16)
        sqr_ = R.rot(2, [512], BF16)
        sbr_ = R.rot(2, [512], BF16)
        rsr_ = R.rot(2, [512], F32)
        t1r_ = R.rot(2, [512], F32)
        t2r_ = R.rot(1, [512], F32)
        recr = R.rot(2, [512], F32)

        def prep_qk(wr, lhs_list, gcol, t0, ncol, c, dsts):
            bp, bpr = nb()
            mm_group(bp, bpr, lhs_list, [xnT[:, k, t0:t0 + ncol] for k in range(DC)], [wr, XN[c]], ncols=ncol)
            sq, sqres = sqr_.next()
            sb, sbres = sbr_.next()
            ACT(sq[:, 0:ncol], bp[:, 0:ncol], AF.Square, [bpr], [sqres], scale=0.125)
            ACT(sb[:, 0:ncol], bp[:, 0:ncol], AF.Identity, [bpr, CONST], [sbres], scale=qkg[:, gcol:gcol + 1])
            bss, bssr = nb()
            brt, brtr = nb()
            MM(bss[:, 0:ncol], bones, sq[:, 0:ncol], True, True, [CONST, sqres], [bssr])
            MM(brt[:, 0:ncol], rotm, sb[:, 0:ncol], True, True, [CONST, sbres], [brtr])
            rs, rsres = rsr_.next()
            t1, t1res = t1r_.next()
            t2, t2res = t2r_.next()
            ACT(rs[:, 0:ncol], bss[:, 0:ncol], AF.Ln, [bssr], [rsres], bias=EPS)
            ACT(rs[:, 0:ncol], rs[:, 0:ncol], AF.Exp, [rsres], [rsres], scale=-0.5)
            TT(t1[:, 0:ncol], sb[:, 0:ncol], cosT[:, t0:t0 + ncol], ALU.mult, [sbres, CONST], [t1res])
            TT(t2[:, 0:ncol], brt[:, 0:ncol], sinT[:, t0:t0 + ncol], ALU.mult, [brtr, CONST], [t2res])
            TT(t1[:, 0:ncol], t1[:, 0:ncol], t2[:, 0:ncol], ALU.add, [t1res, t2res], [t1res])
            for dst, dres, p0, p1 in dsts:
                TT(dst[p0:p1, t0:t0 + ncol], t1[p0:p1, 0:ncol], rs[p0:p1, 0:ncol], ALU.mult, [t1res, rsres], [dres])

        def prep_q_stages(wr, lhs_list, gcol, t0, ncol, c, dsts, bk_p, bk_s, bk_r):
            (bp, bpr), (bss, bssr), (brt, brtr) = bk_p, bk_s, bk_r
            st = {}

            def s0():
                mm_group(bp, bpr, lhs_list, [xnT[:, k, t0:t0 + ncol] for k in range(DC)], [wr, XN[c]], ncols=ncol)

            def s1():
                st["sq"] = sqr_.next()
                st["sb"] = sbr_.next()
                sq, sqres = st["sq"]
                sb, sbres = st["sb"]
                ACT(sq[:, 0:ncol], bp[:, 0:ncol], AF.Square, [bpr], [sqres], scale=0.125)
                ACT(sb[:, 0:ncol], bp[:, 0:ncol], AF.Identity, [bpr, CONST], [sbres], scale=qkg[:, gcol:gcol + 1])

            def s2():
                sq, sqres = st["sq"]
                sb, sbres = st["sb"]
                MM(bss[:, 0:ncol], bones, sq[:, 0:ncol], True, True, [CONST, sqres], [bssr])
                MM(brt[:, 0:ncol], rotm, sb[:, 0:ncol], True, True, [CONST, sbres], [brtr])

            def s3():
                st["rs"] = rsr_.next()
                rs, rsres = st["rs"]
                ACT(rs[:, 0:ncol], bss[:, 0:ncol], AF.Ln, [bssr], [rsres], bias=EPS)
                ACT(rs[:, 0:ncol], rs[:, 0:ncol], AF.Exp, [rsres], [rsres], scale=-0.5)

            def s4():
                sb, sbres = st["sb"]
                rs, rsres = st["rs"]
                t1, t1res = t1r_.next()
                t2, t2res = t2r_.next()
                TT(t1[:, 0:ncol], sb[:, 0:ncol], cosT[:, t0:t0 + ncol], ALU.mult, [sbres, CONST], [t1res])
                TT(t2[:, 0:ncol], brt[:, 0:ncol], sinT[:, t0:t0 + ncol], ALU.mult, [brtr, CONST], [t2res])
                TT(t1[:, 0:ncol], t1[:, 0:ncol], t2[:, 0:ncol], ALU.add, [t1res, t2res], [t1res])
                for dst, dres, p0, p1 in dsts:
                    TT(dst[p0:p1, t0:t0 + ncol], t1[p0:p1, 0:ncol], rs[p0:p1, 0:ncol], ALU.mult,
                       [t1res, rsres], [dres])

            return [s0, s1, s2, s3, s4]

        wkv, wkvr = wload(win_d[:, OFF["k"]: OFF["k"] + 2 * KVD], 2 * KVD)
        for kc in range(NKC):
            (kta, ktares), (ktb, ktbres) = KT[kc]
            MSET(kta, 0.0, [ktares])
            MSET(ktb, 0.0, [ktbres])
            for c0 in range(0, len(chunks), 2):
                stl = []
                for ci, c in enumerate([c for c in (c0, c0 + 1) if c < len(chunks)]):
                    t0, ncol = chunks[c]
                    stl.append(prep_q_stages(wkvr, [wkv[:, k, kc * 128:(kc + 1) * 128] for k in range(DC)], 1,
                                             t0, ncol, c, [(kta, ktares, 0, HD), (ktb, ktbres, HD, 128)],
                                             banks[ci], banks[2 + 2 * ci], banks[3 + 2 * ci]))
                for si in range(5):
                    for stg in stl:
                        stg[si]()
        MSET(VA[:, 0:NT, :], 1.0, [VAr])
        MSET(VA[:, NT, :], 0.0, [VAr])
        MSET(VA[0:NMETA, NT, :].rearrange("p (c b d) -> p c b d", b=3, d=HD)[:, :, 1, :], 1.0, [VAr])
        for i in range(NT + 1):
            rows = 128 if i < NT else NMETA
            col0 = i * 128 if i < NT else T
            c = (i // 4) if i < NT else NQC
            bv, bvr = nb()
            mm_group(bv, bvr, [xnT[:, k, col0:col0 + rows] for k in range(DC)],
                     [wkv[:, k, KVD:2 * KVD] for k in range(DC)], [wkvr, XN[c]], prows=rows, ncols=KVD)
            CP(VA[0:rows, i, :].rearrange("p (c b d) -> p c b d", b=3, d=HD)[:, :, ::2, :],
               bv[0:rows, 0:KVD].rearrange("p (c b d) -> p c b d", b=2, d=HD), [bvr], [VAr])

        key_tiles = [(i * 128, 128) for i in range(NT)] + [(T, 128)]
        nbset[0] = [0, 1, 2, 3]
        unit = [0]
        step = [0]
        nk = len(key_tiles)
        for hp in range(NHP):
            wq, wqr = wload(win_d[:, OFF["q"] + hp * 128: OFF["q"] + (hp + 1) * 128], 128)
            waz, wazr = wload(win_d[:, OFF["az"] + hp * 128: OFF["az"] + (hp + 1) * 128], 128)
            qt, qtres = QTr.next()
            saz, sazres = SAZr.next()
            nbset[0] = list(range(8))
            bas = []
            for c in range(NQC):
                t0 = c * 512
                ba, bar = banks[6 + (c % 2)]
                mm_group(ba, bar, [waz[:, k, 0:128] for k in range(DC)],
                         [xnT[:, k, t0:t0 + 512] for k in range(DC)], [wazr, XN[c]])
                ACT(saz[:, t0:t0 + 512], ba, AF.Silu, [bar], [sazres])
            for c0 in range(0, NQC, 2):
                cs = [c for c in (c0, c0 + 1) if c < NQC]
                stl = []
                for ci, c in enumerate(cs):
                    stl.append(prep_q_stages(wqr, [wq[:, k, 0:128] for k in range(DC)], 0, c * 512, 512, c,
                                             [(qt, qtres, 0, 128)],
                                             banks[ci], banks[2 + 2 * ci], banks[3 + 2 * ci]))
                for si in range(5):
                    for stg in stl:
                        stg[si]()
            nbset[0] = [0, 1, 2, 3]
            kc = hp // G

            def make_epi(obanks, r0, d0, cg):
                def epi():
                    for e in range(GS):
                        c = cg * GS + e
                        t0 = c * 512
                        bo, bor = obanks[e]
                        rec, recres = recr.next()
                        P.add("dve", lambda h, o=rec[r0:r0 + HD], i_=bo[d0:d0 + HD, :]: h.reciprocal(out=o, in_=i_),
                              reads=[bor], writes=[recres])
                        TT(rec[r0:r0 + HD], rec[r0:r0 + HD], saz[r0:r0 + HD, t0:t0 + 512], ALU.mult,
                           [recres, sazres], [recres])
                        TT(big1[r0:r0 + HD, hp, t0:t0 + 512], bo[r0:r0 + HD, :], rec[r0:r0 + HD], ALU.mult,
                           [bor, recres], [B1[hp, c]])
                return epi

            pend = None
            for hh in range(2):
                kt, ktres = KT[kc][hh]
                r0 = hh * HD
                d0 = (1 - hh) * HD
                v0 = kc * 3 * HD + hh * HD
                for cg in range(NQC // GS):
                    unit[0] += 1
                    obanks = [banks[4 + 2 * (unit[0] % 2) + e] if GS == 2 else banks[4 + (unit[0] % 4)]
                              for e in range(GS)]
                    for i, (k0, krows) in enumerate(key_tiles):
                        step[0] += 1
                        sb0 = (2 * (step[0] % 2)) if GS == 2 else (step[0] % 4)
                        sbanks = [banks[sb0 + e] for e in range(GS)]
                        for e in range(GS):
                            t0 = (cg * GS + e) * 512
                            MM(sbanks[e][0], kt[:, k0:k0 + krows], qt[:, t0:t0 + 512],
                               True, True, [ktres, qtres], [sbanks[e][1]])
                        pt, ptres = PTr.next()
                        ACT(pt, psum_all[:, sb0 * 512:(sb0 + GS) * 512], AF.Exp, [sbk[1] for sbk in sbanks], [ptres],
                            scale=0.125)
                        if pend is not None:
                            for a in pend[0]:
                                MM(*a)
                            if pend[1] is not None:
                                pend[1]()
                        pend = ([(obanks[e][0], VA[0:krows, i, v0:v0 + 128], pt[:, e * 512:(e + 1) * 512],
                                  i == 0, i == nk - 1, [VAr, ptres], [obanks[e][1]]) for e in range(GS)],
                                make_epi(obanks, r0, d0, cg) if i == nk - 1 else None)
            for a in pend[0]:
                MM(*a)
            pend[1]()
        nbset[0] = list(range(8))

        RT = Region()
        sgar = RT.rot(2, [512], F32)
        wpool[0] = list(wbase) + extra_wbufs(RT, 2)
        for s in range(NWG):
            wc, wcr = wload(wao_d[:, s * WG:(s + 1) * WG], WG)
            wg_, wgr_ = wload(win_d[:, OFF["ga"] + s * WG: OFF["ga"] + (s + 1) * WG], WG)
            for mi in range(CPG):
                m = s * CPG + mi
                for c in range(NQC):
                    t0 = c * 512
                    by, byr = nb()
                    bg, bgr = nb()
                    mm_group(by, byr, [wc[:, k, mi * 128:(mi + 1) * 128] for k in range(DC)],
                             [big1[:, k, t0:t0 + 512] for k in range(DC)], [wcr] + [B1[k, c] for k in range(DC)])
                    mm_group(bg, bgr, [wg_[:, k, mi * 128:(mi + 1) * 128] for k in range(DC)],
                             [xnT[:, k, t0:t0 + 512] for k in range(DC)], [wgr_, XN[c]])
                    sg, sgres = sgar.next()
                    ACT(sg, bg[:, :], AF.Sigmoid, [bgr], [sgres])
                    TT(sg, by[:, :], sg, ALU.mult, [byr, sgres], [sgres])
                    TT(big2[:, m, t0:t0 + 512], sg, big2[:, m, t0:t0 + 512], ALU.add,
                       [sgres, B2[m, c]], [B2[m, c]])

        R = RT
        ND = 4
        xres = R.rot(ND, [D], F32)
        xres_sems = [P.dsem(f"xres{b}_{i}") for i in range(ND)]
        ost = R.rot(ND, [D], F32)
        ost_sems = [P.dsem(f"ost{b}_{i}") for i in range(ND)]
        wo = [wload(wout_d[:, s * WG:(s + 1) * WG], WG) for s in range(NWG)]
        for i in range(NT):
            c = i // 4
            kx = xres.i % ND
            xb, xr = xres.next()
            DMA("sp", xb, x_d[b, i * 128:(i + 1) * 128, :], [], [xr], xres_sems[kx])
            ko = ost.i % ND
            ob, obr = ost.next()
            for s in range(NWG):
                wv, wr = wo[s]
                bo, bor = nb()
                mm_group(bo, bor, [big2[:, k, i * 128:(i + 1) * 128] for k in range(DC)],
                         [wv[:, k, 0:WG] for k in range(DC)], [wr] + [B2[k, c] for k in range(DC)], ncols=WG)
                TT(ob[:, s * WG:(s + 1) * WG], bo[:, 0:WG], xb[:, s * WG:(s + 1) * WG], ALU.add,
                   [bor, xr], [obr])
            DMA("sp", out_d[b, i * 128:(i + 1) * 128, :], ob, [obr], [], ost_sems[ko])

    P.emit(nc)
    return nc


def rope_tables_T(cfg):
    T, GW = cfg["T"], cfg["GW"]
    nf = HD // 4
    rows = T // GW
    row_ids = np.repeat(np.arange(rows, dtype=np.float32), GW)
    col_ids = np.tile(np.arange(GW, dtype=np.float32), rows)
    row_ids = np.concatenate([row_ids, np.zeros(NMETA, np.float32)])
    col_ids = np.concatenate([col_ids, np.zeros(NMETA, np.float32)])
    inv_freq = (np.float32(ROPE_THETA) ** (-np.arange(nf, dtype=np.float32) / np.float32(nf))).astype(np.float32)
    a_row = row_ids[:, None] * inv_freq[None, :]
    a_col = col_ids[:, None] * inv_freq[None, :]
    ang = np.concatenate([a_row, a_row, a_col, a_col], axis=-1).astype(np.float32)
    cosT = np.cos(ang).T.astype(np.float32)
    sinT = np.sin(ang).T.astype(np.float32)
    return (np.ascontiguousarray(np.concatenate([cosT, cosT], 0)),
            np.ascontiguousarray(np.concatenate([sinT, sinT], 0)))


def const_mats():
    ident = np.eye(128, dtype=np.float32)
    rot = np.zeros((128, 128), np.float32)
    for m in range(128):
        if (m % 32) < 16:
            rot[m + 16, m] = -1.0
        else:
            rot[m - 16, m] = 1.0
    bones = np.zeros((128, 128), np.float32)
    bones[0:64, 0:64] = 1.0
    bones[64:128, 64:128] = 1.0
    ones = np.ones((128, 128), np.float32)
    return np.ascontiguousarray(np.concatenate([ident, rot, bones, ones], 1).astype(ml_dtypes.bfloat16))


def host_inputs(cfg, x, meta_tokens, norm_g, w_in, conv_w, conv_b, conv_norm_g, conv_norm_b,
                w_conv_out, q_norm_g, k_norm_g, w_attn_out, w_out, n_cores):
    D, BPC = cfg["D"], cfg["BPC"]
    DC = D // 128
    f = lambda a: np.ascontiguousarray(np.asarray(a, dtype=np.float32))
    cosT, sinT = rope_tables_T(cfg)
    convw = f(conv_w[0]).T.reshape(DC, 128, CONV_K).transpose(1, 0, 2).reshape(128, DC * CONV_K)
    pv = lambda v: f(v[0]).reshape(DC, 128).T
    pvec = np.concatenate([pv(conv_b), pv(conv_norm_g), pv(conv_norm_b)], axis=1)
    qkg = np.stack([np.tile(f(q_norm_g[0]), 2), np.tile(f(k_norm_g[0]), 2)], axis=1)
    NH, NKV = cfg["NH"], cfg["NKV"]
    G = NH // NKV
    KVD = NKV * HD
    perm = []
    for kc in range(NKV // 2):
        for r in range(G):
            perm += [2 * kc * G + r, (2 * kc + 1) * G + r]
    perm = np.array(perm)
    w_in_p = f(w_in[0]).copy()
    for off in (3 * D, 4 * D + 2 * KVD):
        blk = w_in_p[:, off:off + D].reshape(D, NH, HD)[:, perm, :].reshape(D, D)
        w_in_p[:, off:off + D] = blk
    w_ao_p = f(w_attn_out[0]).reshape(NH, HD, D)[perm].reshape(D, D)
    shared = {
        "meta": f(meta_tokens), "w_in": f(w_in_p), "w_co": f(w_conv_out[0]), "w_ao": f(w_ao_p),
        "w_out": f(w_out[0]), "gN": f(np.broadcast_to(f(norm_g[0])[None, :], (128, D))),
        "convw": f(convw), "pvec": f(pvec), "qkg": f(qkg), "cosT": cosT, "sinT": sinT, "cmat": const_mats(),
    }
    x = f(x)
    return [dict(shared, x=np.ascontiguousarray(x[i * BPC:(i + 1) * BPC])) for i in range(n_cores)]


_NC_CACHE = {}


def kernel(x, meta_tokens, norm_g, w_in, conv_w, conv_b, conv_norm_g, conv_norm_b,
           w_conv_out, q_norm_g, k_norm_g, w_attn_out, w_out):
    cfg = full_cfg()
    n_cores = 8
    in_maps = host_inputs(cfg, x, meta_tokens, norm_g, w_in, conv_w, conv_b, conv_norm_g, conv_norm_b,
                          w_conv_out, q_norm_g, k_norm_g, w_attn_out, w_out, n_cores)
    nc = build_program(cfg)
    res = run_bass_kernel_spmd(nc, in_maps, core_ids=list(range(n_cores)))
    return np.concatenate([np.asarray(r["out"], dtype=np.float32) for r in res.results], axis=0)
```

```python
import math
import numpy as np
import ml_dtypes
import concourse.bass as bass
import concourse.mybir as mybir
from concourse.bass_utils import run_bass_kernel_spmd

F32 = mybir.dt.float32
BF16 = mybir.dt.bfloat16
AF = mybir.ActivationFunctionType
ALU = mybir.AluOpType

NMETA = 16
CONV_K = 31
PAD = CONV_K // 2
HD = 64
EPS = 1e-6
ROPE_THETA = 10000.0
EPOCH = 12000


def full_cfg():
    return dict(D=1024, T=2048, NH=16, NKV=4, BPC=2, GW=64)


class Res:
    __slots__ = ("atoms", "psum")

    def __init__(self, atoms, psum=False):
        self.atoms = tuple(atoms)
        self.psum = psum


class Op:
    __slots__ = ("idx", "eng", "fn", "deps", "dsem", "dcount", "signal", "sig", "eidx")

    def __init__(self, idx, eng, fn, dsem):
        self.idx = idx
        self.eng = eng
        self.fn = fn
        self.deps = {}
        self.dsem = dsem
        self.dcount = 0
        self.signal = False
        self.sig = 0
        self.eidx = 0


class DSem:
    def __init__(self, name):
        self.name = name
        self.count = 0
        self.handle = None


class Prog:
    ENGS = ("pe", "act", "dve", "pool", "sp")

    def __init__(self):
        self.ops = []
        self.state = {}
        self.natoms = 0
        self.dsems = []

    def atoms(self, n=1):
        a = list(range(self.natoms, self.natoms + n))
        self.natoms += n
        return a

    def res(self, psum=False):
        return Res(self.atoms(1), psum)

    def dsem(self, name):
        d = DSem(name)
        self.dsems.append(d)
        return d

    def add(self, eng, fn, reads=(), writes=(), dsem=None):
        op = Op(len(self.ops), eng, fn, dsem)
        if dsem is not None:
            dsem.count += 1
            op.dcount = dsem.count
        deps = op.deps
        st = self.state

        def dep(o, kind):
            if o is op:
                return
            k = deps.get(o)
            if k is None or kind == "raw":
                deps[o] = kind

        for r in reads:
            for a in r.atoms:
                s = st.get(a)
                if s is None:
                    s = st[a] = [None, {}, []]
                if s[0] is not None:
                    dep(s[0], "raw")
                if r.psum:
                    for e, o in s[1].items():
                        if e != eng:
                            dep(o, "excl")
                if dsem is not None:
                    s[2].append(op)
                else:
                    s[1][eng] = op
        for w in writes:
            for a in w.atoms:
                s = st.get(a)
                if s is None:
                    s = st[a] = [None, {}, []]
                if s[0] is not None:
                    if not (dsem is not None and s[0].dsem is dsem and not s[1] and not s[2]):
                        dep(s[0], "waw")
                for e, o in s[1].items():
                    dep(o, "war")
                for o in s[2]:
                    dep(o, "war")
                s[0] = op
                s[1] = {}
                s[2] = []
        self.ops.append(op)
        return op

    def emit(self, nc):
        ops = self.ops
        for op in ops:
            keep = {}
            for d, kind in op.deps.items():
                if d.dsem is None and d.eng == op.eng and op.dsem is None and op.eng == "pe":
                    continue
                keep[d] = kind
            op.deps = keep
            for d in keep:
                if d.dsem is None:
                    d.signal = True
        cnt = {e: 0 for e in self.ENGS}
        for op in ops:
            if op.dsem is None and op.signal:
                cnt[op.eng] += 1
                op.sig = cnt[op.eng]
        import contextlib

        stack = contextlib.ExitStack()
        with stack:
            esems = {}
            for e in self.ENGS:
                n = (cnt[e] + EPOCH - 1) // EPOCH
                esems[e] = [stack.enter_context(nc.semaphore(f"c_{e}_{i}")) for i in range(max(n, 1))]
            for d in self.dsems:
                d.handle = stack.enter_context(nc.semaphore(f"d_{d.name}"))
            block = stack.enter_context(nc.Block())
            per_eng = {e: [o for o in ops if o.eng == e] for e in self.ENGS}

            def run_engine(ename, h):
                known = {e: 0 for e in self.ENGS}
                dknown = {}
                for op in per_eng[ename]:
                    waits = []
                    need = {}
                    dneed = {}
                    for d in op.deps:
                        if d.dsem is not None:
                            if dknown.get(d.dsem, 0) < d.dcount and dneed.get(d.dsem, 0) < d.dcount:
                                dneed[d.dsem] = d.dcount
                        else:
                            if known[d.eng] < d.sig and need.get(d.eng, 0) < d.sig:
                                need[d.eng] = d.sig
                    for e, s in need.items():
                        known[e] = s
                        ep, loc = (s - 1) // EPOCH, (s - 1) % EPOCH + 1
                        waits.append((esems[e][ep], loc))
                    for ds, c in dneed.items():
                        dknown[ds] = c
                        waits.append((ds.handle, 16 * c))
                    attach = op.dsem is None and len(waits) > 0
                    for sem, val in (waits[:-1] if attach else waits):
                        h.wait_ge(sem, val)
                    ins = op.fn(h)
                    if attach:
                        sem, val = waits[-1]
                        ins._wait_ge(sem, val)
                    if op.dsem is not None:
                        ins.then_inc(op.dsem.handle, 16)
                    elif op.signal:
                        ep = (op.sig - 1) // EPOCH
                        ins.then_inc(esems[op.eng][ep], 1)

            @block.tensor
            def _(h):
                run_engine("pe", h)

            @block.scalar
            def _(h):
                run_engine("act", h)

            @block.vector
            def _(h):
                run_engine("dve", h)

            @block.gpsimd
            def _(h):
                run_engine("pool", h)

            @block.sync
            def _(h):
                run_engine("sp", h)
                for d in self.dsems:
                    if d.count:
                        h.wait_ge(d.handle, 16 * d.count)


class Rot:
    def __init__(self, items):
        self.items = items
        self.i = 0

    def next(self):
        it = self.items[self.i % len(self.items)]
        self.i += 1
        return it


def build_program(cfg):
    D, T, NH, NKV, BPC = cfg["D"], cfg["T"], cfg["NH"], cfg["NKV"], cfg["BPC"]
    DC = D // 128
    KVD = NKV * HD
    G = NH // NKV
    L = T + NMETA
    NT = T // 128
    NQC = T // 512
    NHP = NH // 2
    IN_DIM = 7 * D + 2 * KVD
    OFF = dict(val=0, glu=D, z=2 * D, q=3 * D, k=4 * D, v=4 * D + KVD, az=4 * D + 2 * KVD,
               gc=5 * D + 2 * KVD, ga=6 * D + 2 * KVD)
    UW = T + NMETA + 2 * PAD
    NKC = NKV // 2
    VW = NKC * 3 * HD
    WG = min(512, D)
    NWG = D // WG
    CPG = WG // 128

    nc = bass.Bass("TRN2", target_bir_lowering=False)
    P = Prog()

    def dram(name, shape, dt=F32, kind="ExternalInput"):
        return nc.dram_tensor(name, list(shape), dt, kind=kind).ap()

    x_d = dram("x", [BPC, T, D])
    meta_d = dram("meta", [NMETA, D])
    win_d = dram("w_in", [D, IN_DIM])
    wco_d = dram("w_co", [D, D])
    wao_d = dram("w_ao", [D, D])
    wout_d = dram("w_out", [D, D])
    gN_d = dram("gN", [128, D])
    convw_d = dram("convw", [128, DC * CONV_K])
    pvec_d = dram("pvec", [128, 3 * DC])
    qkg_d = dram("qkg", [128, 2])
    cos_d = dram("cosT", [128, L])
    sin_d = dram("sinT", [128, L])
    cmat_d = dram("cmat", [128, 4 * 128], BF16)
    out_d = dram("out", [BPC, T, D], F32, kind="ExternalOutput")

    arena_elems = nc.sbuf_bytes_remaining // 4 - 64
    arena = nc.alloc_sbuf_tensor("arena", [128, arena_elems], F32)
    ATOM = 512
    cur = [0]

    def alloc_bytes(nbytes):
        nb = (nbytes + 63) // 64 * 64
        o = cur[0]
        cur[0] += nb
        assert cur[0] <= arena_elems * 4, f"SBUF overflow {cur[0]} > {arena_elems * 4}"
        return o

    def view(off, shape, dt):
        esz = 2 if dt == BF16 else 4
        n = int(np.prod(shape))
        assert off % 4 == 0
        a = arena[:, off // 4: off // 4 + (n * esz + 3) // 4]
        if dt == BF16:
            a = a.bitcast(BF16)[:, 0:n]
        if len(shape) == 2:
            a = a.rearrange("p (a b) -> p a b", b=shape[1])
        elif len(shape) == 3:
            a = a.rearrange("p (a b c) -> p a b c", b=shape[1], c=shape[2])
        return a

    def alloc(shape, dt):
        esz = 2 if dt == BF16 else 4
        off = alloc_bytes(int(np.prod(shape)) * esz)
        return view(off, shape, dt)

    cmat = alloc([4 * 128], BF16)
    ident_bf, rotm, bones, onesm = (cmat[:, i * 128:(i + 1) * 128] for i in range(4))
    identf = alloc([128], F32)
    cexp = alloc([2], F32)
    gN = alloc([D], F32)
    convw = alloc([DC * CONV_K], F32)
    pvec = alloc([3 * DC], F32)
    qkg = alloc([2], F32)
    cosT = alloc([L], F32)
    sinT = alloc([L], F32)
    CONST = P.res()
    IDF = P.res()
    CEXP = P.res()
    xnT = alloc([DC, L], BF16)
    big1 = alloc([DC, T], BF16)
    big2 = alloc([DC, T], BF16)
    wbufs = [alloc([DC, WG], BF16) for _ in range(2)]
    WB = [P.res() for _ in range(2)]
    WSEM = [P.dsem(f"w{i}") for i in range(2)]
    wrot = [0]
    wbase = [(wbufs[i], WB[i], WSEM[i]) for i in range(2)]
    wpool = [list(wbase)]
    wx_count = [0]

    def extra_wbufs(R, n):
        out = []
        for _ in range(n):
            a, r = R.alloc([DC, WG], BF16)
            wx_count[0] += 1
            out.append((a, r, P.dsem(f"wx{wx_count[0]}")))
        return out
    chunks = [(c * 512, 512) for c in range(NQC)] + [(T, NMETA)]
    XN = [P.res() for _ in chunks]
    B1 = {(j, c): P.res() for j in range(DC) for c in range(NQC)}
    B2 = {(j, c): P.res() for j in range(DC) for c in range(NQC)}

    cur[0] = (cur[0] + ATOM - 1) // ATOM * ATOM
    scratch_base = alloc_bytes(0)
    scratch_size = arena_elems * 4 - scratch_base
    scratch_atoms = P.atoms((scratch_size + ATOM - 1) // ATOM)

    class Region:
        def __init__(self):
            self.off = 0

        def alloc(self, shape, dt, psum=False):
            esz = 2 if dt == BF16 else 4
            nb = (int(np.prod(shape)) * esz + ATOM - 1) // ATOM * ATOM
            o = self.off
            self.off += nb
            assert self.off <= scratch_size, f"scratch overflow {self.off} > {scratch_size}"
            a0, a1 = o // ATOM, (o + nb - 1) // ATOM
            return view(scratch_base + o, shape, dt), Res(scratch_atoms[a0:a1 + 1])

        def rot(self, n, shape, dt):
            return Rot([self.alloc(shape, dt) for _ in range(n)])

    psum_all = nc.alloc_psum_tensor("psum_all", [128, 8 * 512], F32)
    banks = [(psum_all[:, i * 512:(i + 1) * 512], P.res(psum=True)) for i in range(8)]

    def MM(out, lhs, rhs, start, stop, reads, writes):
        P.add("pe", lambda h: h.matmul(out, lhs, rhs, start=start, stop=stop), reads=reads, writes=writes)

    def TR(out, in_, ident, reads, writes):
        P.add("pe", lambda h: h.transpose(out, in_, ident), reads=reads, writes=writes)

    def ACT(out, in_, func, reads, writes, **kw):
        P.add("act", lambda h: h.activation(out=out, in_=in_, func=func, **kw), reads=reads, writes=writes)

    def TT(out, in0, in1, op, reads, writes, eng="dve"):
        P.add(eng, lambda h: h.tensor_tensor(out=out, in0=in0, in1=in1, op=op), reads=reads, writes=writes)

    def TS(out, in0, s1, s2, op0, op1, reads, writes, eng="dve"):
        if op1 is None:
            P.add(eng, lambda h: h.tensor_scalar(out=out, in0=in0, scalar1=s1, scalar2=None, op0=op0),
                  reads=reads, writes=writes)
        else:
            P.add(eng, lambda h: h.tensor_scalar(out=out, in0=in0, scalar1=s1, scalar2=s2, op0=op0, op1=op1),
                  reads=reads, writes=writes)

    def STT(out, in0, scalar, in1, op0, op1, reads, writes, eng="dve"):
        P.add(eng, lambda h: h.scalar_tensor_tensor(out=out, in0=in0, scalar=scalar, in1=in1, op0=op0, op1=op1),
              reads=reads, writes=writes)

    def POW(out, in_, col, reads, writes, p0=0, pn=128):
        n = int(np.prod(out.shape[1:]))
        shp = [pn] + list(out.shape[1:])
        e = cexp[p0:p0 + pn, col:col + 1]
        if len(shp) == 2:
            e = e.broadcast_to(shp)
        P.add("pool", lambda h: h.tensor_tensor(out=out, in0=in_, in1=e, op=ALU.pow),
              reads=list(reads) + [CEXP], writes=writes)

    def CP(out, in_, reads, writes, eng="dve"):
        P.add(eng, lambda h: h.tensor_copy(out=out, in_=in_), reads=reads, writes=writes)

    def MSET(ap, val, writes, eng="pool"):
        P.add(eng, lambda h: h.memset(ap, val), writes=writes)

    def DMA(eng, out, in_, reads, writes, dsem):
        P.add(eng, lambda h: h.dma_start(out=out, in_=in_), reads=reads, writes=writes, dsem=dsem)

    def wload(src2d, ncols):
        wb, wr, ws = wpool[0][wrot[0] % len(wpool[0])]
        wrot[0] += 1
        DMA("pool", wb[:, :, 0:ncols], src2d.rearrange("(j p) e -> p j e", p=128), [], [wr], ws)
        return wb, wr

    def mm_group(bank, brs, lhs_list, rhs_list, reads, prows=128, ncols=512):
        n = len(lhs_list)
        for k in range(n):
            MM(bank[0:prows, 0:ncols], lhs_list[k], rhs_list[k], k == 0, k == n - 1, reads, [brs])

    bsel = [0]

    nbset = [list(range(8))]

    def nb(avoid=()):
        while True:
            bsel[0] += 1
            bk = banks[nbset[0][bsel[0] % len(nbset[0])]]
            if all(bk[0] is not a for a in avoid):
                return bk

    csem = P.dsem("const")
    for dst, src in ((cmat, cmat_d), (gN, gN_d), (convw, convw_d), (pvec, pvec_d), (qkg, qkg_d),
                     (cosT, cos_d), (sinT, sin_d)):
        DMA("sp", dst, src, [], [CONST], csem)
    CP(identf, ident_bf, [CONST], [IDF])
    MSET(cexp[:, 0:1], -0.5, [CEXP])
    MSET(cexp[:, 1:2], -1.0, [CEXP])

    conv_b = lambda j: pvec[:, j:j + 1]
    cn_g = lambda j: pvec[:, DC + j:DC + j + 1]
    cn_b = lambda j: pvec[:, 2 * DC + j:2 * DC + j + 1]

    for b in range(BPC):
        wpool[0] = list(wbase)
        R = Region()
        NXB = 6
        xin = R.rot(NXB, [D], F32)
        xin_sems = [P.dsem(f"xin{b}_{i}") for i in range(NXB)]
        sqj, SQJ = R.alloc([D], BF16)
        msr = R.rot(4, [1], F32)
        rsr = R.rot(4, [1], F32)
        xnr = R.rot(3, [D], F32)
        tcount = [0]

        def a_stage1(i):
            rows = 128 if i < NT else NMETA
            src = x_d[b, i * 128:(i + 1) * 128, :] if i < NT else meta_d[:, :]
            kx = xin.i % NXB
            xb, xr = xin.next()
            DMA("sp", xb[0:rows], src, [], [xr], xin_sems[kx])
            ms, msres = msr.next()
            ACT(sqj[0:rows], xb[0:rows], AF.Square, [xr], [SQJ, msres], scale=float(D) ** -0.5,
                accum_out=ms[0:rows, 0:1])
            rs, rsres = rsr.next()
            ACT(rs[0:rows], ms[0:rows], AF.Ln, [msres], [rsres], bias=EPS)
            ACT(rs[0:rows], rs[0:rows], AF.Exp, [rsres], [rsres], scale=-0.5)
            xn, xnres = xnr.next()
            STT(xn[0:rows], xb[0:rows], rs[0:rows, 0:1], gN[0:rows], ALU.mult, ALU.mult,
                [xr, rsres, CONST], [xnres])
            return xn, xnres

        def a_stage2(i, xn, xnres):
            rows = 128 if i < NT else NMETA
            col0 = i * 128 if i < NT else T
            c = (i // 4) if i < NT else NQC
            for g0 in range(0, DC, 4):
                ng = min(4, DC - g0)
                bank, brs = nb()
                tcount[0] += 1
                for jj in range(ng):
                    j = g0 + jj
                    TR(bank[:, jj * 128: jj * 128 + rows], xn[0:rows, j * 128:(j + 1) * 128],
                       identf[0:rows, 0:rows], [xnres, IDF], [brs])
                srcv = bank[:, 0:ng * 128].rearrange("p (j t) -> p j t", t=128)[:, :, 0:rows]
                dstv = xnT[:, g0:g0 + ng, col0:col0 + rows]
                if tcount[0] % 2:
                    ACT(dstv, srcv, AF.Copy, [brs], [XN[c]])
                else:
                    CP(dstv, srcv, [brs], [XN[c]])

        prev = None
        for i in range(NT + 1):
            cur_ = a_stage1(i)
            if prev is not None:
                a_stage2(i - 1, *prev)
            prev = cur_
        a_stage2(NT, *prev)

        R = Region()
        uT = [R.alloc([UW], BF16) for _ in range(CPG)]
        diag = R.rot(2, [CONV_K, 128], BF16)
        sgr = R.rot(2, [512], F32)
        wpool[0] = list(wbase) + extra_wbufs(R, 1)
        for jj in range(CPG):
            ua, ur = uT[jj]
            MSET(ua[:, 0:PAD], 0.0, [ur])
            MSET(ua[:, UW - PAD:UW], 0.0, [ur])

        for s in range(NWG):
            wval, wvr = wload(win_d[:, OFF["val"] + s * WG: OFF["val"] + (s + 1) * WG], WG)
            wglu, wgr = wload(win_d[:, OFF["glu"] + s * WG: OFF["glu"] + (s + 1) * WG], WG)
            for jj in range(CPG):
                j = s * CPG + jj
                ua, ur = uT[jj]
                dg, dgr = diag.next()
                TT(dg, identf.unsqueeze(1).broadcast_to([128, CONV_K, 128]),
                   convw[:, j * CONV_K:(j + 1) * CONV_K].unsqueeze(2).broadcast_to([128, CONV_K, 128]),
                   ALU.mult, [IDF, CONST], [dgr], eng="pool")
                for c, (t0, ncol) in enumerate(chunks):
                    bv, bvr = nb()
                    bg, bgr = nb()
                    mm_group(bv, bvr, [wval[:, k, jj * 128:(jj + 1) * 128] for k in range(DC)],
                             [xnT[:, k, t0:t0 + ncol] for k in range(DC)], [wvr, XN[c]], ncols=ncol)
                    mm_group(bg, bgr, [wglu[:, k, jj * 128:(jj + 1) * 128] for k in range(DC)],
                             [xnT[:, k, t0:t0 + ncol] for k in range(DC)], [wgr, XN[c]], ncols=ncol)
                    sg, sgres = sgr.next()
                    ACT(sg[:, 0:ncol], bg[:, 0:ncol], AF.Sigmoid, [bgr], [sgres])
                    ucol = (PAD + NMETA + t0) if c < NQC else PAD
                    TT(ua[:, ucol:ucol + ncol], bv[:, 0:ncol], sg[:, 0:ncol], ALU.mult, [bvr, sgres], [ur])
                for c in range(NQC):
                    t0 = c * 512
                    bc, bcr = nb()
                    mm_group(bc, bcr, [dg[:, k, :] for k in range(CONV_K)],
                             [ua[:, t0 + NMETA + k: t0 + NMETA + k + 512] for k in range(CONV_K)], [dgr, ur])
                    ACT(big1[:, j, t0:t0 + 512], bc[:, :], AF.Identity, [bcr, CONST], [B1[j, c]], bias=conv_b(j))
                    ACT(big2[:, j, t0:t0 + 512], bc[:, :], AF.Square, [bcr, CONST], [B2[j, c]], bias=conv_b(j))

        (mean, mres), (msq, qres), (rstd, rres), (nmr, nres) = [R.alloc([512], F32) for _ in range(4)]
        cnr = R.rot(2, [512], F32)
        s1r = R.rot(2, [512], BF16)
        szr = R.rot(2, [512], BF16)
        assert NWG <= 2
        wz = [wload(win_d[:, OFF["z"] + s * WG: OFF["z"] + (s + 1) * WG], WG) for s in range(NWG)]
        for c in range(NQC):
            t0 = c * 512
            bs, bsr = nb()
            bq, bqr = nb()
            mm_group(bs, bsr, [onesm] * DC, [big1[:, j, t0:t0 + 512] for j in range(DC)],
                     [CONST] + [B1[j, c] for j in range(DC)])
            mm_group(bq, bqr, [onesm] * DC, [big2[:, j, t0:t0 + 512] for j in range(DC)],
                     [CONST] + [B2[j, c] for j in range(DC)])
            TS(mean, bs[:, :], 1.0 / D, None, ALU.mult, None, [bsr], [mres])
            TT(msq, mean, mean, ALU.mult, [mres], [qres])
            STT(rstd, bq[:, :], 1.0 / D, msq, ALU.mult, ALU.subtract, [bqr, qres], [rres])
            ACT(rstd, rstd, AF.Ln, [rres], [rres], bias=EPS)
            ACT(rstd, rstd, AF.Exp, [rres], [rres], scale=-0.5)
            STT(nmr, mean, -1.0, rstd, ALU.mult, ALU.mult, [mres, rres], [nres])
            for j in range(DC):
                wzv, wzr = wz[j // CPG]
                jj = j % CPG
                bz, bzr = nb()
                mm_group(bz, bzr, [wzv[:, k, jj * 128:(jj + 1) * 128] for k in range(DC)],
                         [xnT[:, k, t0:t0 + 512] for k in range(DC)], [wzr, XN[c]])
                sz, szres = szr.next()
                ACT(sz, bz[:, :], AF.Silu, [bzr], [szres])
                cn, cnres = cnr.next()
                TT(cn, big1[:, j, t0:t0 + 512], rstd, ALU.mult, [B1[j, c], rres], [cnres])
                TT(cn, cn, nmr, ALU.add, [cnres, nres], [cnres])
                s1, s1res = s1r.next()
                ACT(s1, cn, AF.Silu, [cnres, CONST], [s1res], scale=cn_g(j), bias=cn_b(j))
                TT(big1[:, j, t0:t0 + 512], s1, sz, ALU.mult, [s1res, szres], [B1[j, c]])

        sgcr = R.rot(2, [512], F32)
        for s in range(NWG):
            wc, wcr = wload(wco_d[:, s * WG:(s + 1) * WG], WG)
            wg_, wgr_ = wload(win_d[:, OFF["gc"] + s * WG: OFF["gc"] + (s + 1) * WG], WG)
            for mi in range(CPG):
                m = s * CPG + mi
                for c in range(NQC):
                    t0 = c * 512
                    by, byr = nb()
                    bg, bgr = nb()
                    mm_group(by, byr, [wc[:, k, mi * 128:(mi + 1) * 128] for k in range(DC)],
                             [big1[:, k, t0:t0 + 512] for k in range(DC)], [wcr] + [B1[k, c] for k in range(DC)])
                    mm_group(bg, bgr, [wg_[:, k, mi * 128:(mi + 1) * 128] for k in range(DC)],
                             [xnT[:, k, t0:t0 + 512] for k in range(DC)], [wgr_, XN[c]])
                    sg, sgres = sgcr.next()
                    ACT(sg, bg[:, :], AF.Sigmoid, [bgr], [sgres])
                    TT(big2[:, m, t0:t0 + 512], by[:, :], sg, ALU.mult, [byr, sgres], [B2[m, c]])

        wpool[0] = list(wbase)
        R = Region()
        KW = T + 128
        KT = [(R.alloc([KW], BF16), R.alloc([KW], BF16)) for _ in range(NKC)]
        VA, VAr = R.alloc([NT + 1, VW], BF16)
        QTr = R.rot(2, [T], BF16)
        SAZr = R.rot(2, [T], BF16)
        GS = 2 if NQC % 2 == 0 else 1
        PTr = R.rot(3, [GS * 512], BF16)
        sqr_ = R.rot(2, [512], BF16)
        sbr_ = R.rot(2, [512], BF16)
        rsr_ = R.rot(2, [512], F32)
        t1r_ = R.rot(2, [512], F32)
        t2r_ = R.rot(1, [512], F32)
        recr = R.rot(2, [512], F32)

        def prep_qk(wr, lhs_list, gcol, t0, ncol, c, dsts):
            bp, bpr = nb()
            mm_group(bp, bpr, lhs_list, [xnT[:, k, t0:t0 + ncol] for k in range(DC)], [wr, XN[c]], ncols=ncol)
            sq, sqres = sqr_.next()
            sb, sbres = sbr_.next()
            ACT(sq[:, 0:ncol], bp[:, 0:ncol], AF.Square, [bpr], [sqres], scale=0.125)
            ACT(sb[:, 0:ncol], bp[:, 0:ncol], AF.Identity, [bpr, CONST], [sbres], scale=qkg[:, gcol:gcol + 1])
            bss, bssr = nb()
            brt, brtr = nb()
            MM(bss[:, 0:ncol], bones, sq[:, 0:ncol], True, True, [CONST, sqres], [bssr])
            MM(brt[:, 0:ncol], rotm, sb[:, 0:ncol], True, True, [CONST, sbres], [brtr])
            rs, rsres = rsr_.next()
            t1, t1res = t1r_.next()
            t2, t2res = t2r_.next()
            ACT(rs[:, 0:ncol], bss[:, 0:ncol], AF.Ln, [bssr], [rsres], bias=EPS)
            ACT(rs[:, 0:ncol], rs[:, 0:ncol], AF.Exp, [rsres], [rsres], scale=-0.5)
            TT(t1[:, 0:ncol], sb[:, 0:ncol], cosT[:, t0:t0 + ncol], ALU.mult, [sbres, CONST], [t1res])
            TT(t2[:, 0:ncol], brt[:, 0:ncol], sinT[:, t0:t0 + ncol], ALU.mult, [brtr, CONST], [t2res])
            TT(t1[:, 0:ncol], t1[:, 0:ncol], t2[:, 0:ncol], ALU.add, [t1res, t2res], [t1res])
            for dst, dres, p0, p1 in dsts:
                TT(dst[p0:p1, t0:t0 + ncol], t1[p0:p1, 0:ncol], rs[p0:p1, 0:ncol], ALU.mult, [t1res, rsres], [dres])

        def prep_q_stages(wr, lhs_list, gcol, t0, ncol, c, dsts, bk_p, bk_s, bk_r):
            (bp, bpr), (bss, bssr), (brt, brtr) = bk_p, bk_s, bk_r
            st = {}

            def s0():
                mm_group(bp, bpr, lhs_list, [xnT[:, k, t0:t0 + ncol] for k in range(DC)], [wr, XN[c]], ncols=ncol)

            def s1():
                st["sq"] = sqr_.next()
                st["sb"] = sbr_.next()
                sq, sqres = st["sq"]
                sb, sbres = st["sb"]
                ACT(sq[:, 0:ncol], bp[:, 0:ncol], AF.Square, [bpr], [sqres], scale=0.125)
                ACT(sb[:, 0:ncol], bp[:, 0:ncol], AF.Identity, [bpr, CONST], [sbres], scale=qkg[:, gcol:gcol + 1])

            def s2():
                sq, sqres = st["sq"]
                sb, sbres = st["sb"]
                MM(bss[:, 0:ncol], bones, sq[:, 0:ncol], True, True, [CONST, sqres], [bssr])
                MM(brt[:, 0:ncol], rotm, sb[:, 0:ncol], True, True, [CONST, sbres], [brtr])

            def s3():
                st["rs"] = rsr_.next()
                rs, rsres = st["rs"]
                ACT(rs[:, 0:ncol], bss[:, 0:ncol], AF.Ln, [bssr], [rsres], bias=EPS)
                ACT(rs[:, 0:ncol], rs[:, 0:ncol], AF.Exp, [rsres], [rsres], scale=-0.5)

            def s4():
                sb, sbres = st["sb"]
                rs, rsres = st["rs"]
                t1, t1res = t1r_.next()
                t2, t2res = t2r_.next()
                TT(t1[:, 0:ncol], sb[:, 0:ncol], cosT[:, t0:t0 + ncol], ALU.mult, [sbres, CONST], [t1res])
                TT(t2[:, 0:ncol], brt[:, 0:ncol], sinT[:, t0:t0 + ncol], ALU.mult, [brtr, CONST], [t2res])
                TT(t1[:, 0:ncol], t1[:, 0:ncol], t2[:, 0:ncol], ALU.add, [t1res, t2res], [t1res])
                for dst, dres, p0, p1 in dsts:
                    TT(dst[p0:p1, t0:t0 + ncol], t1[p0:p1, 0:ncol], rs[p0:p1, 0:ncol], ALU.mult,
                       [t1res, rsres], [dres])

            return [s0, s1, s2, s3, s4]

        wkv, wkvr = wload(win_d[:, OFF["k"]: OFF["k"] + 2 * KVD], 2 * KVD)
        for kc in range(NKC):
            (kta, ktares), (ktb, ktbres) = KT[kc]
            MSET(kta, 0.0, [ktares])
            MSET(ktb, 0.0, [ktbres])
            for c0 in range(0, len(chunks), 2):
                stl = []
                for ci, c in enumerate([c for c in (c0, c0 + 1) if c < len(chunks)]):
                    t0, ncol = chunks[c]
                    stl.append(prep_q_stages(wkvr, [wkv[:, k, kc * 128:(kc + 1) * 128] for k in range(DC)], 1,
                                             t0, ncol, c, [(kta, ktares, 0, HD), (ktb, ktbres, HD, 128)],
                                             banks[ci], banks[2 + 2 * ci], banks[3 + 2 * ci]))
                for si in range(5):
                    for stg in stl:
                        stg[si]()
        MSET(VA[:, 0:NT, :], 1.0, [VAr])
        MSET(VA[:, NT, :], 0.0, [VAr])
        MSET(VA[0:NMETA, NT, :].rearrange("p (c b d) -> p c b d", b=3, d=HD)[:, :, 1, :], 1.0, [VAr])
        for i in range(NT + 1):
            rows = 128 if i < NT else NMETA
            col0 = i * 128 if i < NT else T
            c = (i // 4) if i < NT else NQC
            bv, bvr = nb()
            mm_group(bv, bvr, [xnT[:, k, col0:col0 + rows] for k in range(DC)],
                     [wkv[:, k, KVD:2 * KVD] for k in range(DC)], [wkvr, XN[c]], prows=rows, ncols=KVD)
            CP(VA[0:rows, i, :].rearrange("p (c b d) -> p c b d", b=3, d=HD)[:, :, ::2, :],
               bv[0:rows, 0:KVD].rearrange("p (c b d) -> p c b d", b=2, d=HD), [bvr], [VAr])

        key_tiles = [(i * 128, 128) for i in range(NT)] + [(T, 128)]
        nbset[0] = [0, 1, 2, 3]
        unit = [0]
        step = [0]
        nk = len(key_tiles)
        for hp in range(NHP):
            wq, wqr = wload(win_d[:, OFF["q"] + hp * 128: OFF["q"] + (hp + 1) * 128], 128)
            waz, wazr = wload(win_d[:, OFF["az"] + hp * 128: OFF["az"] + (hp + 1) * 128], 128)
            qt, qtres = QTr.next()
            saz, sazres = SAZr.next()
            nbset[0] = list(range(8))
            for c0 in range(0, NQC, 2):
                cs = [c for c in (c0, c0 + 1) if c < NQC]
                stl = []
                for ci, c in enumerate(cs):
                    stl.append(prep_q_stages(wqr, [wq[:, k, 0:128] for k in range(DC)], 0, c * 512, 512, c,
                                             [(qt, qtres, 0, 128)],
                                             banks[ci], banks[2 + 2 * ci], banks[3 + 2 * ci]))
                for si in range(5):
                    for stg in stl:
                        stg[si]()
            bas = []
            for c in range(NQC):
                t0 = c * 512
                ba, bar = banks[6 + (c % 2)]
                mm_group(ba, bar, [waz[:, k, 0:128] for k in range(DC)],
                         [xnT[:, k, t0:t0 + 512] for k in range(DC)], [wazr, XN[c]])
                ACT(saz[:, t0:t0 + 512], ba, AF.Silu, [bar], [sazres])
            nbset[0] = [0, 1, 2, 3]
            kc = hp // G

            def make_epi(obanks, r0, d0, cg):
                def epi():
                    for e in range(GS):
                        c = cg * GS + e
                        t0 = c * 512
                        bo, bor = obanks[e]
                        rec, recres = recr.next()
                        P.add("dve", lambda h, o=rec[r0:r0 + HD], i_=bo[d0:d0 + HD, :]: h.reciprocal(out=o, in_=i_),
                              reads=[bor], writes=[recres])
                        TT(rec[r0:r0 + HD], rec[r0:r0 + HD], saz[r0:r0 + HD, t0:t0 + 512], ALU.mult,
                           [recres, sazres], [recres])
                        TT(big1[r0:r0 + HD, hp, t0:t0 + 512], bo[r0:r0 + HD, :], rec[r0:r0 + HD], ALU.mult,
                           [bor, recres], [B1[hp, c]])
                return epi

            flat = []
            for hh in range(2):
                for cg in range(NQC // GS):
                    unit[0] += 1
                    obanks = [banks[4 + 2 * (unit[0] % 2) + e] if GS == 2 else banks[4 + (unit[0] % 4)]
                              for e in range(GS)]
                    for i, (k0, krows) in enumerate(key_tiles):
                        flat.append((hh, cg, i, k0, krows, obanks))

            def emit_qk(fs):
                hh, cg, i, k0, krows, obanks = fs
                kt, ktres = KT[kc][hh]
                step[0] += 1
                sb0 = (2 * (step[0] % 2)) if GS == 2 else (step[0] % 4)
                sbanks = [banks[sb0 + e] for e in range(GS)]
                for e in range(GS):
                    t0 = (cg * GS + e) * 512
                    MM(sbanks[e][0], kt[:, k0:k0 + krows], qt[:, t0:t0 + 512],
                       True, True, [ktres, qtres], [sbanks[e][1]])
                return sb0, sbanks

            pend = None
            qk_next = emit_qk(flat[0])
            for si, fs in enumerate(flat):
                hh, cg, i, k0, krows, obanks = fs
                r0 = hh * HD
                d0 = (1 - hh) * HD
                v0 = kc * 3 * HD + hh * HD
                sb0, sbanks = qk_next
                if si + 1 < len(flat):
                    qk_next = emit_qk(flat[si + 1])
                pt, ptres = PTr.next()
                ACT(pt, psum_all[:, sb0 * 512:(sb0 + GS) * 512], AF.Exp, [sbk[1] for sbk in sbanks], [ptres],
                    scale=0.125)
                if pend is not None:
                    for a in pend[0]:
                        MM(*a)
                    if pend[1] is not None:
                        pend[1]()
                pend = ([(obanks[e][0], VA[0:krows, i, v0:v0 + 128], pt[:, e * 512:(e + 1) * 512],
                          i == 0, i == nk - 1, [VAr, ptres], [obanks[e][1]]) for e in range(GS)],
                        make_epi(obanks, r0, d0, cg) if i == nk - 1 else None)
            for a in pend[0]:
                MM(*a)
            pend[1]()
        nbset[0] = list(range(8))

        RT = Region()
        sgar = RT.rot(2, [512], F32)
        wpool[0] = list(wbase) + extra_wbufs(RT, 2)
        for s in range(NWG):
            wc, wcr = wload(wao_d[:, s * WG:(s + 1) * WG], WG)
            wg_, wgr_ = wload(win_d[:, OFF["ga"] + s * WG: OFF["ga"] + (s + 1) * WG], WG)
            for mi in range(CPG):
                m = s * CPG + mi
                for c in range(NQC):
                    t0 = c * 512
                    by, byr = nb()
                    bg, bgr = nb()
                    mm_group(by, byr, [wc[:, k, mi * 128:(mi + 1) * 128] for k in range(DC)],
                             [big1[:, k, t0:t0 + 512] for k in range(DC)], [wcr] + [B1[k, c] for k in range(DC)])
                    mm_group(bg, bgr, [wg_[:, k, mi * 128:(mi + 1) * 128] for k in range(DC)],
                             [xnT[:, k, t0:t0 + 512] for k in range(DC)], [wgr_, XN[c]])
                    sg, sgres = sgar.next()
                    ACT(sg, bg[:, :], AF.Sigmoid, [bgr], [sgres])
                    TT(sg, by[:, :], sg, ALU.mult, [byr, sgres], [sgres])
                    TT(big2[:, m, t0:t0 + 512], sg, big2[:, m, t0:t0 + 512], ALU.add,
                       [sgres, B2[m, c]], [B2[m, c]])

        R = RT
        ND = 4
        xres = R.rot(ND, [D], F32)
        xres_sems = [P.dsem(f"xres{b}_{i}") for i in range(ND)]
        ost = R.rot(ND, [D], F32)
        ost_sems = [P.dsem(f"ost{b}_{i}") for i in range(ND)]
        wo = [wload(wout_d[:, s * WG:(s + 1) * WG], WG) for s in range(NWG)]
        for i in range(NT):
            c = i // 4
            kx = xres.i % ND
            xb, xr = xres.next()
            DMA("sp", xb, x_d[b, i * 128:(i + 1) * 128, :], [], [xr], xres_sems[kx])
            ko = ost.i % ND
            ob, obr = ost.next()
            for s in range(NWG):
                wv, wr = wo[s]
                bo, bor = nb()
                mm_group(bo, bor, [big2[:, k, i * 128:(i + 1) * 128] for k in range(DC)],
                         [wv[:, k, 0:WG] for k in range(DC)], [wr] + [B2[k, c] for k in range(DC)], ncols=WG)
                TT(ob[:, s * WG:(s + 1) * WG], bo[:, 0:WG], xb[:, s * WG:(s + 1) * WG], ALU.add,
                   [bor, xr], [obr])
            DMA("sp", out_d[b, i * 128:(i + 1) * 128, :], ob, [obr], [], ost_sems[ko])

    P.emit(nc)
    return nc


def rope_tables_T(cfg):
    T, GW = cfg["T"], cfg["GW"]
    nf = HD // 4
    rows = T // GW
    row_ids = np.repeat(np.arange(rows, dtype=np.float32), GW)
    col_ids = np.tile(np.arange(GW, dtype=np.float32), rows)
    row_ids = np.concatenate([row_ids, np.zeros(NMETA, np.float32)])
    col_ids = np.concatenate([col_ids, np.zeros(NMETA, np.float32)])
    inv_freq = (np.float32(ROPE_THETA) ** (-np.arange(nf, dtype=np.float32) / np.float32(nf))).astype(np.float32)
    a_row = row_ids[:, None] * inv_freq[None, :]
    a_col = col_ids[:, None] * inv_freq[None, :]
    ang = np.concatenate([a_row, a_row, a_col, a_col], axis=-1).astype(np.float32)
    cosT = np.cos(ang).T.astype(np.float32)
    sinT = np.sin(ang).T.astype(np.float32)
    return (np.ascontiguousarray(np.concatenate([cosT, cosT], 0)),
            np.ascontiguousarray(np.concatenate([sinT, sinT], 0)))


def const_mats():
    ident = np.eye(128, dtype=np.float32)
    rot = np.zeros((128, 128), np.float32)
    for m in range(128):
        if (m % 32) < 16:
            rot[m + 16, m] = -1.0
        else:
            rot[m - 16, m] = 1.0
    bones = np.zeros((128, 128), np.float32)
    bones[0:64, 0:64] = 1.0
    bones[64:128, 64:128] = 1.0
    ones = np.ones((128, 128), np.float32)
    return np.ascontiguousarray(np.concatenate([ident, rot, bones, ones], 1).astype(ml_dtypes.bfloat16))


def host_inputs(cfg, x, meta_tokens, norm_g, w_in, conv_w, conv_b, conv_norm_g, conv_norm_b,
                w_conv_out, q_norm_g, k_norm_g, w_attn_out, w_out, n_cores):
    D, BPC = cfg["D"], cfg["BPC"]
    DC = D // 128
    f = lambda a: np.ascontiguousarray(np.asarray(a, dtype=np.float32))
    cosT, sinT = rope_tables_T(cfg)
    convw = f(conv_w[0]).T.reshape(DC, 128, CONV_K).transpose(1, 0, 2).reshape(128, DC * CONV_K)
    pv = lambda v: f(v[0]).reshape(DC, 128).T
    pvec = np.concatenate([pv(conv_b), pv(conv_norm_g), pv(conv_norm_b)], axis=1)
    qkg = np.stack([np.tile(f(q_norm_g[0]), 2), np.tile(f(k_norm_g[0]), 2)], axis=1)
    NH, NKV = cfg["NH"], cfg["NKV"]
    G = NH // NKV
    KVD = NKV * HD
    perm = []
    for kc in range(NKV // 2):
        for r in range(G):
            perm += [2 * kc * G + r, (2 * kc + 1) * G + r]
    perm = np.array(perm)
    w_in_p = f(w_in[0]).copy()
    for off in (3 * D, 4 * D + 2 * KVD):
        blk = w_in_p[:, off:off + D].reshape(D, NH, HD)[:, perm, :].reshape(D, D)
        w_in_p[:, off:off + D] = blk
    w_ao_p = f(w_attn_out[0]).reshape(NH, HD, D)[perm].reshape(D, D)
    shared = {
        "meta": f(meta_tokens), "w_in": f(w_in_p), "w_co": f(w_conv_out[0]), "w_ao": f(w_ao_p),
        "w_out": f(w_out[0]), "gN": f(np.broadcast_to(f(norm_g[0])[None, :], (128, D))),
        "convw": f(convw), "pvec": f(pvec), "qkg": f(qkg), "cosT": cosT, "sinT": sinT, "cmat": const_mats(),
    }
    x = f(x)
    return [dict(shared, x=np.ascontiguousarray(x[i * BPC:(i + 1) * BPC])) for i in range(n_cores)]


_NC_CACHE = {}


def kernel(x, meta_tokens, norm_g, w_in, conv_w, conv_b, conv_norm_g, conv_norm_b,
           w_conv_out, q_norm_g, k_norm_g, w_attn_out, w_out):
    cfg = full_cfg()
    n_cores = 8
    in_maps = host_inputs(cfg, x, meta_tokens, norm_g, w_in, conv_w, conv_b, conv_norm_g, conv_norm_b,
                          w_conv_out, q_norm_g, k_norm_g, w_attn_out, w_out, n_cores)
    nc = build_program(cfg)
    res = run_bass_kernel_spmd(nc, in_maps, core_ids=list(range(n_cores)))
    return np.concatenate([np.asarray(r["out"], dtype=np.float32) for r in res.results], axis=0)
```

```python
import math
import numpy as np
import ml_dtypes
import concourse.bass as bass
import concourse.mybir as mybir
from concourse.bass_utils import run_bass_kernel_spmd

F32 = mybir.dt.float32
BF16 = mybir.dt.bfloat16
AF = mybir.ActivationFunctionType
ALU = mybir.AluOpType

NMETA = 16
CONV_K = 31
PAD = CONV_K // 2
HD = 64
EPS = 1e-6
ROPE_THETA = 10000.0
EPOCH = 12000


def full_cfg():
    return dict(D=1024, T=2048, NH=16, NKV=4, BPC=2, GW=64)


class Res:
    __slots__ = ("atoms", "psum")

    def __init__(self, atoms, psum=False):
        self.atoms = tuple(atoms)
        self.psum = psum


class Op:
    __slots__ = ("idx", "eng", "fn", "deps", "dsem", "dcount", "signal", "sig", "eidx")

    def __init__(self, idx, eng, fn, dsem):
        self.idx = idx
        self.eng = eng
        self.fn = fn
        self.deps = {}
        self.dsem = dsem
        self.dcount = 0
        self.signal = False
        self.sig = 0
        self.eidx = 0


class DSem:
    def __init__(self, name):
        self.name = name
        self.count = 0
        self.handle = None


class Prog:
    ENGS = ("pe", "act", "dve", "pool", "sp")

    def __init__(self):
        self.ops = []
        self.state = {}
        self.natoms = 0
        self.dsems = []

    def atoms(self, n=1):
        a = list(range(self.natoms, self.natoms + n))
        self.natoms += n
        return a

    def res(self, psum=False):
        return Res(self.atoms(1), psum)

    def dsem(self, name):
        d = DSem(name)
        self.dsems.append(d)
        return d

    def add(self, eng, fn, reads=(), writes=(), dsem=None):
        op = Op(len(self.ops), eng, fn, dsem)
        if dsem is not None:
            dsem.count += 1
            op.dcount = dsem.count
        deps = op.deps
        st = self.state

        def dep(o, kind):
            if o is op:
                return
            k = deps.get(o)
            if k is None or kind == "raw":
                deps[o] = kind

        for r in reads:
            for a in r.atoms:
                s = st.get(a)
                if s is None:
                    s = st[a] = [None, {}, []]
                if s[0] is not None:
                    dep(s[0], "raw")
                if r.psum:
                    for e, o in s[1].items():
                        if e != eng:
                            dep(o, "excl")
                if dsem is not None:
                    s[2].append(op)
                else:
                    s[1][eng] = op
        for w in writes:
            for a in w.atoms:
                s = st.get(a)
                if s is None:
                    s = st[a] = [None, {}, []]
                if s[0] is not None:
                    if not (dsem is not None and s[0].dsem is dsem and not s[1] and not s[2]):
                        dep(s[0], "waw")
                for e, o in s[1].items():
                    dep(o, "war")
                for o in s[2]:
                    dep(o, "war")
                s[0] = op
                s[1] = {}
                s[2] = []
        self.ops.append(op)
        return op

    def emit(self, nc):
        ops = self.ops
        for op in ops:
            keep = {}
            for d, kind in op.deps.items():
                if d.dsem is None and d.eng == op.eng and op.dsem is None and op.eng == "pe":
                    continue
                keep[d] = kind
            op.deps = keep
            for d in keep:
                if d.dsem is None:
                    d.signal = True
        cnt = {e: 0 for e in self.ENGS}
        for op in ops:
            if op.dsem is None and op.signal:
                cnt[op.eng] += 1
                op.sig = cnt[op.eng]
        import contextlib

        stack = contextlib.ExitStack()
        with stack:
            esems = {}
            for e in self.ENGS:
                n = (cnt[e] + EPOCH - 1) // EPOCH
                esems[e] = [stack.enter_context(nc.semaphore(f"c_{e}_{i}")) for i in range(max(n, 1))]
            for d in self.dsems:
                d.handle = stack.enter_context(nc.semaphore(f"d_{d.name}"))
            block = stack.enter_context(nc.Block())
            per_eng = {e: [o for o in ops if o.eng == e] for e in self.ENGS}

            def run_engine(ename, h):
                known = {e: 0 for e in self.ENGS}
                dknown = {}
                for op in per_eng[ename]:
                    waits = []
                    need = {}
                    dneed = {}
                    for d in op.deps:
                        if d.dsem is not None:
                            if dknown.get(d.dsem, 0) < d.dcount and dneed.get(d.dsem, 0) < d.dcount:
                                dneed[d.dsem] = d.dcount
                        else:
                            if known[d.eng] < d.sig and need.get(d.eng, 0) < d.sig:
                                need[d.eng] = d.sig
                    for e, s in need.items():
                        known[e] = s
                        ep, loc = (s - 1) // EPOCH, (s - 1) % EPOCH + 1
                        waits.append((esems[e][ep], loc))
                    for ds, c in dneed.items():
                        dknown[ds] = c
                        waits.append((ds.handle, 16 * c))
                    attach = op.dsem is None and len(waits) > 0
                    for sem, val in (waits[:-1] if attach else waits):
                        h.wait_ge(sem, val)
                    ins = op.fn(h)
                    if attach:
                        sem, val = waits[-1]
                        ins._wait_ge(sem, val)
                    if op.dsem is not None:
                        ins.then_inc(op.dsem.handle, 16)
                    elif op.signal:
                        ep = (op.sig - 1) // EPOCH
                        ins.then_inc(esems[op.eng][ep], 1)

            @block.tensor
            def _(h):
                run_engine("pe", h)

            @block.scalar
            def _(h):
                run_engine("act", h)

            @block.vector
            def _(h):
                run_engine("dve", h)

            @block.gpsimd
            def _(h):
                run_engine("pool", h)

            @block.sync
            def _(h):
                run_engine("sp", h)
                for d in self.dsems:
                    if d.count:
                        h.wait_ge(d.handle, 16 * d.count)


class Rot:
    def __init__(self, items):
        self.items = items
        self.i = 0

    def next(self):
        it = self.items[self.i % len(self.items)]
        self.i += 1
        return it


def build_program(cfg):
    D, T, NH, NKV, BPC = cfg["D"], cfg["T"], cfg["NH"], cfg["NKV"], cfg["BPC"]
    DC = D // 128
    KVD = NKV * HD
    G = NH // NKV
    L = T + NMETA
    NT = T // 128
    NQC = T // 512
    NHP = NH // 2
    IN_DIM = 7 * D + 2 * KVD
    OFF = dict(val=0, glu=D, z=2 * D, q=3 * D, k=4 * D, v=4 * D + KVD, az=4 * D + 2 * KVD,
               gc=5 * D + 2 * KVD, ga=6 * D + 2 * KVD)
    UW = T + NMETA + 2 * PAD
    NKC = NKV // 2
    VW = NKC * 3 * HD
    WG = min(512, D)
    NWG = D // WG
    CPG = WG // 128

    nc = bass.Bass("TRN2", target_bir_lowering=False)
    P = Prog()

    def dram(name, shape, dt=F32, kind="ExternalInput"):
        return nc.dram_tensor(name, list(shape), dt, kind=kind).ap()

    x_d = dram("x", [BPC, T, D])
    meta_d = dram("meta", [NMETA, D])
    win_d = dram("w_in", [D, IN_DIM])
    wco_d = dram("w_co", [D, D])
    wao_d = dram("w_ao", [D, D])
    wout_d = dram("w_out", [D, D])
    gN_d = dram("gN", [128, D])
    convw_d = dram("convw", [128, DC * CONV_K])
    pvec_d = dram("pvec", [128, 3 * DC])
    qkg_d = dram("qkg", [128, 2])
    cos_d = dram("cosT", [128, L])
    sin_d = dram("sinT", [128, L])
    cmat_d = dram("cmat", [128, 4 * 128], BF16)
    out_d = dram("out", [BPC, T, D], F32, kind="ExternalOutput")

    arena_elems = nc.sbuf_bytes_remaining // 4 - 64
    arena = nc.alloc_sbuf_tensor("arena", [128, arena_elems], F32)
    ATOM = 512
    cur = [0]

    def alloc_bytes(nbytes):
        nb = (nbytes + 63) // 64 * 64
        o = cur[0]
        cur[0] += nb
        assert cur[0] <= arena_elems * 4, f"SBUF overflow {cur[0]} > {arena_elems * 4}"
        return o

    def view(off, shape, dt):
        esz = 2 if dt == BF16 else 4
        n = int(np.prod(shape))
        assert off % 4 == 0
        a = arena[:, off // 4: off // 4 + (n * esz + 3) // 4]
        if dt == BF16:
            a = a.bitcast(BF16)[:, 0:n]
        if len(shape) == 2:
            a = a.rearrange("p (a b) -> p a b", b=shape[1])
        elif len(shape) == 3:
            a = a.rearrange("p (a b c) -> p a b c", b=shape[1], c=shape[2])
        return a

    def alloc(shape, dt):
        esz = 2 if dt == BF16 else 4
        off = alloc_bytes(int(np.prod(shape)) * esz)
        return view(off, shape, dt)

    cmat = alloc([4 * 128], BF16)
    ident_bf, rotm, bones, onesm = (cmat[:, i * 128:(i + 1) * 128] for i in range(4))
    identf = alloc([128], F32)
    cexp = alloc([2], F32)
    gN = alloc([D], F32)
    convw = alloc([DC * CONV_K], F32)
    pvec = alloc([3 * DC], F32)
    qkg = alloc([2], F32)
    cosT = alloc([L], F32)
    sinT = alloc([L], F32)
    CONST = P.res()
    IDF = P.res()
    CEXP = P.res()
    xnT = alloc([DC, L], BF16)
    big1 = alloc([DC, T], BF16)
    big2 = alloc([DC, T], BF16)
    wbufs = [alloc([DC, WG], BF16) for _ in range(2)]
    WB = [P.res() for _ in range(2)]
    WSEM = [P.dsem(f"w{i}") for i in range(2)]
    wrot = [0]
    wbase = [(wbufs[i], WB[i], WSEM[i]) for i in range(2)]
    wpool = [list(wbase)]
    wx_count = [0]

    def extra_wbufs(R, n):
        out = []
        for _ in range(n):
            a, r = R.alloc([DC, WG], BF16)
            wx_count[0] += 1
            out.append((a, r, P.dsem(f"wx{wx_count[0]}")))
        return out
    chunks = [(c * 512, 512) for c in range(NQC)] + [(T, NMETA)]
    XN = [P.res() for _ in chunks]
    B1 = {(j, c): P.res() for j in range(DC) for c in range(NQC)}
    B2 = {(j, c): P.res() for j in range(DC) for c in range(NQC)}

    cur[0] = (cur[0] + ATOM - 1) // ATOM * ATOM
    scratch_base = alloc_bytes(0)
    scratch_size = arena_elems * 4 - scratch_base
    scratch_atoms = P.atoms((scratch_size + ATOM - 1) // ATOM)

    class Region:
        def __init__(self):
            self.off = 0

        def alloc(self, shape, dt, psum=False):
            esz = 2 if dt == BF16 else 4
            nb = (int(np.prod(shape)) * esz + ATOM - 1) // ATOM * ATOM
            o = self.off
            self.off += nb
            assert self.off <= scratch_size, f"scratch overflow {self.off} > {scratch_size}"
            a0, a1 = o // ATOM, (o + nb - 1) // ATOM
            return view(scratch_base + o, shape, dt), Res(scratch_atoms[a0:a1 + 1])

        def rot(self, n, shape, dt):
            return Rot([self.alloc(shape, dt) for _ in range(n)])

    psum_all = nc.alloc_psum_tensor("psum_all", [128, 8 * 512], F32)
    banks = [(psum_all[:, i * 512:(i + 1) * 512], P.res(psum=True)) for i in range(8)]

    def MM(out, lhs, rhs, start, stop, reads, writes):
        P.add("pe", lambda h: h.matmul(out, lhs, rhs, start=start, stop=stop), reads=reads, writes=writes)

    def TR(out, in_, ident, reads, writes):
        P.add("pe", lambda h: h.transpose(out, in_, ident), reads=reads, writes=writes)

    def ACT(out, in_, func, reads, writes, **kw):
        P.add("act", lambda h: h.activation(out=out, in_=in_, func=func, **kw), reads=reads, writes=writes)

    def TT(out, in0, in1, op, reads, writes, eng="dve"):
        P.add(eng, lambda h: h.tensor_tensor(out=out, in0=in0, in1=in1, op=op), reads=reads, writes=writes)

    def TS(out, in0, s1, s2, op0, op1, reads, writes, eng="dve"):
        if op1 is None:
            P.add(eng, lambda h: h.tensor_scalar(out=out, in0=in0, scalar1=s1, scalar2=None, op0=op0),
                  reads=reads, writes=writes)
        else:
            P.add(eng, lambda h: h.tensor_scalar(out=out, in0=in0, scalar1=s1, scalar2=s2, op0=op0, op1=op1),
                  reads=reads, writes=writes)

    def STT(out, in0, scalar, in1, op0, op1, reads, writes, eng="dve"):
        P.add(eng, lambda h: h.scalar_tensor_tensor(out=out, in0=in0, scalar=scalar, in1=in1, op0=op0, op1=op1),
              reads=reads, writes=writes)

    def POW(out, in_, col, reads, writes, p0=0, pn=128):
        n = int(np.prod(out.shape[1:]))
        shp = [pn] + list(out.shape[1:])
        e = cexp[p0:p0 + pn, col:col + 1]
        if len(shp) == 2:
            e = e.broadcast_to(shp)
        P.add("pool", lambda h: h.tensor_tensor(out=out, in0=in_, in1=e, op=ALU.pow),
              reads=list(reads) + [CEXP], writes=writes)

    def CP(out, in_, reads, writes, eng="dve"):
        P.add(eng, lambda h: h.tensor_copy(out=out, in_=in_), reads=reads, writes=writes)

    def MSET(ap, val, writes, eng="pool"):
        P.add(eng, lambda h: h.memset(ap, val), writes=writes)

    def DMA(eng, out, in_, reads, writes, dsem):
        P.add(eng, lambda h: h.dma_start(out=out, in_=in_), reads=reads, writes=writes, dsem=dsem)

    def wload(src2d, ncols):
        wb, wr, ws = wpool[0][wrot[0] % len(wpool[0])]
        wrot[0] += 1
        DMA("pool", wb[:, :, 0:ncols], src2d.rearrange("(j p) e -> p j e", p=128), [], [wr], ws)
        return wb, wr

    def mm_group(bank, brs, lhs_list, rhs_list, reads, prows=128, ncols=512):
        n = len(lhs_list)
        for k in range(n):
            MM(bank[0:prows, 0:ncols], lhs_list[k], rhs_list[k], k == 0, k == n - 1, reads, [brs])

    bsel = [0]

    nbset = [list(range(8))]

    def nb(avoid=()):
        while True:
            bsel[0] += 1
            bk = banks[nbset[0][bsel[0] % len(nbset[0])]]
            if all(bk[0] is not a for a in avoid):
                return bk

    csem = P.dsem("const")
    for dst, src in ((cmat, cmat_d), (gN, gN_d), (convw, convw_d), (pvec, pvec_d), (qkg, qkg_d),
                     (cosT, cos_d), (sinT, sin_d)):
        DMA("sp", dst, src, [], [CONST], csem)
    CP(identf, ident_bf, [CONST], [IDF])
    MSET(cexp[:, 0:1], -0.5, [CEXP])
    MSET(cexp[:, 1:2], -1.0, [CEXP])

    conv_b = lambda j: pvec[:, j:j + 1]
    cn_g = lambda j: pvec[:, DC + j:DC + j + 1]
    cn_b = lambda j: pvec[:, 2 * DC + j:2 * DC + j + 1]

    for b in range(BPC):
        wpool[0] = list(wbase)
        R = Region()
        NXB = 6
        xin = R.rot(NXB, [D], F32)
        xin_sems = [P.dsem(f"xin{b}_{i}") for i in range(NXB)]
        sqj, SQJ = R.alloc([D], BF16)
        msr = R.rot(4, [1], F32)
        rsr = R.rot(4, [1], F32)
        xnr = R.rot(3, [D], F32)
        tcount = [0]

        def a_stage1(i):
            rows = 128 if i < NT else NMETA
            src = x_d[b, i * 128:(i + 1) * 128, :] if i < NT else meta_d[:, :]
            kx = xin.i % NXB
            xb, xr = xin.next()
            DMA("sp", xb[0:rows], src, [], [xr], xin_sems[kx])
            ms, msres = msr.next()
            ACT(sqj[0:rows], xb[0:rows], AF.Square, [xr], [SQJ, msres], scale=float(D) ** -0.5,
                accum_out=ms[0:rows, 0:1])
            rs, rsres = rsr.next()
            ACT(rs[0:rows], ms[0:rows], AF.Ln, [msres], [rsres], bias=EPS)
            ACT(rs[0:rows], rs[0:rows], AF.Exp, [rsres], [rsres], scale=-0.5)
            xn, xnres = xnr.next()
            STT(xn[0:rows], xb[0:rows], rs[0:rows, 0:1], gN[0:rows], ALU.mult, ALU.mult,
                [xr, rsres, CONST], [xnres])
            return xn, xnres

        def a_stage2(i, xn, xnres):
            rows = 128 if i < NT else NMETA
            col0 = i * 128 if i < NT else T
            c = (i // 4) if i < NT else NQC
            for g0 in range(0, DC, 4):
                ng = min(4, DC - g0)
                bank, brs = nb()
                tcount[0] += 1
                for jj in range(ng):
                    j = g0 + jj
                    TR(bank[:, jj * 128: jj * 128 + rows], xn[0:rows, j * 128:(j + 1) * 128],
                       identf[0:rows, 0:rows], [xnres, IDF], [brs])
                srcv = bank[:, 0:ng * 128].rearrange("p (j t) -> p j t", t=128)[:, :, 0:rows]
                dstv = xnT[:, g0:g0 + ng, col0:col0 + rows]
                if tcount[0] % 2:
                    ACT(dstv, srcv, AF.Copy, [brs], [XN[c]])
                else:
                    CP(dstv, srcv, [brs], [XN[c]])

        prev = None
        for i in range(NT + 1):
            cur_ = a_stage1(i)
            if prev is not None:
                a_stage2(i - 1, *prev)
            prev = cur_
        a_stage2(NT, *prev)

        R = Region()
        uT = [R.alloc([UW], BF16) for _ in range(CPG)]
        diag = R.rot(2, [CONV_K, 128], BF16)
        sgr = R.rot(2, [512], F32)
        wpool[0] = list(wbase) + extra_wbufs(R, 1)
        for jj in range(CPG):
            ua, ur = uT[jj]
            MSET(ua[:, 0:PAD], 0.0, [ur])
            MSET(ua[:, UW - PAD:UW], 0.0, [ur])

        for s in range(NWG):
            wval, wvr = wload(win_d[:, OFF["val"] + s * WG: OFF["val"] + (s + 1) * WG], WG)
            wglu, wgr = wload(win_d[:, OFF["glu"] + s * WG: OFF["glu"] + (s + 1) * WG], WG)
            for jj in range(CPG):
                j = s * CPG + jj
                ua, ur = uT[jj]
                dg, dgr = diag.next()
                TT(dg, identf.unsqueeze(1).broadcast_to([128, CONV_K, 128]),
                   convw[:, j * CONV_K:(j + 1) * CONV_K].unsqueeze(2).broadcast_to([128, CONV_K, 128]),
                   ALU.mult, [IDF, CONST], [dgr], eng="pool")
                for c, (t0, ncol) in enumerate(chunks):
                    bv, bvr = nb()
                    bg, bgr = nb()
                    mm_group(bv, bvr, [wval[:, k, jj * 128:(jj + 1) * 128] for k in range(DC)],
                             [xnT[:, k, t0:t0 + ncol] for k in range(DC)], [wvr, XN[c]], ncols=ncol)
                    mm_group(bg, bgr, [wglu[:, k, jj * 128:(jj + 1) * 128] for k in range(DC)],
                             [xnT[:, k, t0:t0 + ncol] for k in range(DC)], [wgr, XN[c]], ncols=ncol)
                    sg, sgres = sgr.next()
                    ACT(sg[:, 0:ncol], bg[:, 0:ncol], AF.Sigmoid, [bgr], [sgres])
                    ucol = (PAD + NMETA + t0) if c < NQC else PAD
                    TT(ua[:, ucol:ucol + ncol], bv[:, 0:ncol], sg[:, 0:ncol], ALU.mult, [bvr, sgres], [ur])
                for c in range(NQC):
                    t0 = c * 512
                    bc, bcr = nb()
                    mm_group(bc, bcr, [dg[:, k, :] for k in range(CONV_K)],
                             [ua[:, t0 + NMETA + k: t0 + NMETA + k + 512] for k in range(CONV_K)], [dgr, ur])
                    ACT(big1[:, j, t0:t0 + 512], bc[:, :], AF.Identity, [bcr, CONST], [B1[j, c]], bias=conv_b(j))
                    ACT(big2[:, j, t0:t0 + 512], bc[:, :], AF.Square, [bcr, CONST], [B2[j, c]], bias=conv_b(j))

        (mean, mres), (msq, qres), (rstd, rres), (nmr, nres) = [R.alloc([512], F32) for _ in range(4)]
        cnr = R.rot(2, [512], F32)
        s1r = R.rot(2, [512], BF16)
        szr = R.rot(2, [512], BF16)
        assert NWG <= 2
        wz = [wload(win_d[:, OFF["z"] + s * WG: OFF["z"] + (s + 1) * WG], WG) for s in range(NWG)]
        for c in range(NQC):
            t0 = c * 512
            bs, bsr = nb()
            bq, bqr = nb()
            mm_group(bs, bsr, [onesm] * DC, [big1[:, j, t0:t0 + 512] for j in range(DC)],
                     [CONST] + [B1[j, c] for j in range(DC)])
            mm_group(bq, bqr, [onesm] * DC, [big2[:, j, t0:t0 + 512] for j in range(DC)],
                     [CONST] + [B2[j, c] for j in range(DC)])
            TS(mean, bs[:, :], 1.0 / D, None, ALU.mult, None, [bsr], [mres])
            TT(msq, mean, mean, ALU.mult, [mres], [qres])
            STT(rstd, bq[:, :], 1.0 / D, msq, ALU.mult, ALU.subtract, [bqr, qres], [rres])
            ACT(rstd, rstd, AF.Ln, [rres], [rres], bias=EPS)
            ACT(rstd, rstd, AF.Exp, [rres], [rres], scale=-0.5)
            STT(nmr, mean, -1.0, rstd, ALU.mult, ALU.mult, [mres, rres], [nres])
            for j in range(DC):
                wzv, wzr = wz[j // CPG]
                jj = j % CPG
                bz, bzr = nb()
                mm_group(bz, bzr, [wzv[:, k, jj * 128:(jj + 1) * 128] for k in range(DC)],
                         [xnT[:, k, t0:t0 + 512] for k in range(DC)], [wzr, XN[c]])
                sz, szres = szr.next()
                ACT(sz, bz[:, :], AF.Silu, [bzr], [szres])
                cn, cnres = cnr.next()
                TT(cn, big1[:, j, t0:t0 + 512], rstd, ALU.mult, [B1[j, c], rres], [cnres])
                TT(cn, cn, nmr, ALU.add, [cnres, nres], [cnres])
                s1, s1res = s1r.next()
                ACT(s1, cn, AF.Silu, [cnres, CONST], [s1res], scale=cn_g(j), bias=cn_b(j))
                TT(big1[:, j, t0:t0 + 512], s1, sz, ALU.mult, [s1res, szres], [B1[j, c]])

        sgcr = R.rot(2, [512], F32)
        for s in range(NWG):
            wc, wcr = wload(wco_d[:, s * WG:(s + 1) * WG], WG)
            wg_, wgr_ = wload(win_d[:, OFF["gc"] + s * WG: OFF["gc"] + (s + 1) * WG], WG)
            for mi in range(CPG):
                m = s * CPG + mi
                for c in range(NQC):
                    t0 = c * 512
                    by, byr = nb()
                    bg, bgr = nb()
                    mm_group(by, byr, [wc[:, k, mi * 128:(mi + 1) * 128] for k in range(DC)],
                             [big1[:, k, t0:t0 + 512] for k in range(DC)], [wcr] + [B1[k, c] for k in range(DC)])
                    mm_group(bg, bgr, [wg_[:, k, mi * 128:(mi + 1) * 128] for k in range(DC)],
                             [xnT[:, k, t0:t0 + 512] for k in range(DC)], [wgr_, XN[c]])
                    sg, sgres = sgcr.next()
                    ACT(sg, bg[:, :], AF.Sigmoid, [bgr], [sgres])
                    TT(big2[:, m, t0:t0 + 512], by[:, :], sg, ALU.mult, [byr, sgres], [B2[m, c]])

        wpool[0] = list(wbase)
        R = Region()
        KW = T + 128
        KT = [(R.alloc([KW], BF16), R.alloc([KW], BF16)) for _ in range(NKC)]
        VA, VAr = R.alloc([NT + 1, VW], BF16)
        QTr = R.rot(2, [T], BF16)
        SAZr = R.rot(2, [T], BF16)
        GS = 2 if NQC % 2 == 0 else 1
        PTr = R.rot(3, [GS * 512], BF16)
        sqr_ = R.rot(2, [512], BF16)
        sbr_ = R.rot(2, [512], BF16)
        rsr_ = R.rot(2, [512], F32)
        t1r_ = R.rot(2, [512], F32)
        t2r_ = R.rot(1, [512], F32)
        recr = R.rot(2, [512], F32)

        def prep_qk(wr, lhs_list, gcol, t0, ncol, c, dsts):
            bp, bpr = nb()
            mm_group(bp, bpr, lhs_list, [xnT[:, k, t0:t0 + ncol] for k in range(DC)], [wr, XN[c]], ncols=ncol)
            sq, sqres = sqr_.next()
            sb, sbres = sbr_.next()
            ACT(sq[:, 0:ncol], bp[:, 0:ncol], AF.Square, [bpr], [sqres], scale=0.125)
            ACT(sb[:, 0:ncol], bp[:, 0:ncol], AF.Identity, [bpr, CONST], [sbres], scale=qkg[:, gcol:gcol + 1])
            bss, bssr = nb()
            brt, brtr = nb()
            MM(bss[:, 0:ncol], bones, sq[:, 0:ncol], True, True, [CONST, sqres], [bssr])
            MM(brt[:, 0:ncol], rotm, sb[:, 0:ncol], True, True, [CONST, sbres], [brtr])
            rs, rsres = rsr_.next()
            t1, t1res = t1r_.next()
            t2, t2res = t2r_.next()
            ACT(rs[:, 0:ncol], bss[:, 0:ncol], AF.Ln, [bssr], [rsres], bias=EPS)
            ACT(rs[:, 0:ncol], rs[:, 0:ncol], AF.Exp, [rsres], [rsres], scale=-0.5)
            TT(t1[:, 0:ncol], sb[:, 0:ncol], cosT[:, t0:t0 + ncol], ALU.mult, [sbres, CONST], [t1res])
            TT(t2[:, 0:ncol], brt[:, 0:ncol], sinT[:, t0:t0 + ncol], ALU.mult, [brtr, CONST], [t2res])
            TT(t1[:, 0:ncol], t1[:, 0:ncol], t2[:, 0:ncol], ALU.add, [t1res, t2res], [t1res])
            for dst, dres, p0, p1 in dsts:
                TT(dst[p0:p1, t0:t0 + ncol], t1[p0:p1, 0:ncol], rs[p0:p1, 0:ncol], ALU.mult, [t1res, rsres], [dres])

        def prep_q_stages(wr, lhs_list, gcol, t0, ncol, c, dsts, bk_p, bk_s, bk_r):
            (bp, bpr), (bss, bssr), (brt, brtr) = bk_p, bk_s, bk_r
            st = {}

            def s0():
                mm_group(bp, bpr, lhs_list, [xnT[:, k, t0:t0 + ncol] for k in range(DC)], [wr, XN[c]], ncols=ncol)

            def s1():
                st["sq"] = sqr_.next()
                st["sb"] = sbr_.next()
                sq, sqres = st["sq"]
                sb, sbres = st["sb"]
                ACT(sq[:, 0:ncol], bp[:, 0:ncol], AF.Square, [bpr], [sqres], scale=0.125)
                ACT(sb[:, 0:ncol], bp[:, 0:ncol], AF.Identity, [bpr, CONST], [sbres], scale=qkg[:, gcol:gcol + 1])

            def s2():
                sq, sqres = st["sq"]
                sb, sbres = st["sb"]
                MM(bss[:, 0:ncol], bones, sq[:, 0:ncol], True, True, [CONST, sqres], [bssr])
                MM(brt[:, 0:ncol], rotm, sb[:, 0:ncol], True, True, [CONST, sbres], [brtr])

            def s3():
                st["rs"] = rsr_.next()
                rs, rsres = st["rs"]
                ACT(rs[:, 0:ncol], bss[:, 0:ncol], AF.Ln, [bssr], [rsres], bias=EPS)
                ACT(rs[:, 0:ncol], rs[:, 0:ncol], AF.Exp, [rsres], [rsres], scale=-0.5)

            def s4():
                sb, sbres = st["sb"]
                rs, rsres = st["rs"]
                t1, t1res = t1r_.next()
                t2, t2res = t2r_.next()
                TT(t1[:, 0:ncol], sb[:, 0:ncol], cosT[:, t0:t0 + ncol], ALU.mult, [sbres, CONST], [t1res])
                TT(t2[:, 0:ncol], brt[:, 0:ncol], sinT[:, t0:t0 + ncol], ALU.mult, [brtr, CONST], [t2res])
                TT(t1[:, 0:ncol], t1[:, 0:ncol], t2[:, 0:ncol], ALU.add, [t1res, t2res], [t1res])
                for dst, dres, p0, p1 in dsts:
                    TT(dst[p0:p1, t0:t0 + ncol], t1[p0:p1, 0:ncol], rs[p0:p1, 0:ncol], ALU.mult,
                       [t1res, rsres], [dres])

            return [s0, s1, s2, s3, s4]

        wkv, wkvr = wload(win_d[:, OFF["k"]: OFF["k"] + 2 * KVD], 2 * KVD)
        for kc in range(NKC):
            (kta, ktares), (ktb, ktbres) = KT[kc]
            MSET(kta, 0.0, [ktares])
            MSET(ktb, 0.0, [ktbres])
            for c0 in range(0, len(chunks), 2):
                stl = []
                for ci, c in enumerate([c for c in (c0, c0 + 1) if c < len(chunks)]):
                    t0, ncol = chunks[c]
                    stl.append(prep_q_stages(wkvr, [wkv[:, k, kc * 128:(kc + 1) * 128] for k in range(DC)], 1,
                                             t0, ncol, c, [(kta, ktares, 0, HD), (ktb, ktbres, HD, 128)],
                                             banks[ci], banks[2 + 2 * ci], banks[3 + 2 * ci]))
                for si in range(5):
                    for stg in stl:
                        stg[si]()
        MSET(VA[:, 0:NT, :], 1.0, [VAr])
        MSET(VA[:, NT, :], 0.0, [VAr])
        MSET(VA[0:NMETA, NT, :].rearrange("p (c b d) -> p c b d", b=3, d=HD)[:, :, 1, :], 1.0, [VAr])
        for i in range(NT + 1):
            rows = 128 if i < NT else NMETA
            col0 = i * 128 if i < NT else T
            c = (i // 4) if i < NT else NQC
            bv, bvr = nb()
            mm_group(bv, bvr, [xnT[:, k, col0:col0 + rows] for k in range(DC)],
                     [wkv[:, k, KVD:2 * KVD] for k in range(DC)], [wkvr, XN[c]], prows=rows, ncols=KVD)
            CP(VA[0:rows, i, :].rearrange("p (c b d) -> p c b d", b=3, d=HD)[:, :, ::2, :],
               bv[0:rows, 0:KVD].rearrange("p (c b d) -> p c b d", b=2, d=HD), [bvr], [VAr])

        key_tiles = [(i * 128, 128) for i in range(NT)] + [(T, 128)]
        nbset[0] = [0, 1, 2, 3]
        unit = [0]
        step = [0]
        nk = len(key_tiles)
        for hp in range(NHP):
            wq, wqr = wload(win_d[:, OFF["q"] + hp * 128: OFF["q"] + (hp + 1) * 128], 128)
            waz, wazr = wload(win_d[:, OFF["az"] + hp * 128: OFF["az"] + (hp + 1) * 128], 128)
            qt, qtres = QTr.next()
            saz, sazres = SAZr.next()
            nbset[0] = list(range(8))
            for c0 in range(0, NQC, 2):
                cs = [c for c in (c0, c0 + 1) if c < NQC]
                stl = []
                for ci, c in enumerate(cs):
                    stl.append(prep_q_stages(wqr, [wq[:, k, 0:128] for k in range(DC)], 0, c * 512, 512, c,
                                             [(qt, qtres, 0, 128)],
                                             banks[ci], banks[2 + 2 * ci], banks[3 + 2 * ci]))
                for si in range(5):
                    for stg in stl:
                        stg[si]()
                    if si == 0:
                        for c in cs:
                            ba, bar = banks[6 + (c % 2)]
                            mm_group(ba, bar, [waz[:, k, 0:128] for k in range(DC)],
                                     [xnT[:, k, c * 512:c * 512 + 512] for k in range(DC)], [wazr, XN[c]])
                    if si == 3:
                        for c in cs:
                            ba, bar = banks[6 + (c % 2)]
                            ACT(saz[:, c * 512:c * 512 + 512], ba, AF.Silu, [bar], [sazres])
            nbset[0] = [0, 1, 2, 3]
            kc = hp // G

            def make_epi(obanks, r0, d0, cg):
                def epi():
                    for e in range(GS):
                        c = cg * GS + e
                        t0 = c * 512
                        bo, bor = obanks[e]
                        rec, recres = recr.next()
                        P.add("dve", lambda h, o=rec[r0:r0 + HD], i_=bo[d0:d0 + HD, :]: h.reciprocal(out=o, in_=i_),
                              reads=[bor], writes=[recres])
                        TT(rec[r0:r0 + HD], rec[r0:r0 + HD], saz[r0:r0 + HD, t0:t0 + 512], ALU.mult,
                           [recres, sazres], [recres])
                        TT(big1[r0:r0 + HD, hp, t0:t0 + 512], bo[r0:r0 + HD, :], rec[r0:r0 + HD], ALU.mult,
                           [bor, recres], [B1[hp, c]])
                return epi

            flat = []
            for hh in range(2):
                for cg in range(NQC // GS):
                    unit[0] += 1
                    obanks = [banks[4 + 2 * (unit[0] % 2) + e] if GS == 2 else banks[4 + (unit[0] % 4)]
                              for e in range(GS)]
                    for i, (k0, krows) in enumerate(key_tiles):
                        flat.append((hh, cg, i, k0, krows, obanks))

            def emit_qk(fs):
                hh, cg, i, k0, krows, obanks = fs
                kt, ktres = KT[kc][hh]
                step[0] += 1
                sb0 = (2 * (step[0] % 2)) if GS == 2 else (step[0] % 4)
                sbanks = [banks[sb0 + e] for e in range(GS)]
                for e in range(GS):
                    t0 = (cg * GS + e) * 512
                    MM(sbanks[e][0], kt[:, k0:k0 + krows], qt[:, t0:t0 + 512],
                       True, True, [ktres, qtres], [sbanks[e][1]])
                return sb0, sbanks

            pend = None
            qk_next = emit_qk(flat[0])
            for si, fs in enumerate(flat):
                hh, cg, i, k0, krows, obanks = fs
                r0 = hh * HD
                d0 = (1 - hh) * HD
                v0 = kc * 3 * HD + hh * HD
                sb0, sbanks = qk_next
                if si + 1 < len(flat):
                    qk_next = emit_qk(flat[si + 1])
                pt, ptres = PTr.next()
                ACT(pt, psum_all[:, sb0 * 512:(sb0 + GS) * 512], AF.Exp, [sbk[1] for sbk in sbanks], [ptres],
                    scale=0.125)
                if pend is not None:
                    for a in pend[0]:
                        MM(*a)
                    if pend[1] is not None:
                        pend[1]()
                pend = ([(obanks[e][0], VA[0:krows, i, v0:v0 + 128], pt[:, e * 512:(e + 1) * 512],
                          i == 0, i == nk - 1, [VAr, ptres], [obanks[e][1]]) for e in range(GS)],
                        make_epi(obanks, r0, d0, cg) if i == nk - 1 else None)
            for a in pend[0]:
                MM(*a)
            pend[1]()
        nbset[0] = list(range(8))

        RT = Region()
        sgar = RT.rot(2, [512], F32)
        wpool[0] = list(wbase) + extra_wbufs(RT, 2)
        for s in range(NWG):
            wc, wcr = wload(wao_d[:, s * WG:(s + 1) * WG], WG)
            wg_, wgr_ = wload(win_d[:, OFF["ga"] + s * WG: OFF["ga"] + (s + 1) * WG], WG)
            for mi in range(CPG):
                m = s * CPG + mi
                for c in range(NQC):
                    t0 = c * 512
                    by, byr = nb()
                    bg, bgr = nb()
                    mm_group(by, byr, [wc[:, k, mi * 128:(mi + 1) * 128] for k in range(DC)],
                             [big1[:, k, t0:t0 + 512] for k in range(DC)], [wcr] + [B1[k, c] for k in range(DC)])
                    mm_group(bg, bgr, [wg_[:, k, mi * 128:(mi + 1) * 128] for k in range(DC)],
                             [xnT[:, k, t0:t0 + 512] for k in range(DC)], [wgr_, XN[c]])
                    sg, sgres = sgar.next()
                    ACT(sg, bg[:, :], AF.Sigmoid, [bgr], [sgres])
                    TT(sg, by[:, :], sg, ALU.mult, [byr, sgres], [sgres])
                    TT(big2[:, m, t0:t0 + 512], sg, big2[:, m, t0:t0 + 512], ALU.add,
                       [sgres, B2[m, c]], [B2[m, c]])

        R = RT
        ND = 4
        xres = R.rot(ND, [D], F32)
        xres_sems = [P.dsem(f"xres{b}_{i}") for i in range(ND)]
        ost = R.rot(ND, [D], F32)
        ost_sems = [P.dsem(f"ost{b}_{i}") for i in range(ND)]
        wo = [wload(wout_d[:, s * WG:(s + 1) * WG], WG) for s in range(NWG)]
        for i in range(NT):
            c = i // 4
            kx = xres.i % ND
            xb, xr = xres.next()
            DMA("sp", xb, x_d[b, i * 128:(i + 1) * 128, :], [], [xr], xres_sems[kx])
            ko = ost.i % ND
            ob, obr = ost.next()
            for s in range(NWG):
                wv, wr = wo[s]
                bo, bor = nb()
                mm_group(bo, bor, [big2[:, k, i * 128:(i + 1) * 128] for k in range(DC)],
                         [wv[:, k, 0:WG] for k in range(DC)], [wr] + [B2[k, c] for k in range(DC)], ncols=WG)
                TT(ob[:, s * WG:(s + 1) * WG], bo[:, 0:WG], xb[:, s * WG:(s + 1) * WG], ALU.add,
                   [bor, xr], [obr])
            DMA("sp", out_d[b, i * 128:(i + 1) * 128, :], ob, [obr], [], ost_sems[ko])

    P.emit(nc)
    return nc


def rope_tables_T(cfg):
    T, GW = cfg["T"], cfg["GW"]
    nf = HD // 4
    rows = T // GW
    row_ids = np.repeat(np.arange(rows, dtype=np.float32), GW)
    col_ids = np.tile(np.arange(GW, dtype=np.float32), rows)
    row_ids = np.concatenate([row_ids, np.zeros(NMETA, np.float32)])
    col_ids = np.concatenate([col_ids, np.zeros(NMETA, np.float32)])
    inv_freq = (np.float32(ROPE_THETA) ** (-np.arange(nf, dtype=np.float32) / np.float32(nf))).astype(np.float32)
    a_row = row_ids[:, None] * inv_freq[None, :]
    a_col = col_ids[:, None] * inv_freq[None, :]
    ang = np.concatenate([a_row, a_row, a_col, a_col], axis=-1).astype(np.float32)
    cosT = np.cos(ang).T.astype(np.float32)
    sinT = np.sin(ang).T.astype(np.float32)
    return (np.ascontiguousarray(np.concatenate([cosT, cosT], 0)),
            np.ascontiguousarray(np.concatenate([sinT, sinT], 0)))


def const_mats():
    ident = np.eye(128, dtype=np.float32)
    rot = np.zeros((128, 128), np.float32)
    for m in range(128):
        if (m % 32) < 16:
            rot[m + 16, m] = -1.0
        else:
            rot[m - 16, m] = 1.0
    bones = np.zeros((128, 128), np.float32)
    bones[0:64, 0:64] = 1.0
    bones[64:128, 64:128] = 1.0
    ones = np.ones((128, 128), np.float32)
    return np.ascontiguousarray(np.concatenate([ident, rot, bones, ones], 1).astype(ml_dtypes.bfloat16))


def host_inputs(cfg, x, meta_tokens, norm_g, w_in, conv_w, conv_b, conv_norm_g, conv_norm_b,
                w_conv_out, q_norm_g, k_norm_g, w_attn_out, w_out, n_cores):
    D, BPC = cfg["D"], cfg["BPC"]
    DC = D // 128
    f = lambda a: np.ascontiguousarray(np.asarray(a, dtype=np.float32))
    cosT, sinT = rope_tables_T(cfg)
    convw = f(conv_w[0]).T.reshape(DC, 128, CONV_K).transpose(1, 0, 2).reshape(128, DC * CONV_K)
    pv = lambda v: f(v[0]).reshape(DC, 128).T
    pvec = np.concatenate([pv(conv_b), pv(conv_norm_g), pv(conv_norm_b)], axis=1)
    qkg = np.stack([np.tile(f(q_norm_g[0]), 2), np.tile(f(k_norm_g[0]), 2)], axis=1)
    NH, NKV = cfg["NH"], cfg["NKV"]
    G = NH // NKV
    KVD = NKV * HD
    perm = []
    for kc in range(NKV // 2):
        for r in range(G):
            perm += [2 * kc * G + r, (2 * kc + 1) * G + r]
    perm = np.array(perm)
    w_in_p = f(w_in[0]).copy()
    for off in (3 * D, 4 * D + 2 * KVD):
        blk = w_in_p[:, off:off + D].reshape(D, NH, HD)[:, perm, :].reshape(D, D)
        w_in_p[:, off:off + D] = blk
    w_ao_p = f(w_attn_out[0]).reshape(NH, HD, D)[perm].reshape(D, D)
    shared = {
        "meta": f(meta_tokens), "w_in": f(w_in_p), "w_co": f(w_conv_out[0]), "w_ao": f(w_ao_p),
        "w_out": f(w_out[0]), "gN": f(np.broadcast_to(f(norm_g[0])[None, :], (128, D))),
        "convw": f(convw), "pvec": f(pvec), "qkg": f(qkg), "cosT": cosT, "sinT": sinT, "cmat": const_mats(),
    }
    x = f(x)
    return [dict(shared, x=np.ascontiguousarray(x[i * BPC:(i + 1) * BPC])) for i in range(n_cores)]


_NC_CACHE = {}


def kernel(x, meta_tokens, norm_g, w_in, conv_w, conv_b, conv_norm_g, conv_norm_b,
           w_conv_out, q_norm_g, k_norm_g, w_attn_out, w_out):
    cfg = full_cfg()
    n_cores = 8
    in_maps = host_inputs(cfg, x, meta_tokens, norm_g, w_in, conv_w, conv_b, conv_norm_g, conv_norm_b,
                          w_conv_out, q_norm_g, k_norm_g, w_attn_out, w_out, n_cores)
    nc = build_program(cfg)
    res = run_bass_kernel_spmd(nc, in_maps, core_ids=list(range(n_cores)))
    return np.concatenate([np.asarray(r["out"], dtype=np.float32) for r in res.results], axis=0)
```

```python
import math
import numpy as np
import ml_dtypes
import concourse.bass as bass
import concourse.mybir as mybir
from concourse.bass_utils import run_bass_kernel_spmd

F32 = mybir.dt.float32
BF16 = mybir.dt.bfloat16
AF = mybir.ActivationFunctionType
ALU = mybir.AluOpType

NMETA = 16
CONV_K = 31
PAD = CONV_K // 2
HD = 64
EPS = 1e-6
ROPE_THETA = 10000.0
EPOCH = 12000


def full_cfg():
    return dict(D=1024, T=2048, NH=16, NKV=4, BPC=2, GW=64)


class Res:
    __slots__ = ("atoms", "psum")

    def __init__(self, atoms, psum=False):
        self.atoms = tuple(atoms)
        self.psum = psum


class Op:
    __slots__ = ("idx", "eng", "fn", "deps", "dsem", "dcount", "signal", "sig", "eidx")

    def __init__(self, idx, eng, fn, dsem):
        self.idx = idx
        self.eng = eng
        self.fn = fn
        self.deps = {}
        self.dsem = dsem
        self.dcount = 0
        self.signal = False
        self.sig = 0
        self.eidx = 0


class DSem:
    def __init__(self, name):
        self.name = name
        self.count = 0
        self.handle = None


class Prog:
    ENGS = ("pe", "act", "dve", "pool", "sp")

    def __init__(self):
        self.ops = []
        self.state = {}
        self.natoms = 0
        self.dsems = []

    def atoms(self, n=1):
        a = list(range(self.natoms, self.natoms + n))
        self.natoms += n
        return a

    def res(self, psum=False):
        return Res(self.atoms(1), psum)

    def dsem(self, name):
        d = DSem(name)
        self.dsems.append(d)
        return d

    def add(self, eng, fn, reads=(), writes=(), dsem=None):
        op = Op(len(self.ops), eng, fn, dsem)
        if dsem is not None:
            dsem.count += 1
            op.dcount = dsem.count
        deps = op.deps
        st = self.state

        def dep(o, kind):
            if o is op:
                return
            k = deps.get(o)
            if k is None or kind == "raw":
                deps[o] = kind

        for r in reads:
            for a in r.atoms:
                s = st.get(a)
                if s is None:
                    s = st[a] = [None, {}, []]
                if s[0] is not None:
                    dep(s[0], "raw")
                if r.psum:
                    for e, o in s[1].items():
                        if e != eng:
                            dep(o, "excl")
                if dsem is not None:
                    s[2].append(op)
                else:
                    s[1][eng] = op
        for w in writes:
            for a in w.atoms:
                s = st.get(a)
                if s is None:
                    s = st[a] = [None, {}, []]
                if s[0] is not None:
                    if not (dsem is not None and s[0].dsem is dsem and not s[1] and not s[2]):
                        dep(s[0], "waw")
                for e, o in s[1].items():
                    dep(o, "war")
                for o in s[2]:
                    dep(o, "war")
                s[0] = op
                s[1] = {}
                s[2] = []
        self.ops.append(op)
        return op

    def emit(self, nc):
        ops = self.ops
        for op in ops:
            keep = {}
            for d, kind in op.deps.items():
                if d.dsem is None and d.eng == op.eng and op.dsem is None and op.eng == "pe":
                    continue
                keep[d] = kind
            op.deps = keep
            for d in keep:
                if d.dsem is None:
                    d.signal = True
        cnt = {e: 0 for e in self.ENGS}
        for op in ops:
            if op.dsem is None and op.signal:
                cnt[op.eng] += 1
                op.sig = cnt[op.eng]
        import contextlib

        stack = contextlib.ExitStack()
        with stack:
            esems = {}
            for e in self.ENGS:
                n = (cnt[e] + EPOCH - 1) // EPOCH
                esems[e] = [stack.enter_context(nc.semaphore(f"c_{e}_{i}")) for i in range(max(n, 1))]
            for d in self.dsems:
                d.handle = stack.enter_context(nc.semaphore(f"d_{d.name}"))
            block = stack.enter_context(nc.Block())
            per_eng = {e: [o for o in ops if o.eng == e] for e in self.ENGS}

            def run_engine(ename, h):
                known = {e: 0 for e in self.ENGS}
                dknown = {}
                for op in per_eng[ename]:
                    waits = []
                    need = {}
                    dneed = {}
                    for d in op.deps:
                        if d.dsem is not None:
                            if dknown.get(d.dsem, 0) < d.dcount and dneed.get(d.dsem, 0) < d.dcount:
                                dneed[d.dsem] = d.dcount
                        else:
                            if known[d.eng] < d.sig and need.get(d.eng, 0) < d.sig:
                                need[d.eng] = d.sig
                    for e, s in need.items():
                        known[e] = s
                        ep, loc = (s - 1) // EPOCH, (s - 1) % EPOCH + 1
                        waits.append((esems[e][ep], loc))
                    for ds, c in dneed.items():
                        dknown[ds] = c
                        waits.append((ds.handle, 16 * c))
                    attach = op.dsem is None and len(waits) > 0
                    for sem, val in (waits[:-1] if attach else waits):
                        h.wait_ge(sem, val)
                    ins = op.fn(h)
                    if attach:
                        sem, val = waits[-1]
                        ins._wait_ge(sem, val)
                    if op.dsem is not None:
                        ins.then_inc(op.dsem.handle, 16)
                    elif op.signal:
                        ep = (op.sig - 1) // EPOCH
                        ins.then_inc(esems[op.eng][ep], 1)

            @block.tensor
            def _(h):
                run_engine("pe", h)

            @block.scalar
            def _(h):
                run_engine("act", h)

            @block.vector
            def _(h):
                run_engine("dve", h)

            @block.gpsimd
            def _(h):
                run_engine("pool", h)

            @block.sync
            def _(h):
                run_engine("sp", h)
                for d in self.dsems:
                    if d.count:
                        h.wait_ge(d.handle, 16 * d.count)


class Rot:
    def __init__(self, items):
        self.items = items
        self.i = 0

    def next(self):
        it = self.items[self.i % len(self.items)]
        self.i += 1
        return it


def build_program(cfg):
    D, T, NH, NKV, BPC = cfg["D"], cfg["T"], cfg["NH"], cfg["NKV"], cfg["BPC"]
    DC = D // 128
    KVD = NKV * HD
    G = NH // NKV
    L = T + NMETA
    NT = T // 128
    NQC = T // 512
    NHP = NH // 2
    IN_DIM = 7 * D + 2 * KVD
    OFF = dict(val=0, glu=D, z=2 * D, q=3 * D, k=4 * D, v=4 * D + KVD, az=4 * D + 2 * KVD,
               gc=5 * D + 2 * KVD, ga=6 * D + 2 * KVD)
    UW = T + NMETA + 2 * PAD
    NKC = NKV // 2
    VW = NKC * 3 * HD
    WG = min(512, D)
    NWG = D // WG
    CPG = WG // 128

    nc = bass.Bass("TRN2", target_bir_lowering=False)
    P = Prog()

    def dram(name, shape, dt=F32, kind="ExternalInput"):
        return nc.dram_tensor(name, list(shape), dt, kind=kind).ap()

    x_d = dram("x", [BPC, T, D])
    meta_d = dram("meta", [NMETA, D])
    win_d = dram("w_in", [D, IN_DIM])
    wco_d = dram("w_co", [D, D])
    wao_d = dram("w_ao", [D, D])
    wout_d = dram("w_out", [D, D])
    gN_d = dram("gN", [128, D])
    convw_d = dram("convw", [128, DC * CONV_K])
    pvec_d = dram("pvec", [128, 3 * DC])
    qkg_d = dram("qkg", [128, 2])
    cos_d = dram("cosT", [128, L])
    sin_d = dram("sinT", [128, L])
    cmat_d = dram("cmat", [128, 4 * 128], BF16)
    out_d = dram("out", [BPC, T, D], F32, kind="ExternalOutput")

    arena_elems = nc.sbuf_bytes_remaining // 4 - 64
    arena = nc.alloc_sbuf_tensor("arena", [128, arena_elems], F32)
    ATOM = 512
    cur = [0]

    def alloc_bytes(nbytes):
        nb = (nbytes + 63) // 64 * 64
        o = cur[0]
        cur[0] += nb
        assert cur[0] <= arena_elems * 4, f"SBUF overflow {cur[0]} > {arena_elems * 4}"
        return o

    def view(off, shape, dt):
        esz = 2 if dt == BF16 else 4
        n = int(np.prod(shape))
        assert off % 4 == 0
        a = arena[:, off // 4: off // 4 + (n * esz + 3) // 4]
        if dt == BF16:
            a = a.bitcast(BF16)[:, 0:n]
        if len(shape) == 2:
            a = a.rearrange("p (a b) -> p a b", b=shape[1])
        elif len(shape) == 3:
            a = a.rearrange("p (a b c) -> p a b c", b=shape[1], c=shape[2])
        return a

    def alloc(shape, dt):
        esz = 2 if dt == BF16 else 4
        off = alloc_bytes(int(np.prod(shape)) * esz)
        return view(off, shape, dt)

    cmat = alloc([4 * 128], BF16)
    ident_bf, rotm, bones, onesm = (cmat[:, i * 128:(i + 1) * 128] for i in range(4))
    identf = alloc([128], F32)
    cexp = alloc([2], F32)
    gN = alloc([D], F32)
    convw = alloc([DC * CONV_K], F32)
    pvec = alloc([3 * DC], F32)
    qkg = alloc([2], F32)
    cosT = alloc([L], F32)
    sinT = alloc([L], F32)
    CONST = P.res()
    IDF = P.res()
    CEXP = P.res()
    xnT = alloc([DC, L], BF16)
    big1 = alloc([DC, T], BF16)
    big2 = alloc([DC, T], BF16)
    wbufs = [alloc([DC, WG], BF16) for _ in range(2)]
    WB = [P.res() for _ in range(2)]
    WSEM = [P.dsem(f"w{i}") for i in range(2)]
    wrot = [0]
    wbase = [(wbufs[i], WB[i], WSEM[i]) for i in range(2)]
    wpool = [list(wbase)]
    wx_count = [0]

    def extra_wbufs(R, n):
        out = []
        for _ in range(n):
            a, r = R.alloc([DC, WG], BF16)
            wx_count[0] += 1
            out.append((a, r, P.dsem(f"wx{wx_count[0]}")))
        return out
    chunks = [(c * 512, 512) for c in range(NQC)] + [(T, NMETA)]
    XN = [P.res() for _ in chunks]
    B1 = {(j, c): P.res() for j in range(DC) for c in range(NQC)}
    B2 = {(j, c): P.res() for j in range(DC) for c in range(NQC)}

    cur[0] = (cur[0] + ATOM - 1) // ATOM * ATOM
    scratch_base = alloc_bytes(0)
    scratch_size = arena_elems * 4 - scratch_base
    scratch_atoms = P.atoms((scratch_size + ATOM - 1) // ATOM)

    class Region:
        def __init__(self):
            self.off = 0

        def alloc(self, shape, dt, psum=False):
            esz = 2 if dt == BF16 else 4
            nb = (int(np.prod(shape)) * esz + ATOM - 1) // ATOM * ATOM
            o = self.off
            self.off += nb
            assert self.off <= scratch_size, f"scratch overflow {self.off} > {scratch_size}"
            a0, a1 = o // ATOM, (o + nb - 1) // ATOM
            return view(scratch_base + o, shape, dt), Res(scratch_atoms[a0:a1 + 1])

        def rot(self, n, shape, dt):
            return Rot([self.alloc(shape, dt) for _ in range(n)])

    psum_all = nc.alloc_psum_tensor("psum_all", [128, 8 * 512], F32)
    banks = [(psum_all[:, i * 512:(i + 1) * 512], P.res(psum=True)) for i in range(8)]

    def MM(out, lhs, rhs, start, stop, reads, writes):
        P.add("pe", lambda h: h.matmul(out, lhs, rhs, start=start, stop=stop), reads=reads, writes=writes)

    def TR(out, in_, ident, reads, writes):
        P.add("pe", lambda h: h.transpose(out, in_, ident), reads=reads, writes=writes)

    def ACT(out, in_, func, reads, writes, **kw):
        P.add("act", lambda h: h.activation(out=out, in_=in_, func=func, **kw), reads=reads, writes=writes)

    def TT(out, in0, in1, op, reads, writes, eng="dve"):
        P.add(eng, lambda h: h.tensor_tensor(out=out, in0=in0, in1=in1, op=op), reads=reads, writes=writes)

    def TS(out, in0, s1, s2, op0, op1, reads, writes, eng="dve"):
        if op1 is None:
            P.add(eng, lambda h: h.tensor_scalar(out=out, in0=in0, scalar1=s1, scalar2=None, op0=op0),
                  reads=reads, writes=writes)
        else:
            P.add(eng, lambda h: h.tensor_scalar(out=out, in0=in0, scalar1=s1, scalar2=s2, op0=op0, op1=op1),
                  reads=reads, writes=writes)

    def STT(out, in0, scalar, in1, op0, op1, reads, writes, eng="dve"):
        P.add(eng, lambda h: h.scalar_tensor_tensor(out=out, in0=in0, scalar=scalar, in1=in1, op0=op0, op1=op1),
              reads=reads, writes=writes)

    def POW(out, in_, col, reads, writes, p0=0, pn=128):
        n = int(np.prod(out.shape[1:]))
        shp = [pn] + list(out.shape[1:])
        e = cexp[p0:p0 + pn, col:col + 1]
        if len(shp) == 2:
            e = e.broadcast_to(shp)
        P.add("pool", lambda h: h.tensor_tensor(out=out, in0=in_, in1=e, op=ALU.pow),
              reads=list(reads) + [CEXP], writes=writes)

    def CP(out, in_, reads, writes, eng="dve"):
        P.add(eng, lambda h: h.tensor_copy(out=out, in_=in_), reads=reads, writes=writes)

    def MSET(ap, val, writes, eng="pool"):
        P.add(eng, lambda h: h.memset(ap, val), writes=writes)

    def DMA(eng, out, in_, reads, writes, dsem):
        P.add(eng, lambda h: h.dma_start(out=out, in_=in_), reads=reads, writes=writes, dsem=dsem)

    def wload(src2d, ncols):
        wb, wr, ws = wpool[0][wrot[0] % len(wpool[0])]
        wrot[0] += 1
        DMA("pool", wb[:, :, 0:ncols], src2d.rearrange("(j p) e -> p j e", p=128), [], [wr], ws)
        return wb, wr

    def mm_group(bank, brs, lhs_list, rhs_list, reads, prows=128, ncols=512):
        n = len(lhs_list)
        for k in range(n):
            MM(bank[0:prows, 0:ncols], lhs_list[k], rhs_list[k], k == 0, k == n - 1, reads, [brs])

    bsel = [0]

    nbset = [list(range(8))]

    def nb(avoid=()):
        while True:
            bsel[0] += 1
            bk = banks[nbset[0][bsel[0] % len(nbset[0])]]
            if all(bk[0] is not a for a in avoid):
                return bk

    csem = P.dsem("const")
    for dst, src in ((cmat, cmat_d), (gN, gN_d), (convw, convw_d), (pvec, pvec_d), (qkg, qkg_d),
                     (cosT, cos_d), (sinT, sin_d)):
        DMA("sp", dst, src, [], [CONST], csem)
    CP(identf, ident_bf, [CONST], [IDF])
    MSET(cexp[:, 0:1], -0.5, [CEXP])
    MSET(cexp[:, 1:2], -1.0, [CEXP])

    conv_b = lambda j: pvec[:, j:j + 1]
    cn_g = lambda j: pvec[:, DC + j:DC + j + 1]
    cn_b = lambda j: pvec[:, 2 * DC + j:2 * DC + j + 1]

    for b in range(BPC):
        wpool[0] = list(wbase)
        R = Region()
        NXB = 6
        xin = R.rot(NXB, [D], F32)
        xin_sems = [P.dsem(f"xin{b}_{i}") for i in range(NXB)]
        sqj, SQJ = R.alloc([D], BF16)
        msr = R.rot(4, [1], F32)
        rsr = R.rot(4, [1], F32)
        xnr = R.rot(3, [D], F32)
        tcount = [0]

        def a_stage1(i):
            rows = 128 if i < NT else NMETA
            src = x_d[b, i * 128:(i + 1) * 128, :] if i < NT else meta_d[:, :]
            kx = xin.i % NXB
            xb, xr = xin.next()
            DMA("sp", xb[0:rows], src, [], [xr], xin_sems[kx])
            ms, msres = msr.next()
            ACT(sqj[0:rows], xb[0:rows], AF.Square, [xr], [SQJ, msres], scale=float(D) ** -0.5,
                accum_out=ms[0:rows, 0:1])
            rs, rsres = rsr.next()
            ACT(rs[0:rows], ms[0:rows], AF.Ln, [msres], [rsres], bias=EPS)
            ACT(rs[0:rows], rs[0:rows], AF.Exp, [rsres], [rsres], scale=-0.5)
            xn, xnres = xnr.next()
            STT(xn[0:rows], xb[0:rows], rs[0:rows, 0:1], gN[0:rows], ALU.mult, ALU.mult,
                [xr, rsres, CONST], [xnres])
            return xn, xnres

        def a_stage2(i, xn, xnres):
            rows = 128 if i < NT else NMETA
            col0 = i * 128 if i < NT else T
            c = (i // 4) if i < NT else NQC
            for g0 in range(0, DC, 4):
                ng = min(4, DC - g0)
                bank, brs = nb()
                tcount[0] += 1
                for jj in range(ng):
                    j = g0 + jj
                    TR(bank[:, jj * 128: jj * 128 + rows], xn[0:rows, j * 128:(j + 1) * 128],
                       identf[0:rows, 0:rows], [xnres, IDF], [brs])
                srcv = bank[:, 0:ng * 128].rearrange("p (j t) -> p j t", t=128)[:, :, 0:rows]
                dstv = xnT[:, g0:g0 + ng, col0:col0 + rows]
                if tcount[0] % 2:
                    ACT(dstv, srcv, AF.Copy, [brs], [XN[c]])
                else:
                    CP(dstv, srcv, [brs], [XN[c]])

        prev = None
        for i in range(NT + 1):
            cur_ = a_stage1(i)
            if prev is not None:
                a_stage2(i - 1, *prev)
            prev = cur_
        a_stage2(NT, *prev)

        R = Region()
        uT = [R.alloc([UW], BF16) for _ in range(CPG)]
        diag = R.rot(2, [CONV_K, 128], BF16)
        sgr = R.rot(2, [512], F32)
        wpool[0] = list(wbase) + extra_wbufs(R, 1)
        for jj in range(CPG):
            ua, ur = uT[jj]
            MSET(ua[:, 0:PAD], 0.0, [ur])
            MSET(ua[:, UW - PAD:UW], 0.0, [ur])

        for s in range(NWG):
            wval, wvr = wload(win_d[:, OFF["val"] + s * WG: OFF["val"] + (s + 1) * WG], WG)
            wglu, wgr = wload(win_d[:, OFF["glu"] + s * WG: OFF["glu"] + (s + 1) * WG], WG)
            for jj in range(CPG):
                j = s * CPG + jj
                ua, ur = uT[jj]
                dg, dgr = diag.next()
                TT(dg, identf.unsqueeze(1).broadcast_to([128, CONV_K, 128]),
                   convw[:, j * CONV_K:(j + 1) * CONV_K].unsqueeze(2).broadcast_to([128, CONV_K, 128]),
                   ALU.mult, [IDF, CONST], [dgr], eng="pool")
                for c, (t0, ncol) in enumerate(chunks):
                    bv, bvr = nb()
                    bg, bgr = nb()
                    mm_group(bv, bvr, [wval[:, k, jj * 128:(jj + 1) * 128] for k in range(DC)],
                             [xnT[:, k, t0:t0 + ncol] for k in range(DC)], [wvr, XN[c]], ncols=ncol)
                    mm_group(bg, bgr, [wglu[:, k, jj * 128:(jj + 1) * 128] for k in range(DC)],
                             [xnT[:, k, t0:t0 + ncol] for k in range(DC)], [wgr, XN[c]], ncols=ncol)
                    sg, sgres = sgr.next()
                    ACT(sg[:, 0:ncol], bg[:, 0:ncol], AF.Sigmoid, [bgr], [sgres])
                    ucol = (PAD + NMETA + t0) if c < NQC else PAD
                    TT(ua[:, ucol:ucol + ncol], bv[:, 0:ncol], sg[:, 0:ncol], ALU.mult, [bvr, sgres], [ur])
                for c in range(NQC):
                    t0 = c * 512
                    bc, bcr = nb()
                    mm_group(bc, bcr, [dg[:, k, :] for k in range(CONV_K)],
                             [ua[:, t0 + NMETA + k: t0 + NMETA + k + 512] for k in range(CONV_K)], [dgr, ur])
                    ACT(big1[:, j, t0:t0 + 512], bc[:, :], AF.Identity, [bcr, CONST], [B1[j, c]], bias=conv_b(j))
                    ACT(big2[:, j, t0:t0 + 512], bc[:, :], AF.Square, [bcr, CONST], [B2[j, c]], bias=conv_b(j))

        (mean, mres), (msq, qres), (rstd, rres), (nmr, nres) = [R.alloc([512], F32) for _ in range(4)]
        cnr = R.rot(2, [512], F32)
        s1r = R.rot(2, [512], BF16)
        szr = R.rot(2, [512], BF16)
        assert NWG <= 2
        wz = [wload(win_d[:, OFF["z"] + s * WG: OFF["z"] + (s + 1) * WG], WG) for s in range(NWG)]
        for c in range(NQC):
            t0 = c * 512
            bs, bsr = nb()
            bq, bqr = nb()
            mm_group(bs, bsr, [onesm] * DC, [big1[:, j, t0:t0 + 512] for j in range(DC)],
                     [CONST] + [B1[j, c] for j in range(DC)])
            mm_group(bq, bqr, [onesm] * DC, [big2[:, j, t0:t0 + 512] for j in range(DC)],
                     [CONST] + [B2[j, c] for j in range(DC)])
            TS(mean, bs[:, :], 1.0 / D, None, ALU.mult, None, [bsr], [mres])
            TT(msq, mean, mean, ALU.mult, [mres], [qres])
            STT(rstd, bq[:, :], 1.0 / D, msq, ALU.mult, ALU.subtract, [bqr, qres], [rres])
            ACT(rstd, rstd, AF.Ln, [rres], [rres], bias=EPS)
            ACT(rstd, rstd, AF.Exp, [rres], [rres], scale=-0.5)
            STT(nmr, mean, -1.0, rstd, ALU.mult, ALU.mult, [mres, rres], [nres])
            for j in range(DC):
                wzv, wzr = wz[j // CPG]
                jj = j % CPG
                bz, bzr = nb()
                mm_group(bz, bzr, [wzv[:, k, jj * 128:(jj + 1) * 128] for k in range(DC)],
                         [xnT[:, k, t0:t0 + 512] for k in range(DC)], [wzr, XN[c]])
                sz, szres = szr.next()
                ACT(sz, bz[:, :], AF.Silu, [bzr], [szres])
                cn, cnres = cnr.next()
                TT(cn, big1[:, j, t0:t0 + 512], rstd, ALU.mult, [B1[j, c], rres], [cnres])
                TT(cn, cn, nmr, ALU.add, [cnres, nres], [cnres])
                s1, s1res = s1r.next()
                ACT(s1, cn, AF.Silu, [cnres, CONST], [s1res], scale=cn_g(j), bias=cn_b(j))
                TT(big1[:, j, t0:t0 + 512], s1, sz, ALU.mult, [s1res, szres], [B1[j, c]])

        sgcr = R.rot(2, [512], F32)
        for s in range(NWG):
            wc, wcr = wload(wco_d[:, s * WG:(s + 1) * WG], WG)
            wg_, wgr_ = wload(win_d[:, OFF["gc"] + s * WG: OFF["gc"] + (s + 1) * WG], WG)
            for mi in range(CPG):
                m = s * CPG + mi
                for c in range(NQC):
                    t0 = c * 512
                    by, byr = nb()
                    bg, bgr = nb()
                    mm_group(by, byr, [wc[:, k, mi * 128:(mi + 1) * 128] for k in range(DC)],
                             [big1[:, k, t0:t0 + 512] for k in range(DC)], [wcr] + [B1[k, c] for k in range(DC)])
                    mm_group(bg, bgr, [wg_[:, k, mi * 128:(mi + 1) * 128] for k in range(DC)],
                             [xnT[:, k, t0:t0 + 512] for k in range(DC)], [wgr_, XN[c]])
                    sg, sgres = sgcr.next()
                    ACT(sg, bg[:, :], AF.Sigmoid, [bgr], [sgres])
                    TT(big2[:, m, t0:t0 + 512], by[:, :], sg, ALU.mult, [byr, sgres], [B2[m, c]])

        wpool[0] = list(wbase)
        R = Region()
        KW = T + 128
        KT = [(R.alloc([KW], BF16), R.alloc([KW], BF16)) for _ in range(NKC)]
        VA, VAr = R.alloc([NT + 1, VW], BF16)
        QTr = R.rot(2, [T], BF16)
        SAZr = R.rot(2, [T], BF16)
        GS = 2 if NQC % 2 == 0 else 1
        PTr = R.rot(3, [GS * 512], BF16)
        sqr_ = R.rot(2, [512], BF16)
        sbr_ = R.rot(2, [512], BF16)
        rsr_ = R.rot(2, [512], F32)
        t1r_ = R.rot(2, [512], F32)
        t2r_ = R.rot(1, [512], F32)
        recr = R.rot(2, [512], F32)

        def prep_qk(wr, lhs_list, gcol, t0, ncol, c, dsts):
            bp, bpr = nb()
            mm_group(bp, bpr, lhs_list, [xnT[:, k, t0:t0 + ncol] for k in range(DC)], [wr, XN[c]], ncols=ncol)
            sq, sqres = sqr_.next()
            sb, sbres = sbr_.next()
            ACT(sq[:, 0:ncol], bp[:, 0:ncol], AF.Square, [bpr], [sqres], scale=0.125)
            ACT(sb[:, 0:ncol], bp[:, 0:ncol], AF.Identity, [bpr, CONST], [sbres], scale=qkg[:, gcol:gcol + 1])
            bss, bssr = nb()
            brt, brtr = nb()
            MM(bss[:, 0:ncol], bones, sq[:, 0:ncol], True, True, [CONST, sqres], [bssr])
            MM(brt[:, 0:ncol], rotm, sb[:, 0:ncol], True, True, [CONST, sbres], [brtr])
            rs, rsres = rsr_.next()
            t1, t1res = t1r_.next()
            t2, t2res = t2r_.next()
            ACT(rs[:, 0:ncol], bss[:, 0:ncol], AF.Ln, [bssr], [rsres], bias=EPS)
            ACT(rs[:, 0:ncol], rs[:, 0:ncol], AF.Exp, [rsres], [rsres], scale=-0.5)
            TT(t1[:, 0:ncol], sb[:, 0:ncol], cosT[:, t0:t0 + ncol], ALU.mult, [sbres, CONST], [t1res])
            TT(t2[:, 0:ncol], brt[:, 0:ncol], sinT[:, t0:t0 + ncol], ALU.mult, [brtr, CONST], [t2res])
            TT(t1[:, 0:ncol], t1[:, 0:ncol], t2[:, 0:ncol], ALU.add, [t1res, t2res], [t1res])
            for dst, dres, p0, p1 in dsts:
                TT(dst[p0:p1, t0:t0 + ncol], t1[p0:p1, 0:ncol], rs[p0:p1, 0:ncol], ALU.mult, [t1res, rsres], [dres])

        def prep_q_stages(wr, lhs_list, gcol, t0, ncol, c, dsts, bk_p, bk_s, bk_r):
            (bp, bpr), (bss, bssr), (brt, brtr) = bk_p, bk_s, bk_r
            st = {}

            def s0():
                mm_group(bp, bpr, lhs_list, [xnT[:, k, t0:t0 + ncol] for k in range(DC)], [wr, XN[c]], ncols=ncol)

            def s1():
                st["sq"] = sqr_.next()
                st["sb"] = sbr_.next()
                sq, sqres = st["sq"]
                sb, sbres = st["sb"]
                ACT(sq[:, 0:ncol], bp[:, 0:ncol], AF.Square, [bpr], [sqres], scale=0.125)
                ACT(sb[:, 0:ncol], bp[:, 0:ncol], AF.Identity, [bpr, CONST], [sbres], scale=qkg[:, gcol:gcol + 1])

            def s2():
                sq, sqres = st["sq"]
                sb, sbres = st["sb"]
                MM(bss[:, 0:ncol], bones, sq[:, 0:ncol], True, True, [CONST, sqres], [bssr])
                MM(brt[:, 0:ncol], rotm, sb[:, 0:ncol], True, True, [CONST, sbres], [brtr])

            def s3():
                st["rs"] = rsr_.next()
                rs, rsres = st["rs"]
                ACT(rs[:, 0:ncol], bss[:, 0:ncol], AF.Ln, [bssr], [rsres], bias=EPS)
                ACT(rs[:, 0:ncol], rs[:, 0:ncol], AF.Exp, [rsres], [rsres], scale=-0.5)

            def s4():
                sb, sbres = st["sb"]
                rs, rsres = st["rs"]
                t1, t1res = t1r_.next()
                t2, t2res = t2r_.next()
                TT(t1[:, 0:ncol], sb[:, 0:ncol], cosT[:, t0:t0 + ncol], ALU.mult, [sbres, CONST], [t1res])
                TT(t2[:, 0:ncol], brt[:, 0:ncol], sinT[:, t0:t0 + ncol], ALU.mult, [brtr, CONST], [t2res])
                TT(t1[:, 0:ncol], t1[:, 0:ncol], t2[:, 0:ncol], ALU.add, [t1res, t2res], [t1res])
                for dst, dres, p0, p1 in dsts:
                    TT(dst[p0:p1, t0:t0 + ncol], t1[p0:p1, 0:ncol], rs[p0:p1, 0:ncol], ALU.mult,
                       [t1res, rsres], [dres])

            return [s0, s1, s2, s3, s4]

        wkv, wkvr = wload(win_d[:, OFF["k"]: OFF["k"] + 2 * KVD], 2 * KVD)
        for kc in range(NKC):
            (kta, ktares), (ktb, ktbres) = KT[kc]
            MSET(kta, 0.0, [ktares])
            MSET(ktb, 0.0, [ktbres])
            for c0 in range(0, len(chunks), 2):
                stl = []
                for ci, c in enumerate([c for c in (c0, c0 + 1) if c < len(chunks)]):
                    t0, ncol = chunks[c]
                    stl.append(prep_q_stages(wkvr, [wkv[:, k, kc * 128:(kc + 1) * 128] for k in range(DC)], 1,
                                             t0, ncol, c, [(kta, ktares, 0, HD), (ktb, ktbres, HD, 128)],
                                             banks[ci], banks[2 + 2 * ci], banks[3 + 2 * ci]))
                for si in range(5):
                    for stg in stl:
                        stg[si]()
        MSET(VA[:, 0:NT, :], 1.0, [VAr])
        MSET(VA[:, NT, :], 0.0, [VAr])
        MSET(VA[0:NMETA, NT, :].rearrange("p (c b d) -> p c b d", b=3, d=HD)[:, :, 1, :], 1.0, [VAr])
        for i in range(NT + 1):
            rows = 128 if i < NT else NMETA
            col0 = i * 128 if i < NT else T
            c = (i // 4) if i < NT else NQC
            bv, bvr = nb()
            mm_group(bv, bvr, [xnT[:, k, col0:col0 + rows] for k in range(DC)],
                     [wkv[:, k, KVD:2 * KVD] for k in range(DC)], [wkvr, XN[c]], prows=rows, ncols=KVD)
            CP(VA[0:rows, i, :].rearrange("p (c b d) -> p c b d", b=3, d=HD)[:, :, ::2, :],
               bv[0:rows, 0:KVD].rearrange("p (c b d) -> p c b d", b=2, d=HD), [bvr], [VAr])

        key_tiles = [(i * 128, 128) for i in range(NT)] + [(T, 128)]
        nbset[0] = [0, 1, 2, 3]
        unit = [0]
        step = [0]
        nk = len(key_tiles)
        for hp in range(NHP):
            wq, wqr = wload(win_d[:, OFF["q"] + hp * 128: OFF["q"] + (hp + 1) * 128], 128)
            waz, wazr = wload(win_d[:, OFF["az"] + hp * 128: OFF["az"] + (hp + 1) * 128], 128)
            qt, qtres = QTr.next()
            saz, sazres = SAZr.next()
            nbset[0] = list(range(8))
            pairs = []
            for pi, c0 in enumerate(range(0, NQC, 2)):
                cs = [c for c in (c0, c0 + 1) if c < NQC]
                stl = []
                for ci, c in enumerate(cs):
                    bpk = banks[ci] if pi % 2 == 0 else banks[6 + ci]
                    stl.append(prep_q_stages(wqr, [wq[:, k, 0:128] for k in range(DC)], 0, c * 512, 512, c,
                                             [(qt, qtres, 0, 128)],
                                             bpk, banks[2 + 2 * ci], banks[3 + 2 * ci]))
                pairs.append(stl)
            for pi in range(0, len(pairs), 2):
                grp = pairs[pi:pi + 2]
                for stl in grp:
                    for stg in stl:
                        stg[0]()
                for stl in grp:
                    for si in range(1, 5):
                        for stg in stl:
                            stg[si]()
            bas = []
            for c in range(NQC):
                t0 = c * 512
                ba, bar = banks[6 + (c % 2)]
                mm_group(ba, bar, [waz[:, k, 0:128] for k in range(DC)],
                         [xnT[:, k, t0:t0 + 512] for k in range(DC)], [wazr, XN[c]])
                ACT(saz[:, t0:t0 + 512], ba, AF.Silu, [bar], [sazres])
            nbset[0] = [0, 1, 2, 3]
            kc = hp // G

            def make_epi(obanks, r0, d0, cg):
                def epi():
                    for e in range(GS):
                        c = cg * GS + e
                        t0 = c * 512
                        bo, bor = obanks[e]
                        rec, recres = recr.next()
                        P.add("dve", lambda h, o=rec[r0:r0 + HD], i_=bo[d0:d0 + HD, :]: h.reciprocal(out=o, in_=i_),
                              reads=[bor], writes=[recres])
                        TT(rec[r0:r0 + HD], rec[r0:r0 + HD], saz[r0:r0 + HD, t0:t0 + 512], ALU.mult,
                           [recres, sazres], [recres])
                        TT(big1[r0:r0 + HD, hp, t0:t0 + 512], bo[r0:r0 + HD, :], rec[r0:r0 + HD], ALU.mult,
                           [bor, recres], [B1[hp, c]])
                return epi

            flat = []
            for hh in range(2):
                for cg in range(NQC // GS):
                    unit[0] += 1
                    obanks = [banks[4 + 2 * (unit[0] % 2) + e] if GS == 2 else banks[4 + (unit[0] % 4)]
                              for e in range(GS)]
                    for i, (k0, krows) in enumerate(key_tiles):
                        flat.append((hh, cg, i, k0, krows, obanks))

            def emit_qk(fs):
                hh, cg, i, k0, krows, obanks = fs
                kt, ktres = KT[kc][hh]
                step[0] += 1
                sb0 = (2 * (step[0] % 2)) if GS == 2 else (step[0] % 4)
                sbanks = [banks[sb0 + e] for e in range(GS)]
                for e in range(GS):
                    t0 = (cg * GS + e) * 512
                    MM(sbanks[e][0], kt[:, k0:k0 + krows], qt[:, t0:t0 + 512],
                       True, True, [ktres, qtres], [sbanks[e][1]])
                return sb0, sbanks

            pend = None
            qk_next = emit_qk(flat[0])
            for si, fs in enumerate(flat):
                hh, cg, i, k0, krows, obanks = fs
                r0 = hh * HD
                d0 = (1 - hh) * HD
                v0 = kc * 3 * HD + hh * HD
                sb0, sbanks = qk_next
                if si + 1 < len(flat):
                    qk_next = emit_qk(flat[si + 1])
                pt, ptres = PTr.next()
                ACT(pt, psum_all[:, sb0 * 512:(sb0 + GS) * 512], AF.Exp, [sbk[1] for sbk in sbanks], [ptres],
                    scale=0.125)
                if pend is not None:
                    for a in pend[0]:
                        MM(*a)
                    if pend[1] is not None:
                        pend[1]()
                pend = ([(obanks[e][0], VA[0:krows, i, v0:v0 + 128], pt[:, e * 512:(e + 1) * 512],
                          i == 0, i == nk - 1, [VAr, ptres], [obanks[e][1]]) for e in range(GS)],
                        make_epi(obanks, r0, d0, cg) if i == nk - 1 else None)
            for a in pend[0]:
                MM(*a)
            pend[1]()
        nbset[0] = list(range(8))

        RT = Region()
        sgar = RT.rot(2, [512], F32)
        wpool[0] = list(wbase) + extra_wbufs(RT, 2)
        for s in range(NWG):
            wc, wcr = wload(wao_d[:, s * WG:(s + 1) * WG], WG)
            wg_, wgr_ = wload(win_d[:, OFF["ga"] + s * WG: OFF["ga"] + (s + 1) * WG], WG)
            for mi in range(CPG):
                m = s * CPG + mi
                for c in range(NQC):
                    t0 = c * 512
                    by, byr = nb()
                    bg, bgr = nb()
                    mm_group(by, byr, [wc[:, k, mi * 128:(mi + 1) * 128] for k in range(DC)],
                             [big1[:, k, t0:t0 + 512] for k in range(DC)], [wcr] + [B1[k, c] for k in range(DC)])
                    mm_group(bg, bgr, [wg_[:, k, mi * 128:(mi + 1) * 128] for k in range(DC)],
                             [xnT[:, k, t0:t0 + 512] for k in range(DC)], [wgr_, XN[c]])
                    sg, sgres = sgar.next()
                    ACT(sg, bg[:, :], AF.Sigmoid, [bgr], [sgres])
                    TT(sg, by[:, :], sg, ALU.mult, [byr, sgres], [sgres])
                    TT(big2[:, m, t0:t0 + 512], sg, big2[:, m, t0:t0 + 512], ALU.add,
                       [sgres, B2[m, c]], [B2[m, c]])

        R = RT
        ND = 4
        xres = R.rot(ND, [D], F32)
        xres_sems = [P.dsem(f"xres{b}_{i}") for i in range(ND)]
        ost = R.rot(ND, [D], F32)
        ost_sems = [P.dsem(f"ost{b}_{i}") for i in range(ND)]
        wo = [wload(wout_d[:, s * WG:(s + 1) * WG], WG) for s in range(NWG)]
        for i in range(NT):
            c = i // 4
            kx = xres.i % ND
            xb, xr = xres.next()
            DMA("sp", xb, x_d[b, i * 128:(i + 1) * 128, :], [], [xr], xres_sems[kx])
            ko = ost.i % ND
            ob, obr = ost.next()
            for s in range(NWG):
                wv, wr = wo[s]
                bo, bor = nb()
                mm_group(bo, bor, [big2[:, k, i * 128:(i + 1) * 128] for k in range(DC)],
                         [wv[:, k, 0:WG] for k in range(DC)], [wr] + [B2[k, c] for k in range(DC)], ncols=WG)
                TT(ob[:, s * WG:(s + 1) * WG], bo[:, 0:WG], xb[:, s * WG:(s + 1) * WG], ALU.add,
                   [bor, xr], [obr])
            DMA("sp", out_d[b, i * 128:(i + 1) * 128, :], ob, [obr], [], ost_sems[ko])

    P.emit(nc)
    return nc


def rope_tables_T(cfg):
    T, GW = cfg["T"], cfg["GW"]
    nf = HD // 4
    rows = T // GW
    row_ids = np.repeat(np.arange(rows, dtype=np.float32), GW)
    col_ids = np.tile(np.arange(GW, dtype=np.float32), rows)
    row_ids = np.concatenate([row_ids, np.zeros(NMETA, np.float32)])
    col_ids = np.concatenate([col_ids, np.zeros(NMETA, np.float32)])
    inv_freq = (np.float32(ROPE_THETA) ** (-np.arange(nf, dtype=np.float32) / np.float32(nf))).astype(np.float32)
    a_row = row_ids[:, None] * inv_freq[None, :]
    a_col = col_ids[:, None] * inv_freq[None, :]
    ang = np.concatenate([a_row, a_row, a_col, a_col], axis=-1).astype(np.float32)
    cosT = np.cos(ang).T.astype(np.float32)
    sinT = np.sin(ang).T.astype(np.float32)
    return (np.ascontiguousarray(np.concatenate([cosT, cosT], 0)),
            np.ascontiguousarray(np.concatenate([sinT, sinT], 0)))


def const_mats():
    ident = np.eye(128, dtype=np.float32)
    rot = np.zeros((128, 128), np.float32)
    for m in range(128):
        if (m % 32) < 16:
            rot[m + 16, m] = -1.0
        else:
            rot[m - 16, m] = 1.0
    bones = np.zeros((128, 128), np.float32)
    bones[0:64, 0:64] = 1.0
    bones[64:128, 64:128] = 1.0
    ones = np.ones((128, 128), np.float32)
    return np.ascontiguousarray(np.concatenate([ident, rot, bones, ones], 1).astype(ml_dtypes.bfloat16))


def host_inputs(cfg, x, meta_tokens, norm_g, w_in, conv_w, conv_b, conv_norm_g, conv_norm_b,
                w_conv_out, q_norm_g, k_norm_g, w_attn_out, w_out, n_cores):
    D, BPC = cfg["D"], cfg["BPC"]
    DC = D // 128
    f = lambda a: np.ascontiguousarray(np.asarray(a, dtype=np.float32))
    cosT, sinT = rope_tables_T(cfg)
    convw = f(conv_w[0]).T.reshape(DC, 128, CONV_K).transpose(1, 0, 2).reshape(128, DC * CONV_K)
    pv = lambda v: f(v[0]).reshape(DC, 128).T
    pvec = np.concatenate([pv(conv_b), pv(conv_norm_g), pv(conv_norm_b)], axis=1)
    qkg = np.stack([np.tile(f(q_norm_g[0]), 2), np.tile(f(k_norm_g[0]), 2)], axis=1)
    NH, NKV = cfg["NH"], cfg["NKV"]
    G = NH // NKV
    KVD = NKV * HD
    perm = []
    for kc in range(NKV // 2):
        for r in range(G):
            perm += [2 * kc * G + r, (2 * kc + 1) * G + r]
    perm = np.array(perm)
    w_in_p = f(w_in[0]).copy()
    for off in (3 * D, 4 * D + 2 * KVD):
        blk = w_in_p[:, off:off + D].reshape(D, NH, HD)[:, perm, :].reshape(D, D)
        w_in_p[:, off:off + D] = blk
    w_ao_p = f(w_attn_out[0]).reshape(NH, HD, D)[perm].reshape(D, D)
    shared = {
        "meta": f(meta_tokens), "w_in": f(w_in_p), "w_co": f(w_conv_out[0]), "w_ao": f(w_ao_p),
        "w_out": f(w_out[0]), "gN": f(np.broadcast_to(f(norm_g[0])[None, :], (128, D))),
        "convw": f(convw), "pvec": f(pvec), "qkg": f(qkg), "cosT": cosT, "sinT": sinT, "cmat": const_mats(),
    }
    x = f(x)
    return [dict(shared, x=np.ascontiguousarray(x[i * BPC:(i + 1) * BPC])) for i in range(n_cores)]


_NC_CACHE = {}


def kernel(x, meta_tokens, norm_g, w_in, conv_w, conv_b, conv_norm_g, conv_norm_b,
           w_conv_out, q_norm_g, k_norm_g, w_attn_out, w_out):
    cfg = full_cfg()
    n_cores = 8
    in_maps = host_inputs(cfg, x, meta_tokens, norm_g, w_in, conv_w, conv_b, conv_norm_g, conv_norm_b,
                          w_conv_out, q_norm_g, k_norm_g, w_attn_out, w_out, n_cores)
    nc = build_program(cfg)
    res = run_bass_kernel_spmd(nc, in_maps, core_ids=list(range(n_cores)))
    return np.concatenate([np.asarray(r["out"], dtype=np.float32) for r in res.results], axis=0)
```
